# Optimizing a Trainium2 kernel written in Bass

```python
import math
import jax, jax.numpy as jnp
from jax import lax
import numpy as np

D_MODEL = 1024
BATCH = 4
SEQ = 8192
DEPTH = 1
DEC_BATCH = 8
DEC_SEQ = 4096
PAST_LEN = 128

ATTN_WIDTH = D_MODEL // 2
SSM_WIDTH = D_MODEL - ATTN_WIDTH
N_HEADS = 4
HEAD_DIM = ATTN_WIDTH // (2 * N_HEADS)
SSM_GROUP = 16
N_SSM_GROUPS = SSM_WIDTH // SSM_GROUP
STATE = 64
D_FF = 2816
CONV_W = 3
Q_BLOCK = 128
NORM_EPS = 1e-6
SUBLN_EPS = 1e-5
STEP_MIN = 1e-3
STEP_MAX = 1e-1

kernel_name = "hybrid_diffattn_s5_encoder"


def rmsnorm(x, g, eps):
    xf = x.astype(jnp.float32)
    y = xf * lax.rsqrt(jnp.mean(xf * xf, axis=-1, keepdims=True) + eps)
    return (y * g.astype(jnp.float32)).astype(x.dtype)


def alibi_slopes():
    h = jnp.arange(1, N_HEADS + 1, dtype=jnp.float32)
    return jnp.exp2(-8.0 * h / N_HEADS)


def diff_attention(q, k, v, lam, slopes):
    B, S = q.shape[0], q.shape[1]
    nblk = S // Q_BLOCK
    scale = HEAD_DIM ** -0.5
    qb = q.reshape(B, nblk, Q_BLOCK, N_HEADS, 2, HEAD_DIM).transpose(1, 0, 2, 3, 4, 5)
    kpos = jnp.arange(S, dtype=jnp.int32)

    def block(args):
        q_blk, i = args
        qpos = i * Q_BLOCK + jnp.arange(Q_BLOCK, dtype=jnp.int32)
        dist = jnp.abs(qpos[:, None] - kpos[None, :]).astype(jnp.float32)
        bias = -slopes[:, None, None] * dist
        s = jnp.einsum("bqhcd,bkhcd->bhcqk", q_blk, k).astype(jnp.float32) * scale
        p = jax.nn.softmax(s + bias[None, :, None], axis=-1)
        w = p[:, :, 0] - lam * p[:, :, 1]
        return jnp.einsum("bhqk,bkhe->bqhe", w.astype(v.dtype), v)

    out = lax.map(block, (qb, jnp.arange(nblk, dtype=jnp.int32)))
    return out.transpose(1, 0, 2, 3, 4).reshape(B, S, N_HEADS, 2 * HEAD_DIM)


def _scan_combine(e1, e2):
    a1r, a1i, b1r, b1i = e1
    a2r, a2i, b2r, b2i = e2
    ar = a1r * a2r - a1i * a2i
    ai = a1r * a2i + a1i * a2r
    br = a2r * b1r - a2i * b1i + b2r
    bi = a2r * b1i + a2i * b1r + b2i
    return ar, ai, br, bi


def s5_direction(u, a_re, a_im, log_step, b_re, b_im, c_re, c_im, reverse):
    step = jnp.exp(log_step)[:, None]
    mag = jnp.exp(a_re * step)
    lb_re = mag * jnp.cos(a_im * step)
    lb_im = mag * jnp.sin(a_im * step)
    den = a_re * a_re + a_im * a_im
    nr = lb_re - 1.0
    f_re = (nr * a_re + lb_im * a_im) / den
    f_im = (lb_im * a_re - nr * a_im) / den
    bb_re = f_re[..., None] * b_re - f_im[..., None] * b_im
    bb_im = f_re[..., None] * b_im + f_im[..., None] * b_re
    bu_re = jnp.einsum("gpc,blgc->blgp", bb_re, u)
    bu_im = jnp.einsum("gpc,blgc->blgp", bb_im, u)
    lam_re = jnp.broadcast_to(lb_re, bu_re.shape)
    lam_im = jnp.broadcast_to(lb_im, bu_im.shape)
    _, _, h_re, h_im = lax.associative_scan(
        _scan_combine, (lam_re, lam_im, bu_re, bu_im), reverse=reverse, axis=1)
    return jnp.einsum("gcp,blgp->blgc", c_re, h_re) - jnp.einsum("gcp,blgp->blgc", c_im, h_im)


def s5_mixer(u, a_re, a_im, log_step, b_re, b_im, c_re, c_im, d, w_glu, b_glu, g_out):
    B, L, _ = u.shape
    uf = u.astype(jnp.float32).reshape(B, L, N_SSM_GROUPS, SSM_GROUP)
    f32 = lambda t: t.astype(jnp.float32)
    y_f = s5_direction(uf, f32(a_re[0]), f32(a_im[0]), f32(log_step[0]), f32(b_re[0]), f32(b_im[0]),
                       f32(c_re[0]), f32(c_im[0]), False)
    y_b = s5_direction(uf, f32(a_re[1]), f32(a_im[1]), f32(log_step[1]), f32(b_re[1]), f32(b_im[1]),
                       f32(c_re[1]), f32(c_im[1]), True)
    y = (y_f + y_b).reshape(B, L, SSM_WIDTH) + f32(d) * uf.reshape(B, L, SSM_WIDTH)
    y = jax.nn.gelu(y.astype(u.dtype))
    y = y * jax.nn.sigmoid(y @ w_glu + b_glu)
    return rmsnorm(y, g_out, NORM_EPS)


def dwconv_centred(h, w, b):
    hp = jnp.pad(h, ((0, 0), (1, 1), (0, 0)))
    return hp[:, :-2] * w[0] + hp[:, 1:-1] * w[1] + hp[:, 2:] * w[2] + b


def gated_conv_mlp(h, w_up, conv_w, conv_b, w_down):
    z = dwconv_centred(h @ w_up, conv_w, conv_b)
    gate, val = jnp.split(z, 2, axis=-1)
    return (jax.nn.gelu(gate) * val) @ w_down


def trunk(x, g_mix_norm, w_in, lambda_q1, lambda_k1, lambda_q2, lambda_k2, g_subln,
          ssm_a_re, ssm_a_im, ssm_log_step, ssm_b_re, ssm_b_im, ssm_c_re, ssm_c_im, ssm_d,
          w_glu, b_glu, g_ssm_out, w_out, g_ffn_norm, w_up, conv_w, conv_b, w_down, g_final):
    B, S, _ = x.shape
    slopes = alibi_slopes()
    for l in range(DEPTH):
        lam_init = 0.8 - 0.6 * math.exp(-0.3 * l)
        n = rmsnorm(x, g_mix_norm[l], NORM_EPS)
        proj = n @ w_in[l]
        q = proj[..., :ATTN_WIDTH].reshape(B, S, N_HEADS, 2, HEAD_DIM)
        k = proj[..., ATTN_WIDTH:2 * ATTN_WIDTH].reshape(B, S, N_HEADS, 2, HEAD_DIM)
        v = proj[..., 2 * ATTN_WIDTH:3 * ATTN_WIDTH].reshape(B, S, N_HEADS, 2 * HEAD_DIM)
        u = proj[..., 3 * ATTN_WIDTH:]
        lam = (jnp.exp(jnp.sum(lambda_q1[l].astype(jnp.float32) * lambda_k1[l].astype(jnp.float32)))
               - jnp.exp(jnp.sum(lambda_q2[l].astype(jnp.float32) * lambda_k2[l].astype(jnp.float32)))
               + lam_init)
        a = diff_attention(q, k, v, lam, slopes)
        a = (rmsnorm(a, g_subln[l], SUBLN_EPS) * (1.0 - lam_init)).reshape(B, S, ATTN_WIDTH)
        s = s5_mixer(u, ssm_a_re[l], ssm_a_im[l], ssm_log_step[l], ssm_b_re[l], ssm_b_im[l],
                     ssm_c_re[l], ssm_c_im[l], ssm_d[l], w_glu[l], b_glu[l], g_ssm_out[l])
        x = x + jnp.concatenate([a, s], axis=-1) @ w_out[l]
        h = rmsnorm(x, g_ffn_norm[l], NORM_EPS)
        x = x + gated_conv_mlp(h, w_up[l], conv_w[l], conv_b[l], w_down[l])
    return rmsnorm(x, g_final, NORM_EPS)


def setup_inputs(seed: int = 0) -> dict:
    key = jax.random.key(seed)
    ks = jax.random.split(key, 32)
    f = jnp.float32
    nrm = lambda k, shape, s: jax.random.normal(k, shape, f) * s
    G, P, C = N_SSM_GROUPS, STATE, SSM_GROUP
    a_im_init = jnp.pi * jnp.arange(P, dtype=f)
    return {
        "x_prompt": jax.random.normal(ks[0], (BATCH, SEQ, D_MODEL), f),
        "x_sample": jax.random.normal(ks[1], (DEC_BATCH, DEC_SEQ, D_MODEL), f),
        "g_mix_norm": 1.0 + nrm(ks[2], (DEPTH, D_MODEL), 0.02),
        "w_in": nrm(ks[3], (DEPTH, D_MODEL, 3 * ATTN_WIDTH + SSM_WIDTH), D_MODEL ** -0.5),
        "lambda_q1": nrm(ks[4], (DEPTH, HEAD_DIM), 0.1),
        "lambda_k1": nrm(ks[5], (DEPTH, HEAD_DIM), 0.1),
        "lambda_q2": nrm(ks[6], (DEPTH, HEAD_DIM), 0.1),
        "lambda_k2": nrm(ks[7], (DEPTH, HEAD_DIM), 0.1),
        "g_subln": 1.0 + nrm(ks[8], (DEPTH, 2 * HEAD_DIM), 0.02),
        "ssm_a_re": -0.5 + nrm(ks[9], (DEPTH, 2, G, P), 0.01),
        "ssm_a_im": a_im_init + nrm(ks[10], (DEPTH, 2, G, P), 0.01),
        "ssm_log_step": jax.random.uniform(ks[11], (DEPTH, 2, G), f,
                                           math.log(STEP_MIN), math.log(STEP_MAX)),
        "ssm_b_re": nrm(ks[12], (DEPTH, 2, G, P, C), (2.0 * C) ** -0.5),
        "ssm_b_im": nrm(ks[13], (DEPTH, 2, G, P, C), (2.0 * C) ** -0.5),
        "ssm_c_re": nrm(ks[14], (DEPTH, 2, G, C, P), (2.0 * P) ** -0.5),
        "ssm_c_im": nrm(ks[15], (DEPTH, 2, G, C, P), (2.0 * P) ** -0.5),
        "ssm_d": nrm(ks[16], (DEPTH, SSM_WIDTH), 1.0),
        "w_glu": nrm(ks[17], (DEPTH, SSM_WIDTH, SSM_WIDTH), SSM_WIDTH ** -0.5),
        "b_glu": nrm(ks[18], (DEPTH, SSM_WIDTH), 0.02),
        "g_ssm_out": 1.0 + nrm(ks[19], (DEPTH, SSM_WIDTH), 0.02),
        "w_out": nrm(ks[20], (DEPTH, D_MODEL, D_MODEL), D_MODEL ** -0.5),
        "g_ffn_norm": 1.0 + nrm(ks[21], (DEPTH, D_MODEL), 0.02),
        "w_up": nrm(ks[22], (DEPTH, D_MODEL, 2 * D_FF), D_MODEL ** -0.5),
        "conv_w": nrm(ks[23], (DEPTH, CONV_W, 2 * D_FF), CONV_W ** -0.5),
        "conv_b": nrm(ks[24], (DEPTH, 2 * D_FF), 0.02),
        "w_down": nrm(ks[25], (DEPTH, D_FF, D_MODEL), D_FF ** -0.5),
        "g_final": 1.0 + nrm(ks[26], (D_MODEL,), 0.02),
    }


def reference(x_prompt, x_sample, g_mix_norm, w_in, lambda_q1, lambda_k1, lambda_q2, lambda_k2,
              g_subln, ssm_a_re, ssm_a_im, ssm_log_step, ssm_b_re, ssm_b_im, ssm_c_re, ssm_c_im,
              ssm_d, w_glu, b_glu, g_ssm_out, w_out, g_ffn_norm, w_up, conv_w, conv_b, w_down,
              g_final):
    y_prompt = trunk(x_prompt, g_mix_norm, w_in, lambda_q1, lambda_k1, lambda_q2, lambda_k2, g_subln,
                     ssm_a_re, ssm_a_im, ssm_log_step, ssm_b_re, ssm_b_im, ssm_c_re, ssm_c_im, ssm_d,
                     w_glu, b_glu, g_ssm_out, w_out, g_ffn_norm, w_up, conv_w, conv_b, w_down, g_final)
    y_sample = trunk(x_sample, g_mix_norm, w_in, lambda_q1, lambda_k1, lambda_q2, lambda_k2, g_subln,
                     ssm_a_re, ssm_a_im, ssm_log_step, ssm_b_re, ssm_b_im, ssm_c_re, ssm_c_im, ssm_d,
                     w_glu, b_glu, g_ssm_out, w_out, g_ffn_norm, w_up, conv_w, conv_b, w_down, g_final)
    return (y_prompt, y_sample)
```

```python
import math
from contextlib import ExitStack
import numpy as np
import ml_dtypes
import concourse.bass as bass
import concourse.mybir as mybir
from concourse.bass_utils import run_bass_kernel_spmd

F32 = mybir.dt.float32
BF16 = mybir.dt.bfloat16
I32 = mybir.dt.int32
AF = mybir.ActivationFunctionType
ALU = mybir.AluOpType
D = 1024
DFF = 2816
SLOPES = [2.0 ** (-2 * (h + 1)) for h in range(4)]
ID0, TL0, TU0, RP0, AR0, JF0, JB0, JC0, QF0, QR0, NCON = 0, 128, 256, 384, 896, 2944, 3072, 3200, 3208, 3720, 4232
TWO_PI = 2.0 * math.pi


class Em:
    def __init__(self, nc, stack):
        self.nc = nc
        self.streams = {k: [] for k in ("sync", "scalar", "vector", "gpsimd", "tensor")}
        self.csem = {e: stack.enter_context(nc.semaphore("c_" + e)) for e in ("scalar", "vector", "gpsimd", "tensor")}
        self.ccnt = {e: 0 for e in self.csem}
        self.NQ = 10
        self.dsem, self.dcnt, self.drr = {}, {}, {}
        for q in ("sync", "gpsimd", "scalar"):
            self.dsem[q] = [stack.enter_context(nc.semaphore(f"d_{q}_{i}")) for i in range(self.NQ)]
            self.dcnt[q] = [0] * self.NQ
            self.drr[q] = 0
        self.waited = {e: {} for e in self.streams}
        self.bw, self.br = {}, {}
        self.nins = 0

    def semh(self, key):
        return self.csem[key] if isinstance(key, str) else self.dsem[key[1]][key[2]]

    def _deps(self, eng, reads, writes):
        deps = {}

        def add(k, v):
            if deps.get(k, 0) < v:
                deps[k] = v

        for b in reads:
            for k, v in self.bw.get(b, {}).items():
                add(k, v)
        for b in writes:
            for k, v in self.bw.get(b, {}).items():
                if k != eng:
                    add(k, v)
            for k, v in self.br.get(b, {}).items():
                if k != eng:
                    add(k, v)
        return deps

    def _waits(self, eng, deps):
        waits = []
        for k, v in deps.items():
            if self.waited[eng].get(k, 0) >= v:
                continue
            self.waited[eng][k] = v
            waits.append((self.semh(k), v))
        return waits

    def _record(self, key, val, reads, writes, cwrites):
        for b in writes:
            self.bw[b] = {key: val}
            self.br[b] = {}
        for b in cwrites:
            d = self.bw.setdefault(b, {})
            d[key] = max(d.get(key, 0), val)
        for b in reads:
            d = self.br.setdefault(b, {})
            d[key] = max(d.get(key, 0), val)

    def op(self, eng, fn, reads=(), writes=()):
        waits = self._waits(eng, self._deps(eng, reads, writes))
        self.ccnt[eng] += 1
        n = self.ccnt[eng]
        sem = self.csem[eng]

        def run(e, waits=waits, fn=fn, sem=sem):
            for s, v in waits:
                e.wait_ge(s, v)
            fn(e).then_inc(sem, 1)

        self.streams[eng].append(run)
        self._record(eng, n, reads, writes, ())
        self.nins += 1

    def dma(self, q, out, in_, reads=(), writes=(), cwrites=(), **kw):
        slot = self.drr[q]
        self.drr[q] = (slot + 1) % self.NQ
        key = ("dma", q, slot)
        deps = self._deps(q, reads, writes)
        prev = self.dcnt[q][slot]
        if prev > 0:
            deps[key] = max(deps.get(key, 0), prev)
        waits = self._waits(q, deps)
        self.dcnt[q][slot] += 16
        v = self.dcnt[q][slot]
        sem = self.dsem[q][slot]

        def run(e, waits=waits, sem=sem, out=out, in_=in_, kw=kw):
            for s, vv in waits:
                e.wait_ge(s, vv)
            e.dma_start(out=out, in_=in_, **kw).then_inc(sem, 16)

        self.streams[q].append(run)
        self._record(key, v, reads, writes, cwrites)
        self.nins += 1

    def barrier(self):
        allk = {e: n for e, n in self.ccnt.items() if n > 0}
        for q in self.dsem:
            for i in range(self.NQ):
                if self.dcnt[q][i] > 0:
                    allk[("dma", q, i)] = self.dcnt[q][i]
        for eng in self.streams:
            waits = self._waits(eng, {k: v for k, v in allk.items() if k != eng})

            def run(e, waits=waits):
                for s, v in waits:
                    e.wait_ge(s, v)

            self.streams[eng].append(run)

    def finish(self):
        self.barrier()
        nc = self.nc
        with nc.Block() as block:
            @block.sync
            def _(e):
                for f in self.streams["sync"]:
                    f(e)

            @block.scalar
            def _(e):
                for f in self.streams["scalar"]:
                    f(e)

            @block.vector
            def _(e):
                for f in self.streams["vector"]:
                    f(e)

            @block.gpsimd
            def _(e):
                for f in self.streams["gpsimd"]:
                    f(e)

            @block.tensor
            def _(e):
                for f in self.streams["tensor"]:
                    f(e)


def build(T, dbg=False, phases="ABCDEF"):
    NT, NB, SEG = T // 128, T // 512, T // 2
    SEGP = SEG + 2
    nc = bass.Bass("TRN2", target_bir_lowering=False)

    def din(name, shape, dt=F32):
        return nc.dram_tensor(name, list(shape), dt, kind="ExternalInput").ap()

    skind = "ExternalOutput" if dbg else "Internal"

    def dscr(name, shape, dt):
        return nc.dram_tensor(name, list(shape), dt, kind=skind).ap()

    x = din("x", [T, D])
    w_in = din("w_in", [D, 2048])
    w_out = din("w_out", [D, D])
    w_up = din("w_up", [D, 2 * DFF])
    w_down = din("w_down", [DFF, D])
    w_glu = din("w_glu", [512, 512])
    gmix_pk = din("gmix_pk", [128, 8])
    gffn_pk = din("gffn_pk", [128, 8])
    gfin = din("gfin", [1, D])
    gsub = din("gsub", [128, 1])
    lamv = din("lamv", [1, 256])
    cw_pk = din("cw_pk", [128, 44, 3])
    cb_pk = din("cb_pk", [128, 44])
    ssmv = din("ssmv", [128, 12])
    a_row = din("a_row", [2, 3, 2048])
    a_sm = din("a_sm", [2, 128, 48])
    bmA = din("bmA", [2, 128, 4096])
    bmB = din("bmB", [2, 128, 4096])
    cmat = din("cmat", [2, 128, 4096])
    consts = din("consts", [128, NCON])
    cbias = din("cbias", [128, 4 * NB * NT])
    flag = din("flag", [128, 1])
    y = nc.dram_tensor("y", [T, D], F32, kind="ExternalOutput").ap()

    q_s = dscr("q_s", [512, T], BF16)
    k_s = dscr("k_s", [512, T], BF16)
    u_s = dscr("u_s", [512, T], BF16)
    v_s = dscr("v_s", [T, 512], BF16)
    a_s = dscr("a_s", [512, T], BF16)
    s_s = dscr("s_s", [512, T], BF16)
    yf_s = dscr("yf_s", [512, T], F32)
    ys_s = dscr("ys_s", [512, T], F32)
    x1_s = dscr("x1_s", [T, D], F32)
    h_s = dscr("h_s", [8, 128, 2 * SEGP], BF16)
    wus = dscr("wus", [44, 128, 8, 128], BF16)

    top = ExitStack()
    em = Em(nc, top)

    def rstd_chain(st4, key, scale, eps):
        em.op("vector", lambda e: e.tensor_scalar(out=st4[:, 1:2], in0=st4[:, 0:1], scalar1=scale, scalar2=eps,
                                                  op0=ALU.mult, op1=ALU.add), [key + "_0"], [key + "_1"])
        em.op("scalar", lambda e: e.activation(out=st4[:, 2:3], in_=st4[:, 1:2], func=AF.Sqrt), [key + "_1"], [key + "_2"])
        em.op("vector", lambda e: e.reciprocal(out=st4[:, 3:4], in_=st4[:, 2:3]), [key + "_2"], [key + "_3"])

    def phA():
        with ExitStack() as st:
            sb = lambda n, s, dt=F32: st.enter_context(nc.sbuf_tensor(n, list(s), dt))
            ps = lambda n, s, dt=F32: st.enter_context(nc.psum_tensor(n, list(s), dt))
            identf = sb("a_identf", [128, 128])
            identb = sb("a_identb", [128, 128], BF16)
            em.dma("sync", identf[:], consts[:, ID0:ID0 + 128], [], ["identf"])
            em.op("vector", lambda e: e.tensor_copy(out=identb[:], in_=identf[:]), ["identf"], ["identb"])
            gm = sb("a_gm", [128, 8])
            em.dma("sync", gm[:], gmix_pk[:, :], [], ["gm"])
            wbf = sb("a_wbf", [128, 8, 2048], BF16)
            wst = [sb(f"a_wst{i}", [128, 2048]) for i in range(2)]
            for k in range(8):
                b = k % 2
                em.dma("sync", wst[b][:], w_in[k * 128:(k + 1) * 128, :], [], [f"wst{b}"])
                if b == 0:
                    em.op("vector", lambda e, k=k, b=b: e.tensor_scalar(out=wbf[:, k, :], in0=wst[b][:], scalar1=gm[:, k:k + 1],
                                                                        scalar2=None, op0=ALU.mult), [f"wst{b}", "gm"], [f"wbf{k}"])
                else:
                    em.op("scalar", lambda e, k=k, b=b: e.activation(out=wbf[:, k, :], in_=wst[b][:], func=AF.Copy,
                                                                     scale=gm[:, k:k + 1]), [f"wst{b}", "gm"], [f"wbf{k}"])
            xt = [sb(f"a_xt{i}", [128, D]) for i in range(3)]
            junk = sb("a_junk", [128, D], BF16)
            xn = [sb(f"a_xn{i}", [128, D], BF16) for i in range(2)]
            s4 = [sb(f"a_s4{i}", [128, 4]) for i in range(2)]
            xT = [sb(f"a_xT{i}", [128, 8, 512], BF16) for i in range(3)]
            tp = [ps(f"a_tp{i}", [128, D], BF16) for i in range(2)]
            pp = [ps(f"a_pp{i}", [128, 512]) for i in range(4)]
            stg = [sb(f"a_stg{i}", [128, 512], BF16) for i in range(4)]
            cntb = [0]
            wk = [f"wbf{k}" for k in range(8)]

            def a_stage1(blk):
                xb = blk % 3
                for i in range(4):
                    t = blk * 4 + i
                    a, n2 = t % 3, t % 2
                    em.dma("sync", xt[a][:], x[t * 128:(t + 1) * 128, :], [], [f"xt{a}"])
                    em.op("scalar", lambda e, a=a, n2=n2: e.activation(out=junk[:], in_=xt[a][:], func=AF.Square,
                                                                       accum_out=s4[n2][:, 0:1]), [f"xt{a}"], ["junk", f"s4{n2}_0"])
                    rstd_chain(s4[n2], f"s4{n2}", 1.0 / D, 1e-6)
                    em.op("scalar", lambda e, a=a, n2=n2: e.activation(out=xn[n2][:], in_=xt[a][:], func=AF.Copy,
                                                                       scale=s4[n2][:, 3:4]), [f"xt{a}", f"s4{n2}_3"], [f"xn{n2}"])
                    for k in range(8):
                        em.op("tensor", lambda e, k=k, n2=n2: e.transpose(out=tp[n2][:, k * 128:(k + 1) * 128],
                                                                          in_=xn[n2][:, k * 128:(k + 1) * 128], identity=identb[:]),
                              [f"xn{n2}", "identb"], [f"tp{n2}"])
                    em.op("vector", lambda e, i=i, n2=n2, xb=xb: e.tensor_copy(
                        out=xT[xb][:, :, i * 128:(i + 1) * 128], in_=tp[n2][:].rearrange("p (k j) -> p k j", k=8)),
                          [f"tp{n2}"], [f"xT{xb}"])

            def a_stage2(blk):
                xb = blk % 3
                for m in range(16):
                    pb = cntb[0] % 4
                    sg = cntb[0] % 4
                    cntb[0] += 1
                    if m < 12:
                        col = m * 128 if m < 8 else 1536 + (m - 8) * 128
                        for k in range(8):
                            em.op("tensor", lambda e, k=k, col=col, pb=pb, xb=xb: e.matmul(
                                pp[pb][:], lhsT=wbf[:, k, col:col + 128], rhs=xT[xb][:, k, :], start=(k == 0), stop=(k == 7)),
                                  [wk[k], f"xT{xb}"], [f"pp{pb}"])
                        dst = (q_s if m < 4 else k_s if m < 8 else u_s)[(m % 4) * 128:(m % 4 + 1) * 128, blk * 512:(blk + 1) * 512]
                        dk = "q_s" if m < 4 else "k_s" if m < 8 else "u_s"
                    else:
                        i = m - 12
                        for k in range(8):
                            em.op("tensor", lambda e, k=k, i=i, pb=pb, xb=xb: e.matmul(
                                pp[pb][:], lhsT=xT[xb][:, k, i * 128:(i + 1) * 128], rhs=wbf[:, k, 1024:1536],
                                start=(k == 0), stop=(k == 7)), [wk[k], f"xT{xb}"], [f"pp{pb}"])
                        dst = v_s[(blk * 4 + i) * 128:(blk * 4 + i + 1) * 128, :]
                        dk = "v_s"
                    if m < 4:
                        em.op("scalar", lambda e, pb=pb, sg=sg: e.activation(out=stg[sg][:], in_=pp[pb][:], func=AF.Copy, scale=0.125),
                              [f"pp{pb}"], [f"stg{sg}"])
                    elif m % 2 == 0:
                        em.op("vector", lambda e, pb=pb, sg=sg: e.tensor_copy(out=stg[sg][:], in_=pp[pb][:]), [f"pp{pb}"], [f"stg{sg}"])
                    else:
                        em.op("scalar", lambda e, pb=pb, sg=sg: e.activation(out=stg[sg][:], in_=pp[pb][:], func=AF.Copy),
                              [f"pp{pb}"], [f"stg{sg}"])
                    em.dma("gpsimd", dst, stg[sg][:], [f"stg{sg}"], cwrites=[dk])
            a_stage1(0)
            if NB > 1:
                a_stage1(1)
            for blk in range(NB):
                if blk + 2 < NB:
                    a_stage1(blk + 2)
                a_stage2(blk)
            em.barrier()

    def phB():
        with ExitStack() as st:
            sb = lambda n, s, dt=F32: st.enter_context(nc.sbuf_tensor(n, list(s), dt))
            ps = lambda n, s, dt=F32: st.enter_context(nc.psum_tensor(n, list(s), dt))
            ramp = sb("b_ramp", [128, 512])
            absr = sb("b_absr", [128, 2048])
            em.dma("sync", ramp[:], consts[:, RP0:RP0 + 512], [], ["ramps"])
            em.dma("sync", absr[:], consts[:, AR0:AR0 + 2048], [], ["absr"])
            cb = sb("b_cb", [128, 4 * NB * NT])
            em.dma("sync", cb[:], cbias[:, :], [], ["cb"])
            onesf = sb("b_onesf", [128, 128])
            onesb = sb("b_onesb", [128, 128], BF16)
            em.op("vector", lambda e: e.memset(onesf[:], 1.0), [], ["onesf"])
            em.op("vector", lambda e: e.memset(onesb[:], 1.0), [], ["onesb"])
            lv = sb("b_lv", [128, 256])
            em.dma("sync", lv[:], lamv[0:1, :].to_broadcast([128, 256]), [], ["lv"])
            lp = sb("b_lp", [128, 128])
            l4 = sb("b_l4", [128, 8])
            em.op("vector", lambda e: e.tensor_tensor(out=lp[:, 0:64], in0=lv[:, 0:64], in1=lv[:, 64:128], op=ALU.mult), ["lv"], ["lp0"])
            em.op("vector", lambda e: e.tensor_tensor(out=lp[:, 64:128], in0=lv[:, 128:192], in1=lv[:, 192:256], op=ALU.mult), ["lv"], ["lp1"])
            em.op("scalar", lambda e: e.activation(out=lv[:, 0:64], in_=lp[:, 0:64], func=AF.Copy, accum_out=l4[:, 0:1]), ["lp0"], ["l40", "lvj"])
            em.op("scalar", lambda e: e.activation(out=lv[:, 64:128], in_=lp[:, 64:128], func=AF.Copy, accum_out=l4[:, 1:2]), ["lp1"], ["l41", "lvj2"])
            em.op("scalar", lambda e: e.activation(out=l4[:, 2:4], in_=l4[:, 0:2], func=AF.Exp), ["l40", "l41"], ["l42"])
            em.op("vector", lambda e: e.tensor_tensor(out=l4[:, 4:5], in0=l4[:, 3:4], in1=l4[:, 2:3], op=ALU.subtract), ["l42"], ["l44"])
            em.op("vector", lambda e: e.tensor_scalar(out=l4[:, 5:6], in0=l4[:, 4:5], scalar1=-0.2, scalar2=None, op0=ALU.add), ["l44"], ["neglam"])
            epsb = sb("b_epsb", [128, 1])
            em.op("vector", lambda e: e.memset(epsb[:], 1e-5), [], ["epsb"])
            gs = sb("b_gs", [128, 2])
            em.dma("sync", gs[:, 0:1], gsub[:, :], [], ["gs0"])
            em.op("vector", lambda e: e.tensor_scalar(out=gs[:, 1:2], in0=gs[:, 0:1], scalar1=0.8, scalar2=None, op0=ALU.mult), ["gs0"], ["gs1"])

            KT = [sb(f"b_KT{i}", [128, T], BF16) for i in range(2)]
            QT = [[sb(f"b_QT{i}_{c}", [128, T], BF16) for c in range(2)] for i in range(2)]
            for i in range(2):
                em.op("gpsimd", lambda e, i=i: e.memset(QT[i][0][64:128, :], 0.0), [], [f"QTz{i}0"])
                em.op("gpsimd", lambda e, i=i: e.memset(QT[i][1][0:64, :], 0.0), [], [f"QTz{i}1"])
            VV = [sb(f"b_VV{i}", [128, NT, 128], BF16) for i in range(2)]
            NSC = 4
            scp = [ps(f"b_scp{i}", [128, 512]) for i in range(NSC)]
            Op = [ps(f"b_O{i}", [128, 512]) for i in range(2)]
            Sp = [ps(f"b_S{i}", [128, 512]) for i in range(2)]
            NSB = 4
            sbt = [sb(f"b_sbt{i}", [128, 512]) for i in range(NSB)]
            NPT = 6
            pT = [sb(f"b_pT{i}", [128, 512], BF16) for i in range(NPT)]
            rs = [sb(f"b_rs{i}", [128, 512]) for i in range(2)]
            oc = [sb(f"b_oc{i}", [128, 512]) for i in range(2)]
            wt = sb("b_wt", [128, 512])
            sq = sb("b_sq", [128, 512])
            a1 = sb("b_a1", [128, 512])
            aT = [sb(f"b_aT{i}", [128, 512], BF16) for i in range(2)]
            LA = 3
            THR = 40.0
            v_r = v_s.rearrange("(kt kp) e -> kp kt e", kp=128)
            nq = 0
            sc_box = [0]
            pending = []
            for h in range(4):
                hb = h % 2
                em.dma("sync", KT[hb][:], k_s[h * 128:(h + 1) * 128, :], ["k_s"], [f"KT{hb}"])
                em.dma("sync", QT[hb][0][0:64, :], q_s[h * 128:h * 128 + 64, :], ["q_s", f"QTz{hb}0"], [f"QT{hb}_0"])
                em.dma("sync", QT[hb][1][64:128, :], q_s[h * 128 + 64:(h + 1) * 128, :], ["q_s", f"QTz{hb}1"], [f"QT{hb}_1"])
                VCH = max(1, NT // 4)
                for j in range(0, NT, VCH):
                    em.dma("sync", VV[hb][:, j:j + VCH, :], v_r[:, j:j + VCH, h * 128:(h + 1) * 128], ["v_s"], cwrites=[f"VV{hb}"],
                           writes=([f"VV{hb}"] if j == 0 else []))
                units = []
                for Qb in range(NB):
                    kept = []
                    for kt in range(NT):
                        rel = kt - 4 * Qb
                        if rel < 0:
                            dmin = 512 * Qb - (128 * kt + 127)
                        elif rel >= 4:
                            dmin = 128 * kt - (512 * Qb + 511)
                        else:
                            dmin = 0
                        if SLOPES[h] * dmin <= THR:
                            kept.append(kt)
                    for c in range(2):
                        for ii, kt in enumerate(kept):
                            units.append((Qb, c, kt, ii == 0, ii == len(kept) - 1))
                N = len(units)
                slot = {}
                for i in range(N + LA):
                    if i < N:
                        Qb, c, kt, first, last = units[i]
                        s3 = sc_box[0] % NSC
                        sc_box[0] += 1
                        s4_, p5 = i % NSB, i % NPT
                        slot[i] = p5
                        em.op("tensor", lambda e, c=c, kt=kt, Qb=Qb, s3=s3, hb=hb: e.matmul(
                            scp[s3][:], lhsT=KT[hb][:, kt * 128:(kt + 1) * 128],
                            rhs=QT[hb][c][:, Qb * 512:(Qb + 1) * 512], start=True, stop=True),
                              [f"KT{hb}", f"QT{hb}_{c}"], [f"scp{s3}"])
                        rel = kt - 4 * Qb
                        if 0 <= rel < 4:
                            tab, sc, tk = absr[:, rel * 512:(rel + 1) * 512], -SLOPES[h], "absr"
                        elif rel < 0:
                            tab, sc, tk = ramp[:], -SLOPES[h], "ramps"
                        else:
                            tab, sc, tk = ramp[:], SLOPES[h], "ramps"
                        em.op("vector", lambda e, tab=tab, sc=sc, s3=s3, s4_=s4_: e.scalar_tensor_tensor(
                            out=sbt[s4_][:], in0=tab, scalar=sc, in1=scp[s3][:], op0=ALU.mult, op1=ALU.add),
                              [f"scp{s3}", tk], [f"sbt{s4_}"])
                        col = (h * NB + Qb) * NT + kt
                        em.op("scalar", lambda e, s4_=s4_, p5=p5, col=col: e.activation(
                            out=pT[p5][:], in_=sbt[s4_][:], func=AF.Exp, bias=cb[:, col:col + 1]),
                              [f"sbt{s4_}", "cb"], [f"pT{p5}"])
                    j = i - LA
                    if j >= 0:
                        Qb, c, kt, first, last = units[j]
                        p5 = slot[j]
                        em.op("tensor", lambda e, c=c, kt=kt, p5=p5, hb=hb, first=first, last=last: e.matmul(
                            Op[c][:], lhsT=VV[hb][:, kt, :], rhs=pT[p5][:], start=first, stop=last),
                              [f"VV{hb}", f"pT{p5}"], [f"O{c}"])
                        em.op("tensor", lambda e, c=c, p5=p5, first=first, last=last: e.matmul(
                            Sp[c][:], lhsT=onesb[:], rhs=pT[p5][:], start=first, stop=last),
                              ["onesb", f"pT{p5}"], [f"S{c}"])
                        if last:
                            def mk_rc(c, k):
                                return lambda: em.op("vector", lambda e: e.reciprocal(out=rs[c][:, k * 128:(k + 1) * 128], in_=Sp[c][:, k * 128:(k + 1) * 128]),
                                                     [f"S{c}"], [f"rs{c}_{k}"])
                            for k in range(4):
                                pending.append([i + k, mk_rc(c, k)])

                            def mk_oc(c):
                                return lambda: em.op("vector", lambda e: e.tensor_tensor(out=oc[c][:], in0=Op[c][:], in1=rs[c][:], op=ALU.mult),
                                                     [f"O{c}"] + [f"rs{c}_{k}" for k in range(4)], [f"oc{c}"])
                            pending.append([i + 4, mk_oc(c)])
                            if c == 1:
                                ab = nq % 2
                                nq += 1

                                def fin_a():
                                    em.op("vector", lambda e: e.scalar_tensor_tensor(out=wt[:], in0=oc[1][:], scalar=l4[:, 5:6], in1=oc[0][:],
                                                                                     op0=ALU.mult, op1=ALU.add), ["oc0", "oc1", "neglam"], ["wt"])
                                    em.op("gpsimd", lambda e: e.tensor_tensor(out=sq[:], in0=wt[:], in1=wt[:], op=ALU.mult), ["wt"], ["sq"])

                                def fin_b(h=h):
                                    q3 = sc_box[0] % NSC
                                    sc_box[0] += 1
                                    em.op("tensor", lambda e: e.matmul(scp[q3][:], lhsT=onesf[:], rhs=sq[:], start=True, stop=True),
                                          ["onesf", "sq"], [f"scp{q3}"])
                                    em.op("scalar", lambda e: e.activation(out=a1[:], in_=scp[q3][:], func=AF.Ln, scale=1.0 / 128, bias=epsb[:, 0:1]),
                                          [f"scp{q3}", "epsb"], ["a1"])
                                    em.op("scalar", lambda e: e.activation(out=a1[:], in_=a1[:], func=AF.Exp, scale=-0.5), ["a1"], ["a1"])

                                def fin_c(h=h, Qb=Qb, ab=ab):
                                    em.op("vector", lambda e: e.tensor_tensor(out=a1[:], in0=wt[:], in1=a1[:], op=ALU.mult), ["wt", "a1"], ["a1"])
                                    em.op("scalar", lambda e: e.activation(out=aT[ab][:], in_=a1[:], func=AF.Copy, scale=gs[:, 1:2]),
                                          ["a1", "gs1"], [f"aT{ab}"])
                                    em.dma("gpsimd", a_s[h * 128:(h + 1) * 128, Qb * 512:(Qb + 1) * 512], aT[ab][:], [f"aT{ab}"], cwrites=["a_s"])
                                pending.append([i + 5, fin_a])
                                pending.append([i + 8, fin_b])
                                pending.append([i + 10, fin_c])
                    due = [p_ for p_ in pending if p_[0] <= i]
                    for p_ in due:
                        pending.remove(p_)
                        p_[1]()
                for p_ in list(pending):
                    p_[1]()
                pending.clear()
            em.barrier()

    def phC(d):
        if True:
            with ExitStack() as st:
                sb = lambda n, s, dt=F32: st.enter_context(nc.sbuf_tensor(n, list(s), dt))
                ps = lambda n, s, dt=F32: st.enter_context(nc.psum_tensor(n, list(s), dt))
                P = f"c{d}_"
                JC = JC0 + d
                JR = JF0 if d == 0 else JB0
                jc = sb(P + "jc", [128, 1])
                em.dma("sync", jc[:], consts[:, JC:JC + 1], [], ["jc"], allow_slow_non_contiguous=True)
                jrow = sb(P + "jrow", [128, 128])
                em.dma("sync", jrow[:], consts[:, JR:JR + 128], [], ["jrow"])
                tri = sb(P + "tri", [128, 128])
                T0 = TL0 if d == 0 else TU0
                em.dma("sync", tri[:], consts[:, T0:T0 + 128], [], ["trif"])
                trib = sb(P + "trib", [128, 128], BF16)
                ntrib = sb(P + "ntrib", [128, 128], BF16)
                em.op("vector", lambda e: e.tensor_copy(out=trib[:], in_=tri[:]), ["trif"], ["trib"])
                em.op("vector", lambda e: e.tensor_scalar(out=ntrib[:], in0=tri[:], scalar1=-1.0, scalar2=None, op0=ALU.mult), ["trif"], ["ntrib"])
                Er = sb(P + "Er", [128, 2048])
                Ei = sb(P + "Ei", [128, 2048])
                Fr = sb(P + "Fr", [128, 2048])
                Fi = sb(P + "Fi", [128, 2048])
                with ExitStack() as st2:
                    sb2 = lambda n, s, dt=F32: st2.enter_context(nc.sbuf_tensor(n, list(s), dt))
                    W = [sb2(P + f"w{i}", [128, 2048]) for i in range(10)]
                    WI = sb2(P + "wi", [128, 2048], I32)
                    wkey = [f"W{i}" for i in range(10)]

                    def vop(fn, r, w):
                        em.op("vector", fn, r, w)

                    def sincos(ph, phk, out_s, out_c, ok_s, ok_c, tmp, tmpk, tmp2, tmp2k):
                        for (off, o, okk) in ((0.0, out_s, ok_s), (math.pi / 2, out_c, ok_c)):
                            vop(lambda e, off=off: e.tensor_scalar(out=tmp[:], in0=ph[:], scalar1=off, scalar2=1.0 / TWO_PI,
                                                                   op0=ALU.add, op1=ALU.mult), [phk], [tmpk])
                            vop(lambda e: e.tensor_copy(out=WI[:], in_=tmp[:]), [tmpk], ["WI"])
                            vop(lambda e: e.tensor_copy(out=tmp[:], in_=WI[:]), ["WI"], [tmpk])
                            vop(lambda e: e.scalar_tensor_tensor(out=tmp2[:], in0=tmp[:], scalar=-TWO_PI, in1=ph[:],
                                                                 op0=ALU.mult, op1=ALU.add), [tmpk, phk], [tmp2k])
                            vop(lambda e, off=off: e.tensor_scalar(out=tmp2[:], in0=tmp2[:], scalar1=off, scalar2=math.pi,
                                                                   op0=ALU.add, op1=ALU.min), [tmp2k], [tmp2k])
                            vop(lambda e: e.tensor_scalar(out=tmp2[:], in0=tmp2[:], scalar1=-math.pi, scalar2=None, op0=ALU.max), [tmp2k], [tmp2k])
                            em.op("scalar", lambda e, o=o: e.activation(out=o[:], in_=tmp2[:], func=AF.Sin), [tmp2k], [okk])

                    AR, AI, LS = W[0], W[1], W[2]
                    em.dma("sync", AR[:], a_row[d, 0:1, :].to_broadcast([128, 2048]), [], [wkey[0]])
                    em.dma("sync", AI[:], a_row[d, 1:2, :].to_broadcast([128, 2048]), [], [wkey[1]])
                    em.dma("sync", LS[:], a_row[d, 2:3, :].to_broadcast([128, 2048]), [], [wkey[2]])
                    em.op("scalar", lambda e: e.activation(out=LS[:], in_=LS[:], func=AF.Exp), [wkey[2]], [wkey[2]])
                    ars, ais = W[3], W[4]
                    vop(lambda e: e.tensor_tensor(out=ars[:], in0=AR[:], in1=LS[:], op=ALU.mult), [wkey[0], wkey[2]], [wkey[3]])
                    vop(lambda e: e.tensor_tensor(out=ais[:], in0=AI[:], in1=LS[:], op=ALU.mult), [wkey[1], wkey[2]], [wkey[4]])
                    sincos(ais, wkey[4], W[5], W[6], wkey[5], wkey[6], W[7], wkey[7], W[8], wkey[8])
                    mag = W[7]
                    em.op("scalar", lambda e: e.activation(out=mag[:], in_=ars[:], func=AF.Exp), [wkey[3]], [wkey[7]])
                    lbr, lbi = W[6], W[5]
                    vop(lambda e: e.tensor_tensor(out=lbr[:], in0=W[6][:], in1=mag[:], op=ALU.mult), [wkey[6], wkey[7]], [wkey[6]])
                    vop(lambda e: e.tensor_tensor(out=lbi[:], in0=W[5][:], in1=mag[:], op=ALU.mult), [wkey[5], wkey[7]], [wkey[5]])
                    vop(lambda e: e.tensor_scalar(out=lbr[:], in0=lbr[:], scalar1=-1.0, scalar2=None, op0=ALU.add), [wkey[6]], [wkey[6]])
                    den = W[7]
                    vop(lambda e: e.tensor_tensor(out=den[:], in0=AR[:], in1=AR[:], op=ALU.mult), [wkey[0]], [wkey[7]])
                    vop(lambda e: e.tensor_tensor(out=W[8][:], in0=AI[:], in1=AI[:], op=ALU.mult), [wkey[1]], [wkey[8]])
                    vop(lambda e: e.tensor_tensor(out=den[:], in0=den[:], in1=W[8][:], op=ALU.add), [wkey[7], wkey[8]], [wkey[7]])
                    vop(lambda e: e.reciprocal(out=den[:], in_=den[:]), [wkey[7]], [wkey[7]])
                    vop(lambda e: e.tensor_tensor(out=W[8][:], in0=lbr[:], in1=AR[:], op=ALU.mult), [wkey[6], wkey[0]], [wkey[8]])
                    vop(lambda e: e.tensor_tensor(out=W[9][:], in0=lbi[:], in1=AI[:], op=ALU.mult), [wkey[5], wkey[1]], [wkey[9]])
                    vop(lambda e: e.tensor_tensor(out=W[8][:], in0=W[8][:], in1=W[9][:], op=ALU.add), [wkey[8], wkey[9]], [wkey[8]])
                    vop(lambda e: e.tensor_tensor(out=W[8][:], in0=W[8][:], in1=den[:], op=ALU.mult), [wkey[8], wkey[7]], [wkey[8]])
                    vop(lambda e: e.tensor_tensor(out=W[9][:], in0=lbi[:], in1=AR[:], op=ALU.mult), [wkey[5], wkey[0]], [wkey[9]])
                    vop(lambda e: e.tensor_tensor(out=W[2][:], in0=lbr[:], in1=AI[:], op=ALU.mult), [wkey[6], wkey[1]], [wkey[2]])
                    vop(lambda e: e.tensor_tensor(out=W[9][:], in0=W[9][:], in1=W[2][:], op=ALU.subtract), [wkey[9], wkey[2]], [wkey[9]])
                    vop(lambda e: e.tensor_tensor(out=W[9][:], in0=W[9][:], in1=den[:], op=ALU.mult), [wkey[9], wkey[7]], [wkey[9]])
                    fre, fim = W[8], W[9]
                    ph = W[0]
                    vop(lambda e: e.tensor_scalar(out=ph[:], in0=ais[:], scalar1=jc[:, 0:1], scalar2=None, op0=ALU.mult), [wkey[4], "jc"], [wkey[0]])
                    emag = W[1]
                    em.op("scalar", lambda e: e.activation(out=emag[:], in_=ars[:], func=AF.Exp, scale=jc[:, 0:1]), [wkey[3], "jc"], [wkey[1]])
                    sincos(ph, wkey[0], W[5], W[6], wkey[5], wkey[6], W[7], wkey[7], W[2], wkey[2])
                    vop(lambda e: e.tensor_tensor(out=W[7][:], in0=fre[:], in1=W[6][:], op=ALU.mult), [wkey[8], wkey[6]], [wkey[7]])
                    vop(lambda e: e.tensor_tensor(out=W[2][:], in0=fim[:], in1=W[5][:], op=ALU.mult), [wkey[9], wkey[5]], [wkey[2]])
                    vop(lambda e: e.tensor_tensor(out=W[7][:], in0=W[7][:], in1=W[2][:], op=ALU.subtract), [wkey[7], wkey[2]], [wkey[7]])
                    vop(lambda e: e.tensor_tensor(out=Er[:], in0=W[7][:], in1=emag[:], op=ALU.mult), [wkey[7], wkey[1]], ["Er"])
                    vop(lambda e: e.tensor_tensor(out=W[7][:], in0=fre[:], in1=W[5][:], op=ALU.mult), [wkey[8], wkey[5]], [wkey[7]])
                    vop(lambda e: e.tensor_tensor(out=W[2][:], in0=fim[:], in1=W[6][:], op=ALU.mult), [wkey[9], wkey[6]], [wkey[2]])
                    vop(lambda e: e.tensor_tensor(out=W[7][:], in0=W[7][:], in1=W[2][:], op=ALU.add), [wkey[7], wkey[2]], [wkey[7]])
                    vop(lambda e: e.tensor_tensor(out=Ei[:], in0=W[7][:], in1=emag[:], op=ALU.mult), [wkey[7], wkey[1]], ["Ei"])
                    asm = sb2(P + "asm", [128, 48])
                    em.dma("sync", asm[:], a_sm[d, :, :], [], ["asm"])
                    em.op("scalar", lambda e: e.activation(out=asm[:, 32:48], in_=asm[:, 32:48], func=AF.Exp), ["asm"], ["asm"])
                    vop(lambda e: e.tensor_tensor(out=asm[:, 0:16], in0=asm[:, 0:16], in1=asm[:, 32:48], op=ALU.mult), ["asm"], ["asm"])
                    vop(lambda e: e.tensor_tensor(out=asm[:, 16:32], in0=asm[:, 16:32], in1=asm[:, 32:48], op=ALU.mult), ["asm"], ["asm"])
                    v3 = lambda t: t[:].rearrange("p (a b) -> p a b", a=16)
                    jb3 = jrow[:, :].unsqueeze(1).to_broadcast([128, 16, 128])
                    fa = W[0]
                    vop(lambda e: e.tensor_tensor(out=v3(fa), in0=asm[:, 0:16].unsqueeze(2).to_broadcast([128, 16, 128]), in1=jb3, op=ALU.mult),
                        ["asm", "jrow"], [wkey[0]])
                    fmag = W[1]
                    em.op("scalar", lambda e: e.activation(out=fmag[:], in_=fa[:], func=AF.Exp), [wkey[0]], [wkey[1]])
                    fph = W[3]
                    vop(lambda e: e.tensor_tensor(out=v3(fph), in0=asm[:, 16:32].unsqueeze(2).to_broadcast([128, 16, 128]), in1=jb3, op=ALU.mult),
                        ["asm", "jrow"], [wkey[3]])
                    sincos(fph, wkey[3], W[5], W[6], wkey[5], wkey[6], W[7], wkey[7], W[2], wkey[2])
                    vop(lambda e: e.tensor_tensor(out=Fr[:], in0=W[6][:], in1=fmag[:], op=ALU.mult), [wkey[6], wkey[1]], ["Fr"])
                    vop(lambda e: e.tensor_tensor(out=Fi[:], in0=W[5][:], in1=fmag[:], op=ALU.mult), [wkey[5], wkey[1]], ["Fi"])
                    em.barrier()
                BA = sb(P + "BA", [128, 4096], BF16)
                BB = sb(P + "BB", [128, 4096], BF16)
                CM = sb(P + "CM", [128, 4096], BF16)
                with ExitStack() as st2:
                    sb2 = lambda n, s, dt=F32: st2.enter_context(nc.sbuf_tensor(n, list(s), dt))
                    wa = sb2(P + "wa", [128, 4096])
                    wb = sb2(P + "wb", [128, 4096])
                    wc = sb2(P + "wc", [128, 4096])
                    em.dma("sync", wa[:], bmA[d, :, :], [], ["wa"])
                    em.dma("sync", wb[:], bmB[d, :, :], [], ["wb"])
                    em.dma("sync", wc[:], cmat[d, :, :], [], ["wc"])
                    em.op("vector", lambda e: e.tensor_copy(out=BA[:], in_=wa[:]), ["wa"], ["BA"])
                    w4 = lambda t: t[:].rearrange("p (a h c) -> p a h c", a=8, h=2)
                    em.op("vector", lambda e: e.tensor_scalar(out=w4(BB)[:, :, 0, :], in0=w4(wb)[:, :, 0, :], scalar1=-1.0, scalar2=None, op0=ALU.mult), ["wb"], ["BB0"])
                    em.op("vector", lambda e: e.tensor_copy(out=w4(BB)[:, :, 1, :], in_=w4(wb)[:, :, 1, :]), ["wb"], ["BB1"])
                    c5 = lambda t: t[:].rearrange("p (a h c) -> p a h c", a=8, h=2)
                    em.op("vector", lambda e: e.tensor_copy(out=c5(CM)[:, :, 0, :], in_=c5(wc)[:, :, 0, :]), ["wc"], ["CM0"])
                    em.op("vector", lambda e: e.tensor_scalar(out=c5(CM)[:, :, 1, :], in0=c5(wc)[:, :, 1, :], scalar1=-1.0, scalar2=None, op0=ALU.mult), ["wc"], ["CM1"])
                    em.barrier()
                dv = sb(P + "dv", [128, 12])
                em.dma("sync", dv[:], ssmv[:, :], [], ["dv"])
                fl = sb(P + "fl", [128, 1])
                em.dma("sync", fl[:], flag[:, :], [], ["fl"])
                uT = [sb(P + f"uT{i}", [128, 4, 128], BF16) for i in range(3)]
                psA = ps(P + "psA", [128, 512])
                psB = ps(P + "psB", [128, 512])
                NG = 3
                psG = [ps(P + f"psG{i}", [128, 512]) for i in range(NG)]
                psY = [ps(P + f"psY{i}", [128, 512]) for i in range(2)]
                T1 = [sb(P + f"T1{i}", [128, 512], BF16) for i in range(2)]
                T2 = [sb(P + f"T2{i}", [128, 512], BF16) for i in range(2)]
                g1 = [sb(P + f"g1{i}", [128, 512], BF16) for i in range(NG)]
                gl = sb(P + "gl", [128, 8, 4])
                t1 = [sb(P + f"t1{i}", [128, 512], BF16) for i in range(NG)]
                t2 = [sb(P + f"t2{i}", [128, 512], BF16) for i in range(NG)]
                hc = [sb(P + f"hc{i}", [128, 8, 4]) for i in range(2)]
                cq = sb(P + "cq", [128, 8, 8])
                yo = [sb(P + f"yo{i}", [128, 4, 128]) for i in range(2)]
                yfl = [sb(P + f"yfl{i}", [128, 4, 128]) for i in range(2)]
                em.op("gpsimd", lambda e: e.memset(hc[0][:], 0.0), [], ["hc0"])
                Er3 = Er[:].rearrange("p (a c) -> p a c", a=8)
                Ei3 = Ei[:].rearrange("p (a c) -> p a c", a=8)
                Fr4 = Fr[:].rearrange("p (a h j) -> p a h j", a=8, h=2)
                Fi4 = Fi[:].rearrange("p (a h j) -> p a h j", a=8, h=2)
                Frb_ = sb(P + "Frb", [128, 2048], BF16)
                Fib_ = sb(P + "Fib", [128, 2048], BF16)
                nFib_ = sb(P + "nFib", [128, 2048], BF16)
                em.op("vector", lambda e: e.tensor_copy(out=Frb_[:], in_=Fr[:]), ["Fr"], ["Frb"])
                em.op("vector", lambda e: e.tensor_copy(out=Fib_[:], in_=Fi[:]), ["Fi"], ["Fib"])
                em.op("vector", lambda e: e.tensor_scalar(out=nFib_[:], in0=Fi[:], scalar1=-1.0, scalar2=None, op0=ALU.mult), ["Fi"], ["nFib"])
                Frb4 = Frb_[:].rearrange("p (a h j) -> p a h j", a=8, h=2)
                Fib4 = Fib_[:].rearrange("p (a h j) -> p a h j", a=8, h=2)
                nFib4 = nFib_[:].rearrange("p (a h j) -> p a h j", a=8, h=2)
                LCOL = 127 if d == 0 else 0
                order = list(range(NT)) if d == 0 else list(range(NT - 1, -1, -1))
                u_r = u_s.rearrange("(k p) t -> p k t", p=128)
                yf_r = yf_s.rearrange("(k p) t -> p k t", p=128)
                ys_r = ys_s.rearrange("(k p) t -> p k t", p=128)
                NP = NT * 8
                r4 = lambda t_: t_[:].rearrange("p (r h j) -> p r h j", r=2, h=2)
                h3 = lambda a_: a_.rearrange("p (h j) -> p h j", h=2)

                def stage_ab(gi):
                    ci, p = gi // 8, gi % 8
                    n = order[ci]
                    ub, b2 = ci % 3, gi % 2
                    kq, pq = p // 2, p % 2
                    co = kq * 1024 + pq * 512
                    if p == 0:
                        em.dma("sync", uT[ub][:], u_r[:, :, n * 128:(n + 1) * 128], ["u_s"], [f"uT{ub}"])
                    em.op("tensor", lambda e: e.matmul(psA[:], lhsT=uT[ub][:, kq, :], rhs=BA[:, co:co + 512], start=True, stop=True),
                          [f"uT{ub}", "BA"], ["psA"])
                    em.op("vector", lambda e: e.tensor_tensor(
                        out=T1[b2][:].rearrange("p (h c) -> p h c", h=2), in0=psA[:].rearrange("p (h c) -> p h c", h=2),
                        in1=Er3[:, p, :].unsqueeze(1).to_broadcast([128, 2, 256]), op=ALU.mult), ["psA", "Er"], [f"T1{b2}"])
                    em.op("vector", lambda e: e.tensor_tensor(
                        out=T2[b2][:].rearrange("p (h c) -> p h c", h=2), in0=psA[:].rearrange("p (h c) -> p h c", h=2),
                        in1=Ei3[:, p, :].unsqueeze(1).to_broadcast([128, 2, 256]), op=ALU.mult), ["psA", "Ei"], [f"T2{b2}"])

                def stage_cum(gi):
                    b2, g3 = gi % 2, gi % NG
                    for tl in range(4):
                        stl = (tl + 2) % 4
                        tm, tk = (ntrib, "ntrib") if tl < 2 else (trib, "trib")
                        em.op("tensor", lambda e, tl=tl: e.matmul(
                            psG[g3][:, tl * 128:(tl + 1) * 128], lhsT=T1[b2][:, tl * 128:(tl + 1) * 128], rhs=trib[:],
                            start=True, stop=False), [f"T1{b2}", "trib"], [f"psG{g3}"])
                        em.op("tensor", lambda e, tl=tl, stl=stl, tm=tm: e.matmul(
                            psG[g3][:, tl * 128:(tl + 1) * 128], lhsT=T2[b2][:, stl * 128:(stl + 1) * 128], rhs=tm[:],
                            start=False, stop=True), [f"T2{b2}", tk], [f"psG{g3}"])

                def stage_s2(gi):
                    ci, p = gi // 8, gi % 8
                    n = order[ci]
                    g3 = gi % NG
                    cb_, nb_ = ci % 2, (ci + 1) % 2
                    if p == 0 and ci != 0:
                        first_of_seg = ((n * 128) % SEG == 0) if d == 0 else (((n + 1) * 128) % SEG == 0)
                        if first_of_seg:
                            em.op("gpsimd", lambda e: e.tensor_scalar(out=hc[cb_][:], in0=hc[cb_][:], scalar1=fl[:, 0:1], scalar2=None, op0=ALU.mult),
                                  [f"hc{cb_}", "fl"], [f"hc{cb_}"])
                    for tl in range(4):
                        em.op("scalar", lambda e, tl=tl: e.activation(
                            out=g1[g3][:, tl * 128:(tl + 1) * 128], in_=psG[g3][:, tl * 128:(tl + 1) * 128], func=AF.Identity,
                            bias=hc[cb_][:, p, tl:tl + 1]), [f"psG{g3}", f"hc{cb_}"], [f"g1{g3}"])
                    for tl in range(4):
                        cc_ = tl * 128 + LCOL
                        em.op("scalar", lambda e, tl=tl, cc_=cc_: e.activation(
                            out=gl[:, p, tl:tl + 1], in_=psG[g3][:, cc_:cc_ + 1], func=AF.Identity,
                            bias=hc[cb_][:, p, tl:tl + 1]), [f"psG{g3}", f"hc{cb_}"], [f"gl{p}"])
                    em.op("vector", lambda e: e.tensor_tensor(out=h3(t1[g3][:, 0:256]), in0=h3(g1[g3][:, 0:256]), in1=Frb4[:, p, :, :], op=ALU.mult),
                          [f"g1{g3}", "Frb"], [f"t1{g3}a"])
                    em.op("vector", lambda e: e.tensor_tensor(out=h3(t1[g3][:, 256:512]), in0=h3(g1[g3][:, 256:512]), in1=Frb4[:, p, :, :], op=ALU.mult),
                          [f"g1{g3}", "Frb"], [f"t1{g3}b"])
                    em.op("vector", lambda e: e.tensor_tensor(out=h3(t2[g3][:, 0:256]), in0=h3(g1[g3][:, 256:512]), in1=nFib4[:, p, :, :], op=ALU.mult),
                          [f"g1{g3}", "nFib"], [f"t2{g3}a"])
                    em.op("vector", lambda e: e.tensor_tensor(out=h3(t2[g3][:, 256:512]), in0=h3(g1[g3][:, 0:256]), in1=Fib4[:, p, :, :], op=ALU.mult),
                          [f"g1{g3}", "Fib"], [f"t2{g3}b"])
                    Frl2 = Fr4[:, p, :, LCOL]
                    Fil = Fi4[:, p, :, LCOL]
                    em.op("vector", lambda e: e.tensor_tensor(out=cq[:, p, 0:2], in0=gl[:, p, 0:2], in1=Frl2, op=ALU.mult), [f"gl{p}", "Fr"], [f"cq{p}a"])
                    em.op("vector", lambda e: e.tensor_tensor(out=cq[:, p, 2:4], in0=gl[:, p, 2:4], in1=Frl2, op=ALU.mult), [f"gl{p}", "Fr"], [f"cq{p}a2"])
                    em.op("vector", lambda e: e.scalar_tensor_tensor(out=cq[:, p, 4:6], in0=gl[:, p, 2:4], scalar=-1.0, in1=Fil, op0=ALU.mult, op1=ALU.mult),
                          [f"gl{p}", "Fi"], [f"cq{p}b"])
                    em.op("vector", lambda e: e.tensor_tensor(out=cq[:, p, 6:8], in0=gl[:, p, 0:2], in1=Fil, op=ALU.mult), [f"gl{p}", "Fi"], [f"cq{p}c"])
                    em.op("vector", lambda e: e.tensor_tensor(out=hc[nb_][:, p, :], in0=cq[:, p, 0:4], in1=cq[:, p, 4:8], op=ALU.add),
                          [f"cq{p}a", f"cq{p}a2", f"cq{p}b", f"cq{p}c"], [f"hc{nb_}"])

                def stage_cmm(gi):
                    ci, p = gi // 8, gi % 8
                    n = order[ci]
                    g3 = gi % NG
                    kq, pq = p // 2, p % 2
                    yb, ub = ci % 2, ci % 3
                    for tl in range(4):
                        for (src, sk) in ((t1, "t1"), (t2, "t2")):
                            first = (pq == 0 and tl == 0 and sk == "t1")
                            last = (pq == 1 and tl == 3 and sk == "t2")
                            cc = p * 512 + tl * 128
                            rk = [f"t1{g3}a", f"t1{g3}b"] if sk == "t1" else [f"t2{g3}a", f"t2{g3}b"]
                            em.op("tensor", lambda e, tl=tl, src=src, first=first, last=last, cc=cc: e.matmul(
                                psY[yb][:, kq * 128:(kq + 1) * 128], lhsT=CM[:, cc:cc + 128], rhs=src[g3][:, tl * 128:(tl + 1) * 128],
                                start=first, stop=last), rk + ["CM0", "CM1"], [f"psY{yb}"])
                    if p != 7:
                        return
                    if d == 0:
                        em.op("scalar", lambda e: e.activation(out=yo[yb][:].rearrange("p k j -> p (k j)"), in_=psY[yb][:], func=AF.Copy),
                              [f"psY{yb}"], [f"yo{yb}"])
                        em.dma("gpsimd", yf_r[:, :, n * 128:(n + 1) * 128], yo[yb][:], [f"yo{yb}"], cwrites=["yf_s"])
                    else:
                        em.dma("sync", yfl[yb][:], yf_r[:, :, n * 128:(n + 1) * 128], ["yf_s"], [f"yfl{yb}"])
                        em.op("vector", lambda e: e.tensor_tensor(out=yo[yb][:].rearrange("p k j -> p (k j)"), in0=psY[yb][:],
                                                                  in1=yfl[yb][:].rearrange("p k j -> p (k j)"), op=ALU.add),
                              [f"psY{yb}", f"yfl{yb}"], [f"yo{yb}"])
                        em.op("gpsimd", lambda e: e.tensor_tensor(out=yfl[yb][:], in0=uT[ub][:],
                                                                  in1=dv[:, 0:4].unsqueeze(2).to_broadcast([128, 4, 128]), op=ALU.mult),
                              [f"uT{ub}", "dv", f"yo{yb}"], [f"yfl{yb}"])
                        em.op("gpsimd", lambda e: e.tensor_tensor(out=yo[yb][:], in0=yo[yb][:], in1=yfl[yb][:], op=ALU.add),
                              [f"yo{yb}", f"yfl{yb}"], [f"yo{yb}"])
                        em.dma("gpsimd", ys_r[:, :, n * 128:(n + 1) * 128], yo[yb][:], [f"yo{yb}"], cwrites=["ys_s"])

                for i in range(NP + 2):
                    if i < NP:
                        stage_ab(i)
                    if 0 <= i - 1 < NP:
                        stage_cum(i - 1)
                        stage_s2(i - 1)
                    if 0 <= i - 2 < NP:
                        stage_cmm(i - 2)
                em.barrier()

    def phD():
        with ExitStack() as st:
            sb = lambda n, s, dt=F32: st.enter_context(nc.sbuf_tensor(n, list(s), dt))
            ps = lambda n, s, dt=F32: st.enter_context(nc.psum_tensor(n, list(s), dt))
            wg = sb("d_wg", [128, 4, 512], BF16)
            em.dma("gpsimd", wg[:], w_glu.rearrange("(k p) n -> p k n", p=128), [], ["wg"])
            dv = sb("d_dv", [128, 12])
            em.dma("sync", dv[:], ssmv[:, :], [], ["dv"])
            onesf = sb("d_onesf", [128, 128])
            em.op("vector", lambda e: e.memset(onesf[:], 1.0), [], ["onesf"])
            ysb = [sb(f"d_ys{i}", [128, 4, 512]) for i in range(2)]
            y1 = [sb(f"d_y1{i}", [128, 4, 512]) for i in range(2)]
            y1b = [sb(f"d_y1b{i}", [128, 4, 512], BF16) for i in range(2)]
            sg = [sb(f"d_sg{i}", [128, 512]) for i in range(2)]
            y2 = [sb(f"d_y2{i}", [128, 4, 512]) for i in range(2)]
            sqd = [sb(f"d_sq{i}", [128, 512]) for i in range(2)]
            rsd = sb("d_rsd", [128, 512])
            rsd2 = sb("d_rsd2", [128, 512])
            so = [sb(f"d_so{i}", [128, 4, 512], BF16) for i in range(2)]
            pz = [ps(f"d_pz{i}", [128, 512]) for i in range(4)]
            pq_ = ps("d_pq", [128, 512])
            ys_r = ys_s.rearrange("(k p) t -> p k t", p=128)
            s_r = s_s.rearrange("(k p) t -> p k t", p=128)
            def d_stage1(blk):
                b = blk % 2
                em.dma("sync", ysb[b][:], ys_r[:, :, blk * 512:(blk + 1) * 512], ["ys_s"], [f"ys{b}"])
                em.op("scalar", lambda e: e.activation(out=y1[b][:], in_=ysb[b][:], func=AF.Gelu_apprx_tanh), [f"ys{b}"], [f"y1{b}"])
                em.op("vector", lambda e: e.tensor_copy(out=y1b[b][:], in_=y1[b][:]), [f"y1{b}"], [f"y1b{b}"])
                for mo in range(4):
                    for k in range(4):
                        em.op("tensor", lambda e, k=k, mo=mo: e.matmul(pz[mo][:], lhsT=wg[:, k, mo * 128:(mo + 1) * 128], rhs=y1b[b][:, k, :],
                                                                       start=(k == 0), stop=(k == 3)), ["wg", f"y1b{b}"], [f"pz{mo}"])
                for mo in range(4):
                    zb = mo % 2
                    em.op("scalar", lambda e, mo=mo, zb=zb: e.activation(out=sg[zb][:], in_=pz[mo][:], func=AF.Sigmoid, bias=dv[:, 4 + mo:5 + mo]),
                          [f"pz{mo}", "dv"], [f"sg{zb}"])
                    em.op("vector", lambda e, mo=mo, zb=zb: e.tensor_tensor(out=y2[b][:, mo, :], in0=y1[b][:, mo, :], in1=sg[zb][:], op=ALU.mult),
                          [f"y1{b}", f"sg{zb}"], [f"y2{b}_{mo}"])
                    em.op("gpsimd", lambda e, mo=mo, zb=zb: e.tensor_tensor(out=sqd[zb][:], in0=y2[b][:, mo, :], in1=y2[b][:, mo, :], op=ALU.mult),
                          [f"y2{b}_{mo}"], [f"sqd{zb}"])
                    em.op("tensor", lambda e, mo=mo, zb=zb: e.matmul(pq2[b][:], lhsT=onesf[:], rhs=sqd[zb][:], start=(mo == 0), stop=(mo == 3)),
                          ["onesf", f"sqd{zb}"], [f"pq{b}"])

            def d_stage2(blk):
                b = blk % 2
                em.op("vector", lambda e: e.tensor_scalar(out=rsd[:], in0=pq2[b][:], scalar1=1.0 / 512, scalar2=1e-6, op0=ALU.mult, op1=ALU.add), [f"pq{b}"], ["rsd"])
                em.op("scalar", lambda e: e.activation(out=rsd2[:], in_=rsd[:], func=AF.Sqrt), ["rsd"], ["rsd2"])
                em.op("vector", lambda e: e.reciprocal(out=rsd[:], in_=rsd2[:]), ["rsd2"], ["rsd"])
                for mo in range(4):
                    em.op("vector", lambda e, mo=mo: e.scalar_tensor_tensor(out=so[b][:, mo, :], in0=y2[b][:, mo, :], scalar=dv[:, 8 + mo:9 + mo],
                                                                           in1=rsd[:], op0=ALU.mult, op1=ALU.mult),
                          [f"y2{b}_{mo}", "rsd", "dv"], [f"so{b}"])
                em.dma("gpsimd", s_r[:, :, blk * 512:(blk + 1) * 512], so[b][:], [f"so{b}"], cwrites=["s_s"])

            pq2 = [pq_, ps("d_pq1", [128, 512])]
            d_stage1(0)
            for blk in range(NB):
                if blk + 1 < NB:
                    d_stage1(blk + 1)
                d_stage2(blk)
            em.barrier()

    def phE():
        with ExitStack() as st:
            sb = lambda n, s, dt=F32: st.enter_context(nc.sbuf_tensor(n, list(s), dt))
            ps = lambda n, s, dt=F32: st.enter_context(nc.psum_tensor(n, list(s), dt))
            identf = sb("e_identf", [128, 128])
            identb = sb("e_identb", [128, 128], BF16)
            em.dma("sync", identf[:], consts[:, ID0:ID0 + 128], [], ["identf"])
            em.op("vector", lambda e: e.tensor_copy(out=identb[:], in_=identf[:]), ["identf"], ["identb"])
            wo = sb("e_wo", [128, 8, D], BF16)
            em.dma("gpsimd", wo[:], w_out.rearrange("(k p) n -> p k n", p=128), [], ["wo"])
            fl = sb("e_fl", [128, 1])
            em.dma("sync", fl[:], flag[:, :], [], ["fl"])
            zt = sb("e_zt", [128, 8, 1], BF16)
            em.op("vector", lambda e: e.memset(zt[:], 0.0), [], ["zt"])
            h_r = h_s.rearrange("k p t -> p k t")
            em.dma("gpsimd", h_r[:, :, 0:1], zt[:], ["zt"], cwrites=["h_s"], allow_slow_non_contiguous=True)
            em.dma("gpsimd", h_r[:, :, 2 * SEGP - 1:2 * SEGP], zt[:], ["zt"], cwrites=["h_s"], allow_slow_non_contiguous=True)
            cat = [sb(f"e_cat{i}", [128, 8, 512], BF16) for i in range(2)]
            xt = [sb(f"e_xt{i}", [128, D]) for i in range(3)]
            x1 = [sb(f"e_x1{i}", [128, D]) for i in range(3)]
            junk = sb("e_junk", [128, D], BF16)
            s4 = [sb(f"e_s4{i}", [128, 4]) for i in range(2)]
            hn = [sb(f"e_hn{i}", [128, D], BF16) for i in range(2)]
            hT = [sb(f"e_hT{i}", [128, 8, 512], BF16) for i in range(2)]
            halo = [sb(f"e_halo{i}", [128, 8, 1], BF16) for i in range(2)]
            po = [ps(f"e_po{i}", [128, 512]) for i in range(4)]
            tp = [ps(f"e_tp{i}", [128, D], BF16) for i in range(2)]
            a_r = a_s.rearrange("(k p) t -> p k t", p=128)
            s_r = s_s.rearrange("(k p) t -> p k t", p=128)
            pcc = [0]

            def e_stage1(t):
                blk, i = t // 4, t % 4
                cbf = blk % 2
                a, n2 = t % 3, t % 2
                if i == 0:
                    em.dma("sync", cat[cbf][:, 0:4, :], a_r[:, :, blk * 512:(blk + 1) * 512], ["a_s"], [f"cat{cbf}a"])
                    em.dma("sync", cat[cbf][:, 4:8, :], s_r[:, :, blk * 512:(blk + 1) * 512], ["s_s"], [f"cat{cbf}s"])
                em.dma("sync", xt[a][:], x[t * 128:(t + 1) * 128, :], [], [f"xt{a}"])
                for half in range(2):
                    pb = pcc[0] % 4
                    pcc[0] += 1
                    for k in range(8):
                        em.op("tensor", lambda e, k=k, half=half, pb=pb: e.matmul(
                            po[pb][:], lhsT=cat[cbf][:, k, i * 128:(i + 1) * 128], rhs=wo[:, k, half * 512:(half + 1) * 512],
                            start=(k == 0), stop=(k == 7)), [f"cat{cbf}a", f"cat{cbf}s", "wo"], [f"po{pb}"])
                    em.op("vector", lambda e, half=half, pb=pb: e.tensor_tensor(
                        out=x1[a][:, half * 512:(half + 1) * 512], in0=po[pb][:], in1=xt[a][:, half * 512:(half + 1) * 512], op=ALU.add),
                          [f"po{pb}", f"xt{a}"], [f"x1{a}_{half}"])
                em.dma("gpsimd", x1_s[t * 128:(t + 1) * 128, :], x1[a][:], [f"x1{a}_0", f"x1{a}_1"], cwrites=["x1_s"])
                em.op("scalar", lambda e: e.activation(out=junk[:], in_=x1[a][:], func=AF.Square, accum_out=s4[n2][:, 0:1]),
                      [f"x1{a}_0", f"x1{a}_1"], ["junk", f"s4{n2}_0"])
                rstd_chain(s4[n2], f"s4{n2}", 1.0 / D, 1e-6)
                em.op("scalar", lambda e: e.activation(out=hn[n2][:], in_=x1[a][:], func=AF.Copy, scale=s4[n2][:, 3:4]),
                      [f"x1{a}_0", f"x1{a}_1", f"s4{n2}_3"], [f"hn{n2}"])

            def e_stage2(t):
                blk, i = t // 4, t % 4
                cbf = blk % 2
                n2 = t % 2
                for k in range(8):
                    em.op("tensor", lambda e, k=k: e.transpose(out=tp[n2][:, k * 128:(k + 1) * 128], in_=hn[n2][:, k * 128:(k + 1) * 128],
                                                               identity=identb[:]), [f"hn{n2}", "identb"], [f"tp{n2}"])
                em.op("vector", lambda e: e.tensor_copy(out=hT[cbf][:, :, i * 128:(i + 1) * 128],
                                                        in_=tp[n2][:].rearrange("p (k j) -> p k j", k=8)),
                      [f"tp{n2}"], [f"hT{cbf}"])
                if i != 3:
                    return
                seg = (blk * 512) // SEG
                off = seg * SEGP + 1 + (blk * 512 - seg * SEG)
                em.dma("gpsimd", h_r[:, :, off:off + 512], hT[cbf][:], [f"hT{cbf}"], cwrites=["h_s"])
                if (blk + 1) * 512 == SEG:
                    em.op("vector", lambda e: e.tensor_scalar(out=halo[0][:], in0=hT[cbf][:, :, 511:512], scalar1=fl[:, 0:1], scalar2=None,
                                                              op0=ALU.mult), [f"hT{cbf}", "fl"], ["halo0"])
                    em.dma("gpsimd", h_r[:, :, SEGP:SEGP + 1], halo[0][:], ["halo0"], cwrites=["h_s"], allow_slow_non_contiguous=True)
                if blk * 512 == SEG:
                    em.op("vector", lambda e: e.tensor_scalar(out=halo[1][:], in0=hT[cbf][:, :, 0:1], scalar1=fl[:, 0:1], scalar2=None,
                                                              op0=ALU.mult), [f"hT{cbf}", "fl"], ["halo1"])
                    em.dma("gpsimd", h_r[:, :, SEGP - 1:SEGP], halo[1][:], ["halo1"], cwrites=["h_s"], allow_slow_non_contiguous=True)

            e_stage1(0)
            for t in range(NT):
                if t + 1 < NT:
                    e_stage1(t + 1)
                e_stage2(t)
            em.barrier()

    def phF():
        with ExitStack() as st:
            sb = lambda n, s, dt=F32: st.enter_context(nc.sbuf_tensor(n, list(s), dt))
            ps = lambda n, s, dt=F32: st.enter_context(nc.psum_tensor(n, list(s), dt))
            gf = sb("f_gf", [128, 8])
            em.dma("sync", gf[:], gffn_pk[:, :], [], ["gf"])
            with ExitStack() as st2:
                sb2 = lambda n, s, dt=F32: st2.enter_context(nc.sbuf_tensor(n, list(s), dt))
                wst = [sb2(f"f_wst{i}", [128, 2 * DFF]) for i in range(2)]
                wsb = [sb2(f"f_wsb{i}", [128, 2 * DFF], BF16) for i in range(2)]
                wus_r = wus.rearrange("m p k c -> p k m c")
                for k in range(8):
                    b = k % 2
                    em.dma("sync", wst[b][:], w_up[k * 128:(k + 1) * 128, :], [], [f"wst{b}"])
                    if b == 0:
                        em.op("vector", lambda e, k=k, b=b: e.tensor_scalar(out=wsb[b][:], in0=wst[b][:], scalar1=gf[:, k:k + 1], scalar2=None, op0=ALU.mult),
                              [f"wst{b}", "gf"], [f"wsb{b}"])
                    else:
                        em.op("scalar", lambda e, k=k, b=b: e.activation(out=wsb[b][:], in_=wst[b][:], func=AF.Copy, scale=gf[:, k:k + 1]),
                              [f"wst{b}", "gf"], [f"wsb{b}"])
                    em.dma("gpsimd", wus_r[:, k, :, :], wsb[b][:].rearrange("p (m c) -> p m c", c=128), [f"wsb{b}"], cwrites=["wus"])
                em.barrier()
            wd = sb("f_wd", [128, 22, D], BF16)
            em.dma("gpsimd", wd[:], w_down.rearrange("(m p) n -> p m n", p=128), [], ["wd"])
            cw = sb("f_cw", [128, 44, 3])
            cbv = sb("f_cb", [128, 44])
            em.dma("sync", cw[:], cw_pk[:, :, :], [], ["cw"])
            em.dma("sync", cbv[:], cb_pk[:, :], [], ["cbv"])
            gfin_t = sb("f_gfin", [128, D])
            em.dma("sync", gfin_t[:], gfin[0:1, :].to_broadcast([128, D]), [], ["gfin"])
            hw = [sb(f"f_hw{i}", [128, 8, 512], BF16) for i in range(3)]
            wt_ = [sb(f"f_wt{i}", [128, 8, 128], BF16) for i in range(8)]
            zc = [sb(f"f_zc{i}", [128, 512]) for i in range(4)]
            ga = [sb(f"f_ga{i}", [128, 512]) for i in range(2)]
            act = [sb(f"f_act{i}", [128, 22, 512], BF16) for i in range(2)]
            x1t = [sb(f"f_x1{i}", [128, D]) for i in range(8)]
            x2t = [sb(f"f_x2{i}", [128, D]) for i in range(2)]
            yt = [sb(f"f_yt{i}", [128, D]) for i in range(2)]
            junk = sb("f_junk", [128, D], BF16)
            s4 = [sb(f"f_s4{i}", [128, 4]) for i in range(2)]
            pu = [ps(f"f_pu{i}", [128, 512]) for i in range(4)]
            pd = [ps(f"f_pd{i}", [128, 512]) for i in range(4)]
            h_r = h_s.rearrange("k p t -> p k t")
            cnts = {"uc": 0, "dc": 0, "tc": 0}
            wins = []
            for seg in range(2):
                b = 0
                while 510 * b < SEG:
                    W_ = min(512, SEGP - 510 * b)
                    wins.append(dict(seg=seg, b=b, W=W_, nv=W_ - 2, c0=seg * SEGP + 510 * b, tok0=seg * SEG + 510 * b, idx=len(wins)))
                    b += 1

            def f_load(w):
                hb = w["idx"] % 3
                em.dma("sync", hw[hb][:, :, 0:w["W"]], h_r[:, :, w["c0"]:w["c0"] + w["W"]], ["h_s"], [f"hw{hb}"])

            def f_loadx(w):
                w["xb"] = []
                ntt = (w["nv"] + 127) // 128
                for tt in range(ntt):
                    r0 = tt * 128
                    nr = min(128, w["nv"] - r0)
                    xb_ = cnts["tc"] % 8
                    cnts["tc"] += 1
                    w["xb"].append(xb_)
                    em.dma("sync", x1t[xb_][0:nr, :], x1_s[w["tok0"] + r0:w["tok0"] + r0 + nr, :], ["x1_s"], [f"x1t{xb_}"])

            def f_up(w, mm0, mm1):
                hb, ab = w["idx"] % 3, w["idx"] % 2
                W_, nv = w["W"], w["nv"]
                for mm in range(mm0, mm1):
                    m = (mm // 2) + (22 if mm % 2 else 0)
                    uc = cnts["uc"]
                    cnts["uc"] += 1
                    w4, p3, z4 = uc % 8, uc % 4, uc % 4
                    em.dma("sync", wt_[w4][:], wus[m, :, :, :], ["wus"], [f"wt{w4}"])
                    for k in range(8):
                        em.op("tensor", lambda e, k=k, w4=w4, p3=p3: e.matmul(
                            pu[p3][:, 0:W_], lhsT=wt_[w4][:, k, :], rhs=hw[hb][:, k, 0:W_], start=(k == 0), stop=(k == 7)),
                              [f"wt{w4}", f"hw{hb}"], [f"pu{p3}"])
                    em.op("scalar", lambda e, m=m, p3=p3, z4=z4: e.activation(
                        out=zc[z4][:, 0:nv], in_=pu[p3][:, 1:1 + nv], func=AF.Identity, scale=cw[:, m, 1:2], bias=cbv[:, m:m + 1]),
                          [f"pu{p3}", "cw", "cbv"], [f"zc{z4}"])
                    em.op("vector", lambda e, m=m, p3=p3, z4=z4: e.scalar_tensor_tensor(
                        out=zc[z4][:, 0:nv], in0=pu[p3][:, 0:nv], scalar=cw[:, m, 0:1], in1=zc[z4][:, 0:nv], op0=ALU.mult, op1=ALU.add),
                          [f"pu{p3}", "cw", f"zc{z4}"], [f"zc{z4}"])
                    em.op("vector", lambda e, m=m, p3=p3, z4=z4: e.scalar_tensor_tensor(
                        out=zc[z4][:, 0:nv], in0=pu[p3][:, 2:2 + nv], scalar=cw[:, m, 2:3], in1=zc[z4][:, 0:nv], op0=ALU.mult, op1=ALU.add),
                          [f"pu{p3}", "cw", f"zc{z4}"], [f"zc{z4}"])
                    g2_ = (mm // 2) % 2
                    if mm % 2 == 0:
                        em.op("scalar", lambda e, z4=z4, g2_=g2_: e.activation(out=ga[g2_][:, 0:nv], in_=zc[z4][:, 0:nv], func=AF.Gelu_apprx_tanh),
                              [f"zc{z4}"], [f"ga{g2_}"])
                    else:
                        mg = mm // 2
                        em.op("gpsimd", lambda e, z4=z4, g2_=g2_, mg=mg: e.tensor_tensor(
                            out=act[ab][:, mg, 0:nv], in0=ga[g2_][:, 0:nv], in1=zc[z4][:, 0:nv], op=ALU.mult),
                              [f"ga{g2_}", f"zc{z4}"], [f"act{ab}"])

            def f_down(w):
                ab = w["idx"] % 2
                nv, tok0 = w["nv"], w["tok0"]
                ntt = (nv + 127) // 128
                for tt in range(ntt):
                    r0 = tt * 128
                    nr = min(128, nv - r0)
                    xb_ = w["xb"][tt]
                    ob = xb_ % 2
                    for half in range(2):
                        p4 = cnts["dc"] % 4
                        cnts["dc"] += 1
                        for m in range(22):
                            em.op("tensor", lambda e, m=m, half=half, p4=p4, r0=r0, nr=nr: e.matmul(
                                pd[p4][0:nr, :], lhsT=act[ab][:, m, r0:r0 + nr], rhs=wd[:, m, half * 512:(half + 1) * 512],
                                start=(m == 0), stop=(m == 21)), [f"act{ab}", "wd"], [f"pd{p4}"])
                        em.op("vector", lambda e, half=half, p4=p4, nr=nr, xb_=xb_, ob=ob: e.tensor_tensor(
                            out=x2t[ob][0:nr, half * 512:(half + 1) * 512], in0=pd[p4][0:nr, :], in1=x1t[xb_][0:nr, half * 512:(half + 1) * 512], op=ALU.add),
                              [f"pd{p4}", f"x1t{xb_}"], [f"x2t{ob}_{half}"])
                    em.op("scalar", lambda e, nr=nr, ob=ob: e.activation(out=junk[0:nr, :], in_=x2t[ob][0:nr, :], func=AF.Square, accum_out=s4[ob][0:nr, 0:1]),
                          [f"x2t{ob}_0", f"x2t{ob}_1"], ["junk", f"s4{ob}_0"])
                    rstd_chain(s4[ob], f"s4{ob}", 1.0 / D, 1e-6)
                    em.op("vector", lambda e, nr=nr, ob=ob: e.scalar_tensor_tensor(
                        out=yt[ob][0:nr, :], in0=x2t[ob][0:nr, :], scalar=s4[ob][0:nr, 3:4], in1=gfin_t[0:nr, :], op0=ALU.mult, op1=ALU.mult),
                          [f"x2t{ob}_0", f"x2t{ob}_1", f"s4{ob}_3", "gfin"], [f"yt{ob}"])
                    em.dma("gpsimd", y[tok0 + r0:tok0 + r0 + nr, :], yt[ob][0:nr, :], [f"yt{ob}"], cwrites=["y"])

            NPRE = 8
            f_load(wins[0])
            if len(wins) > 1:
                f_load(wins[1])
            f_loadx(wins[0])
            f_up(wins[0], 0, 44)
            for wi_, w in enumerate(wins):
                nxt = wins[wi_ + 1] if wi_ + 1 < len(wins) else None
                nn = wins[wi_ + 2] if wi_ + 2 < len(wins) else None
                if nxt is not None:
                    f_up(nxt, 0, NPRE)
                if nn is not None:
                    f_load(nn)
                if nxt is not None:
                    f_loadx(nxt)
                f_down(w)
                if nxt is not None:
                    f_up(nxt, NPRE, 44)
            em.barrier()

    if "A" in phases:
        phA()
    if "B" in phases:
        phB()
    if "C" in phases:
        phC(0)
        phC(1)
    if "D" in phases:
        phD()
    if "E" in phases:
        phE()
    if "F" in phases:
        phF()
    em.finish()
    top.close()
    return nc


def make_consts():
    c = np.zeros((128, NCON), np.float32)
    p = np.arange(128, dtype=np.float32)[:, None]
    j = np.arange(128, dtype=np.float32)[None, :]
    c[:, ID0:ID0 + 128] = np.eye(128, dtype=np.float32)
    c[:, TL0:TL0 + 128] = (p <= j).astype(np.float32)
    c[:, TU0:TU0 + 128] = (p >= j).astype(np.float32)
    qf = np.arange(512, dtype=np.float32)[None, :]
    c[:, RP0:RP0 + 512] = qf - p
    for m in range(4):
        c[:, AR0 + 512 * m:AR0 + 512 * (m + 1)] = np.abs(qf - p - 128.0 * m)
    c[:, JF0:JF0 + 128] = j + 1.0
    c[:, JB0:JB0 + 128] = 128.0 - j
    c[:, QF0:QF0 + 512] = qf
    c[:, QR0:QR0 + 512] = 511.0 - qf
    c[:, JC0] = -(p[:, 0] + 1.0)
    c[:, JC0 + 1] = -(128.0 - p[:, 0])
    return c


def make_cbias(T, cross_val):
    NT, NB, SEG = T // 128, T // 512, T // 2
    cb = np.zeros((4, NB, NT), np.float32)
    for h in range(4):
        for Qb in range(NB):
            for kt in range(NT):
                rel = kt - 4 * Qb
                if 0 <= rel < 4:
                    v = 0.0
                elif rel < 0:
                    v = -SLOPES[h] * (512 * Qb - 128 * kt)
                else:
                    v = -SLOPES[h] * (128 * kt - 512 * Qb)
                if (512 * Qb) // SEG != (128 * kt) // SEG:
                    v += cross_val
                cb[h, Qb, kt] = v
    return np.ascontiguousarray(np.broadcast_to(cb.reshape(1, -1), (128, 4 * NB * NT))).astype(np.float32)


def host_weights(inp):
    f = lambda a: np.ascontiguousarray(np.asarray(a, dtype=np.float32))
    w = {}
    w["w_in"] = f(inp["w_in"][0])
    w["w_out"] = f(inp["w_out"][0])
    w["w_up"] = f(inp["w_up"][0])
    w["w_down"] = f(inp["w_down"][0])
    w["w_glu"] = f(inp["w_glu"][0])
    w["gmix_pk"] = f(np.asarray(inp["g_mix_norm"][0]).reshape(8, 128).T)
    w["gffn_pk"] = f(np.asarray(inp["g_ffn_norm"][0]).reshape(8, 128).T)
    w["gfin"] = f(np.asarray(inp["g_final"]).reshape(1, D))
    w["gsub"] = f(np.asarray(inp["g_subln"][0]).reshape(128, 1))
    w["lamv"] = f(np.concatenate([np.asarray(inp[k][0]) for k in ("lambda_q1", "lambda_k1", "lambda_q2", "lambda_k2")]).reshape(1, 256))
    w["cw_pk"] = f(np.asarray(inp["conv_w"][0]).reshape(3, 44, 128).transpose(2, 1, 0))
    w["cb_pk"] = f(np.asarray(inp["conv_b"][0]).reshape(44, 128).T)
    sv = np.zeros((128, 12), np.float32)
    sv[:, 0:4] = np.asarray(inp["ssm_d"][0]).reshape(4, 128).T
    sv[:, 4:8] = np.asarray(inp["b_glu"][0]).reshape(4, 128).T
    sv[:, 8:12] = np.asarray(inp["g_ssm_out"][0]).reshape(4, 128).T
    w["ssmv"] = sv
    a_re = np.asarray(inp["ssm_a_re"][0], np.float32)
    a_im = np.asarray(inp["ssm_a_im"][0], np.float32)
    ls = np.asarray(inp["ssm_log_step"][0], np.float32)
    lsx = np.repeat(ls[:, :, None], 64, axis=2)
    a_row = np.stack([a_re.reshape(2, 2048), a_im.reshape(2, 2048), lsx.reshape(2, 2048)], axis=1)
    w["a_row"] = f(a_row)
    a_sm = np.zeros((2, 128, 48), np.float32)
    for d in range(2):
        a_sm[d, :, 0:16] = a_re[d].reshape(16, 128).T
        a_sm[d, :, 16:32] = a_im[d].reshape(16, 128).T
        a_sm[d, :, 32:48] = lsx[d].reshape(16, 128).T
    w["a_sm"] = a_sm
    b_re = np.asarray(inp["ssm_b_re"][0], np.float32)
    b_im = np.asarray(inp["ssm_b_im"][0], np.float32)
    c_re = np.asarray(inp["ssm_c_re"][0], np.float32)
    c_im = np.asarray(inp["ssm_c_im"][0], np.float32)
    bmA = np.zeros((2, 128, 4, 2, 512), np.float32)
    bmB = np.zeros((2, 128, 4, 2, 512), np.float32)
    cm = np.zeros((2, 128, 8, 4, 128), np.float32)
    for d in range(2):
        for g in range(32):
            kq, gg = g // 8, g % 8
            pq, gl = gg // 4, gg % 4
            rows = slice(gg * 16, gg * 16 + 16)
            bmA[d, rows, kq, pq, gl * 64:gl * 64 + 64] = b_re[d, g].T
            bmA[d, rows, kq, pq, 256 + gl * 64:256 + gl * 64 + 64] = b_im[d, g].T
            bmB[d, rows, kq, pq, gl * 64:gl * 64 + 64] = b_im[d, g].T
            bmB[d, rows, kq, pq, 256 + gl * 64:256 + gl * 64 + 64] = b_re[d, g].T
            p = g // 4
            half, glh = gl // 2, gl % 2
            srows = slice(glh * 64, glh * 64 + 64)
            chc = slice((g % 8) * 16, (g % 8) * 16 + 16)
            cm[d, srows, p, half, chc] = c_re[d, g].T
            cm[d, srows, p, 2 + half, chc] = c_im[d, g].T
    w["bmA"] = bmA.reshape(2, 128, 4096)
    w["bmB"] = bmB.reshape(2, 128, 4096)
    w["cmat"] = cm.reshape(2, 128, 4096)
    w["consts"] = make_consts()
    return w


_NC_CACHE = {}


def kernel(**inputs):
    T = 8192
    w = host_weights(inputs)
    xp = np.asarray(inputs["x_prompt"], np.float32)
    xs = np.asarray(inputs["x_sample"], np.float32)
    cb_p = make_cbias(T, 0.0)
    cb_s = make_cbias(T, -30000.0)
    in_maps = []
    for c in range(8):
        m = dict(w)
        if c < 4:
            m["x"] = np.ascontiguousarray(xp[c])
            m["cbias"] = cb_p
            m["flag"] = np.ones((128, 1), np.float32)
        else:
            m["x"] = np.ascontiguousarray(xs[2 * (c - 4):2 * (c - 4) + 2].reshape(T, D))
            m["cbias"] = cb_s
            m["flag"] = np.zeros((128, 1), np.float32)
        in_maps.append(m)
    if T not in _NC_CACHE:
        _NC_CACHE[T] = build(T)
    res = run_bass_kernel_spmd(_NC_CACHE[T], in_maps, core_ids=list(range(8)))
    ys = [np.asarray(r["y"], np.float32) for r in res.results]
    y_prompt = np.stack(ys[0:4], axis=0)
    y_sample = np.concatenate([ys[c].reshape(2, T // 2, D) for c in range(4, 8)], axis=0)
    return (y_prompt, y_sample)
```

```python
import math
from contextlib import ExitStack
import numpy as np
import ml_dtypes
import concourse.bass as bass
import concourse.mybir as mybir
from concourse.bass_utils import run_bass_kernel_spmd

F32 = mybir.dt.float32
BF16 = mybir.dt.bfloat16
I32 = mybir.dt.int32
AF = mybir.ActivationFunctionType
ALU = mybir.AluOpType
D = 1024
DFF = 2816
SLOPES = [2.0 ** (-2 * (h + 1)) for h in range(4)]
ID0, TL0, TU0, RP0, AR0, JF0, JB0, JC0, QF0, QR0, NCON = 0, 128, 256, 384, 896, 2944, 3072, 3200, 3208, 3720, 4232
TWO_PI = 2.0 * math.pi


class Em:
    def __init__(self, nc, stack):
        self.nc = nc
        self.streams = {k: [] for k in ("sync", "scalar", "vector", "gpsimd", "tensor")}
        self.csem = {e: stack.enter_context(nc.semaphore("c_" + e)) for e in ("scalar", "vector", "gpsimd", "tensor")}
        self.ccnt = {e: 0 for e in self.csem}
        self.NQ = 10
        self.dsem, self.dcnt, self.drr = {}, {}, {}
        for q in ("sync", "gpsimd", "scalar"):
            self.dsem[q] = [stack.enter_context(nc.semaphore(f"d_{q}_{i}")) for i in range(self.NQ)]
            self.dcnt[q] = [0] * self.NQ
            self.drr[q] = 0
        self.waited = {e: {} for e in self.streams}
        self.bw, self.br = {}, {}
        self.nins = 0

    def semh(self, key):
        return self.csem[key] if isinstance(key, str) else self.dsem[key[1]][key[2]]

    def _deps(self, eng, reads, writes):
        deps = {}

        def add(k, v):
            if deps.get(k, 0) < v:
                deps[k] = v

        for b in reads:
            for k, v in self.bw.get(b, {}).items():
                add(k, v)
        for b in writes:
            for k, v in self.bw.get(b, {}).items():
                if k != eng:
                    add(k, v)
            for k, v in self.br.get(b, {}).items():
                if k != eng:
                    add(k, v)
        return deps

    def _waits(self, eng, deps):
        waits = []
        for k, v in deps.items():
            if self.waited[eng].get(k, 0) >= v:
                continue
            self.waited[eng][k] = v
            waits.append((self.semh(k), v))
        return waits

    def _record(self, key, val, reads, writes, cwrites):
        for b in writes:
            self.bw[b] = {key: val}
            self.br[b] = {}
        for b in cwrites:
            d = self.bw.setdefault(b, {})
            d[key] = max(d.get(key, 0), val)
        for b in reads:
            d = self.br.setdefault(b, {})
            d[key] = max(d.get(key, 0), val)

    def op(self, eng, fn, reads=(), writes=()):
        waits = self._waits(eng, self._deps(eng, reads, writes))
        self.ccnt[eng] += 1
        n = self.ccnt[eng]
        sem = self.csem[eng]

        def run(e, waits=waits, fn=fn, sem=sem):
            for s, v in waits:
                e.wait_ge(s, v)
            fn(e).then_inc(sem, 1)

        self.streams[eng].append(run)
        self._record(eng, n, reads, writes, ())
        self.nins += 1

    def dma(self, q, out, in_, reads=(), writes=(), cwrites=(), **kw):
        slot = self.drr[q]
        self.drr[q] = (slot + 1) % self.NQ
        key = ("dma", q, slot)
        deps = self._deps(q, reads, writes)
        prev = self.dcnt[q][slot]
        if prev > 0:
            deps[key] = max(deps.get(key, 0), prev)
        waits = self._waits(q, deps)
        self.dcnt[q][slot] += 16
        v = self.dcnt[q][slot]
        sem = self.dsem[q][slot]

        def run(e, waits=waits, sem=sem, out=out, in_=in_, kw=kw):
            for s, vv in waits:
                e.wait_ge(s, vv)
            e.dma_start(out=out, in_=in_, **kw).then_inc(sem, 16)

        self.streams[q].append(run)
        self._record(key, v, reads, writes, cwrites)
        self.nins += 1

    def barrier(self):
        allk = {e: n for e, n in self.ccnt.items() if n > 0}
        for q in self.dsem:
            for i in range(self.NQ):
                if self.dcnt[q][i] > 0:
                    allk[("dma", q, i)] = self.dcnt[q][i]
        for eng in self.streams:
            waits = self._waits(eng, {k: v for k, v in allk.items() if k != eng})

            def run(e, waits=waits):
                for s, v in waits:
                    e.wait_ge(s, v)

            self.streams[eng].append(run)

    def finish(self):
        self.barrier()
        nc = self.nc
        with nc.Block() as block:
            @block.sync
            def _(e):
                for f in self.streams["sync"]:
                    f(e)

            @block.scalar
            def _(e):
                for f in self.streams["scalar"]:
                    f(e)

            @block.vector
            def _(e):
                for f in self.streams["vector"]:
                    f(e)

            @block.gpsimd
            def _(e):
                for f in self.streams["gpsimd"]:
                    f(e)

            @block.tensor
            def _(e):
                for f in self.streams["tensor"]:
                    f(e)


def build(T, dbg=False, phases="ABCDEF"):
    NT, NB, SEG = T // 128, T // 512, T // 2
    SEGP = SEG + 2
    nc = bass.Bass("TRN2", target_bir_lowering=False)

    def din(name, shape, dt=F32):
        return nc.dram_tensor(name, list(shape), dt, kind="ExternalInput").ap()

    skind = "ExternalOutput" if dbg else "Internal"

    def dscr(name, shape, dt):
        return nc.dram_tensor(name, list(shape), dt, kind=skind).ap()

    x = din("x", [T, D])
    w_in = din("w_in", [D, 2048])
    w_out = din("w_out", [D, D])
    w_up = din("w_up", [D, 2 * DFF])
    w_down = din("w_down", [DFF, D])
    w_glu = din("w_glu", [512, 512])
    gmix_pk = din("gmix_pk", [128, 8])
    gffn_pk = din("gffn_pk", [128, 8])
    gfin = din("gfin", [1, D])
    gsub = din("gsub", [128, 1])
    lamv = din("lamv", [1, 256])
    cw_pk = din("cw_pk", [128, 44, 3])
    cb_pk = din("cb_pk", [128, 44])
    ssmv = din("ssmv", [128, 12])
    a_row = din("a_row", [2, 3, 2048])
    a_sm = din("a_sm", [2, 128, 48])
    bmA = din("bmA", [2, 128, 4096])
    bmB = din("bmB", [2, 128, 4096])
    cmat = din("cmat", [2, 128, 4096])
    consts = din("consts", [128, NCON])
    cbias = din("cbias", [128, 4 * NB * NT])
    flag = din("flag", [128, 1])
    y = nc.dram_tensor("y", [T, D], F32, kind="ExternalOutput").ap()

    q_s = dscr("q_s", [512, T], BF16)
    k_s = dscr("k_s", [512, T], BF16)
    u_s = dscr("u_s", [512, T], BF16)
    v_s = dscr("v_s", [T, 512], BF16)
    a_s = dscr("a_s", [512, T], BF16)
    s_s = dscr("s_s", [512, T], BF16)
    yf_s = dscr("yf_s", [512, T], F32)
    ys_s = dscr("ys_s", [512, T], F32)
    x1_s = dscr("x1_s", [T, D], F32)
    h_s = dscr("h_s", [8, 128, 2 * SEGP], BF16)
    wus = dscr("wus", [44, 128, 8, 128], BF16)

    top = ExitStack()
    em = Em(nc, top)

    def rstd_chain(st4, key, scale, eps):
        em.op("vector", lambda e: e.tensor_scalar(out=st4[:, 1:2], in0=st4[:, 0:1], scalar1=scale, scalar2=eps,
                                                  op0=ALU.mult, op1=ALU.add), [key + "_0"], [key + "_1"])
        em.op("scalar", lambda e: e.activation(out=st4[:, 2:3], in_=st4[:, 1:2], func=AF.Sqrt), [key + "_1"], [key + "_2"])
        em.op("vector", lambda e: e.reciprocal(out=st4[:, 3:4], in_=st4[:, 2:3]), [key + "_2"], [key + "_3"])

    def phA():
        with ExitStack() as st:
            sb = lambda n, s, dt=F32: st.enter_context(nc.sbuf_tensor(n, list(s), dt))
            ps = lambda n, s, dt=F32: st.enter_context(nc.psum_tensor(n, list(s), dt))
            identf = sb("a_identf", [128, 128])
            identb = sb("a_identb", [128, 128], BF16)
            em.dma("sync", identf[:], consts[:, ID0:ID0 + 128], [], ["identf"])
            em.op("vector", lambda e: e.tensor_copy(out=identb[:], in_=identf[:]), ["identf"], ["identb"])
            gm = sb("a_gm", [128, 8])
            em.dma("sync", gm[:], gmix_pk[:, :], [], ["gm"])
            wbf = sb("a_wbf", [128, 8, 2048], BF16)
            wst = [sb(f"a_wst{i}", [128, 2048]) for i in range(2)]
            for k in range(8):
                b = k % 2
                em.dma("sync", wst[b][:], w_in[k * 128:(k + 1) * 128, :], [], [f"wst{b}"])
                if b == 0:
                    em.op("vector", lambda e, k=k, b=b: e.tensor_scalar(out=wbf[:, k, :], in0=wst[b][:], scalar1=gm[:, k:k + 1],
                                                                        scalar2=None, op0=ALU.mult), [f"wst{b}", "gm"], [f"wbf{k}"])
                else:
                    em.op("scalar", lambda e, k=k, b=b: e.activation(out=wbf[:, k, :], in_=wst[b][:], func=AF.Copy,
                                                                     scale=gm[:, k:k + 1]), [f"wst{b}", "gm"], [f"wbf{k}"])
            xt = [sb(f"a_xt{i}", [128, D]) for i in range(3)]
            junk = sb("a_junk", [128, D], BF16)
            xn = [sb(f"a_xn{i}", [128, D], BF16) for i in range(2)]
            s4 = [sb(f"a_s4{i}", [128, 4]) for i in range(2)]
            xT = [sb(f"a_xT{i}", [128, 8, 512], BF16) for i in range(3)]
            tp = [ps(f"a_tp{i}", [128, D], BF16) for i in range(2)]
            pp = [ps(f"a_pp{i}", [128, 512]) for i in range(4)]
            stg = [sb(f"a_stg{i}", [128, 512], BF16) for i in range(4)]
            cntb = [0]
            wk = [f"wbf{k}" for k in range(8)]

            def a_stage1(blk):
                xb = blk % 3
                for i in range(4):
                    t = blk * 4 + i
                    a, n2 = t % 3, t % 2
                    em.dma("sync", xt[a][:], x[t * 128:(t + 1) * 128, :], [], [f"xt{a}"])
                    em.op("scalar", lambda e, a=a, n2=n2: e.activation(out=junk[:], in_=xt[a][:], func=AF.Square,
                                                                       accum_out=s4[n2][:, 0:1]), [f"xt{a}"], ["junk", f"s4{n2}_0"])
                    rstd_chain(s4[n2], f"s4{n2}", 1.0 / D, 1e-6)
                    em.op("scalar", lambda e, a=a, n2=n2: e.activation(out=xn[n2][:], in_=xt[a][:], func=AF.Copy,
                                                                       scale=s4[n2][:, 3:4]), [f"xt{a}", f"s4{n2}_3"], [f"xn{n2}"])
                    for k in range(8):
                        em.op("tensor", lambda e, k=k, n2=n2: e.transpose(out=tp[n2][:, k * 128:(k + 1) * 128],
                                                                          in_=xn[n2][:, k * 128:(k + 1) * 128], identity=identb[:]),
                              [f"xn{n2}", "identb"], [f"tp{n2}"])
                    em.op("vector", lambda e, i=i, n2=n2, xb=xb: e.tensor_copy(
                        out=xT[xb][:, :, i * 128:(i + 1) * 128], in_=tp[n2][:].rearrange("p (k j) -> p k j", k=8)),
                          [f"tp{n2}"], [f"xT{xb}"])

            def a_stage2(blk):
                xb = blk % 3
                for m in range(16):
                    pb = cntb[0] % 4
                    sg = cntb[0] % 4
                    cntb[0] += 1
                    if m < 12:
                        col = m * 128 if m < 8 else 1536 + (m - 8) * 128
                        for k in range(8):
                            em.op("tensor", lambda e, k=k, col=col, pb=pb, xb=xb: e.matmul(
                                pp[pb][:], lhsT=wbf[:, k, col:col + 128], rhs=xT[xb][:, k, :], start=(k == 0), stop=(k == 7)),
                                  [wk[k], f"xT{xb}"], [f"pp{pb}"])
                        dst = (q_s if m < 4 else k_s if m < 8 else u_s)[(m % 4) * 128:(m % 4 + 1) * 128, blk * 512:(blk + 1) * 512]
                        dk = "q_s" if m < 4 else "k_s" if m < 8 else "u_s"
                    else:
                        i = m - 12
                        for k in range(8):
                            em.op("tensor", lambda e, k=k, i=i, pb=pb, xb=xb: e.matmul(
                                pp[pb][:], lhsT=xT[xb][:, k, i * 128:(i + 1) * 128], rhs=wbf[:, k, 1024:1536],
                                start=(k == 0), stop=(k == 7)), [wk[k], f"xT{xb}"], [f"pp{pb}"])
                        dst = v_s[(blk * 4 + i) * 128:(blk * 4 + i + 1) * 128, :]
                        dk = "v_s"
                    if m < 4:
                        em.op("scalar", lambda e, pb=pb, sg=sg: e.activation(out=stg[sg][:], in_=pp[pb][:], func=AF.Copy, scale=0.125),
                              [f"pp{pb}"], [f"stg{sg}"])
                    elif m % 2 == 0:
                        em.op("vector", lambda e, pb=pb, sg=sg: e.tensor_copy(out=stg[sg][:], in_=pp[pb][:]), [f"pp{pb}"], [f"stg{sg}"])
                    else:
                        em.op("scalar", lambda e, pb=pb, sg=sg: e.activation(out=stg[sg][:], in_=pp[pb][:], func=AF.Copy),
                              [f"pp{pb}"], [f"stg{sg}"])
                    em.dma("gpsimd", dst, stg[sg][:], [f"stg{sg}"], cwrites=[dk])
            a_stage1(0)
            if NB > 1:
                a_stage1(1)
            for blk in range(NB):
                if blk + 2 < NB:
                    a_stage1(blk + 2)
                a_stage2(blk)
            em.barrier()

    def phB():
        with ExitStack() as st:
            sb = lambda n, s, dt=F32: st.enter_context(nc.sbuf_tensor(n, list(s), dt))
            ps = lambda n, s, dt=F32: st.enter_context(nc.psum_tensor(n, list(s), dt))
            ramp = sb("b_ramp", [128, 512])
            absr = sb("b_absr", [128, 2048])
            em.dma("sync", ramp[:], consts[:, RP0:RP0 + 512], [], ["ramps"])
            em.dma("sync", absr[:], consts[:, AR0:AR0 + 2048], [], ["absr"])
            cb = sb("b_cb", [128, 4 * NB * NT])
            em.dma("sync", cb[:], cbias[:, :], [], ["cb"])
            onesf = sb("b_onesf", [128, 128])
            onesb = sb("b_onesb", [128, 128], BF16)
            em.op("vector", lambda e: e.memset(onesf[:], 1.0), [], ["onesf"])
            em.op("vector", lambda e: e.memset(onesb[:], 1.0), [], ["onesb"])
            lv = sb("b_lv", [128, 256])
            em.dma("sync", lv[:], lamv[0:1, :].to_broadcast([128, 256]), [], ["lv"])
            lp = sb("b_lp", [128, 128])
            l4 = sb("b_l4", [128, 8])
            em.op("vector", lambda e: e.tensor_tensor(out=lp[:, 0:64], in0=lv[:, 0:64], in1=lv[:, 64:128], op=ALU.mult), ["lv"], ["lp0"])
            em.op("vector", lambda e: e.tensor_tensor(out=lp[:, 64:128], in0=lv[:, 128:192], in1=lv[:, 192:256], op=ALU.mult), ["lv"], ["lp1"])
            em.op("scalar", lambda e: e.activation(out=lv[:, 0:64], in_=lp[:, 0:64], func=AF.Copy, accum_out=l4[:, 0:1]), ["lp0"], ["l40", "lvj"])
            em.op("scalar", lambda e: e.activation(out=lv[:, 64:128], in_=lp[:, 64:128], func=AF.Copy, accum_out=l4[:, 1:2]), ["lp1"], ["l41", "lvj2"])
            em.op("scalar", lambda e: e.activation(out=l4[:, 2:4], in_=l4[:, 0:2], func=AF.Exp), ["l40", "l41"], ["l42"])
            em.op("vector", lambda e: e.tensor_tensor(out=l4[:, 4:5], in0=l4[:, 3:4], in1=l4[:, 2:3], op=ALU.subtract), ["l42"], ["l44"])
            em.op("vector", lambda e: e.tensor_scalar(out=l4[:, 5:6], in0=l4[:, 4:5], scalar1=-0.2, scalar2=None, op0=ALU.add), ["l44"], ["neglam"])
            epsb = sb("b_epsb", [128, 1])
            em.op("vector", lambda e: e.memset(epsb[:], 1e-5), [], ["epsb"])
            gs = sb("b_gs", [128, 2])
            em.dma("sync", gs[:, 0:1], gsub[:, :], [], ["gs0"])
            em.op("vector", lambda e: e.tensor_scalar(out=gs[:, 1:2], in0=gs[:, 0:1], scalar1=0.8, scalar2=None, op0=ALU.mult), ["gs0"], ["gs1"])

            KT = [sb(f"b_KT{i}", [128, T], BF16) for i in range(2)]
            QT = [[sb(f"b_QT{i}_{c}", [128, T], BF16) for c in range(2)] for i in range(2)]
            for i in range(2):
                em.op("gpsimd", lambda e, i=i: e.memset(QT[i][0][64:128, :], 0.0), [], [f"QTz{i}0"])
                em.op("gpsimd", lambda e, i=i: e.memset(QT[i][1][0:64, :], 0.0), [], [f"QTz{i}1"])
            VV = [sb(f"b_VV{i}", [128, NT, 128], BF16) for i in range(2)]
            NSC = 4
            scp = [ps(f"b_scp{i}", [128, 512]) for i in range(NSC)]
            Op = [ps(f"b_O{i}", [128, 512]) for i in range(2)]
            Sp = [ps(f"b_S{i}", [128, 512]) for i in range(2)]
            NSB = 4
            sbt = [sb(f"b_sbt{i}", [128, 512]) for i in range(NSB)]
            NPT = 6
            pT = [sb(f"b_pT{i}", [128, 512], BF16) for i in range(NPT)]
            rs = [sb(f"b_rs{i}", [128, 512]) for i in range(2)]
            oc = [sb(f"b_oc{i}", [128, 512]) for i in range(2)]
            wt = sb("b_wt", [128, 512])
            sq = sb("b_sq", [128, 512])
            a1 = sb("b_a1", [128, 512])
            aT = [sb(f"b_aT{i}", [128, 512], BF16) for i in range(2)]
            LA = 3
            THR = 40.0
            v_r = v_s.rearrange("(kt kp) e -> kp kt e", kp=128)
            nq = 0
            sc_box = [0]
            pending = []
            for h in range(4):
                hb = h % 2
                em.dma("sync", KT[hb][:], k_s[h * 128:(h + 1) * 128, :], ["k_s"], [f"KT{hb}"])
                em.dma("sync", QT[hb][0][0:64, :], q_s[h * 128:h * 128 + 64, :], ["q_s", f"QTz{hb}0"], [f"QT{hb}_0"])
                em.dma("sync", QT[hb][1][64:128, :], q_s[h * 128 + 64:(h + 1) * 128, :], ["q_s", f"QTz{hb}1"], [f"QT{hb}_1"])
                VCH = max(1, NT // 4)
                for j in range(0, NT, VCH):
                    em.dma("sync", VV[hb][:, j:j + VCH, :], v_r[:, j:j + VCH, h * 128:(h + 1) * 128], ["v_s"], cwrites=[f"VV{hb}"],
                           writes=([f"VV{hb}"] if j == 0 else []))
                units = []
                for Qb in range(NB):
                    kept = []
                    for kt in range(NT):
                        rel = kt - 4 * Qb
                        if rel < 0:
                            dmin = 512 * Qb - (128 * kt + 127)
                        elif rel >= 4:
                            dmin = 128 * kt - (512 * Qb + 511)
                        else:
                            dmin = 0
                        if SLOPES[h] * dmin <= THR:
                            kept.append(kt)
                    for c in range(2):
                        for ii, kt in enumerate(kept):
                            units.append((Qb, c, kt, ii == 0, ii == len(kept) - 1))
                N = len(units)
                slot = {}
                for i in range(N + LA):
                    if i < N:
                        Qb, c, kt, first, last = units[i]
                        s3 = sc_box[0] % NSC
                        sc_box[0] += 1
                        s4_, p5 = i % NSB, i % NPT
                        slot[i] = p5
                        em.op("tensor", lambda e, c=c, kt=kt, Qb=Qb, s3=s3, hb=hb: e.matmul(
                            scp[s3][:], lhsT=KT[hb][:, kt * 128:(kt + 1) * 128],
                            rhs=QT[hb][c][:, Qb * 512:(Qb + 1) * 512], start=True, stop=True),
                              [f"KT{hb}", f"QT{hb}_{c}"], [f"scp{s3}"])
                        rel = kt - 4 * Qb
                        if 0 <= rel < 4:
                            tab, sc, tk = absr[:, rel * 512:(rel + 1) * 512], -SLOPES[h], "absr"
                        elif rel < 0:
                            tab, sc, tk = ramp[:], -SLOPES[h], "ramps"
                        else:
                            tab, sc, tk = ramp[:], SLOPES[h], "ramps"
                        em.op("vector", lambda e, tab=tab, sc=sc, s3=s3, s4_=s4_: e.scalar_tensor_tensor(
                            out=sbt[s4_][:], in0=tab, scalar=sc, in1=scp[s3][:], op0=ALU.mult, op1=ALU.add),
                              [f"scp{s3}", tk], [f"sbt{s4_}"])
                        col = (h * NB + Qb) * NT + kt
                        em.op("scalar", lambda e, s4_=s4_, p5=p5, col=col: e.activation(
                            out=pT[p5][:], in_=sbt[s4_][:], func=AF.Exp, bias=cb[:, col:col + 1]),
                              [f"sbt{s4_}", "cb"], [f"pT{p5}"])
                    j = i - LA
                    if j >= 0:
                        Qb, c, kt, first, last = units[j]
                        p5 = slot[j]
                        em.op("tensor", lambda e, c=c, kt=kt, p5=p5, hb=hb, first=first, last=last: e.matmul(
                            Op[c][:], lhsT=VV[hb][:, kt, :], rhs=pT[p5][:], start=first, stop=last),
                              [f"VV{hb}", f"pT{p5}"], [f"O{c}"])
                        em.op("tensor", lambda e, c=c, p5=p5, first=first, last=last: e.matmul(
                            Sp[c][:], lhsT=onesb[:], rhs=pT[p5][:], start=first, stop=last),
                              ["onesb", f"pT{p5}"], [f"S{c}"])
                        if last:
                            def mk_rc(c, k):
                                return lambda: em.op("vector", lambda e: e.reciprocal(out=rs[c][:, k * 128:(k + 1) * 128], in_=Sp[c][:, k * 128:(k + 1) * 128]),
                                                     [f"S{c}"], [f"rs{c}_{k}"])
                            for k in range(4):
                                pending.append([i + k, mk_rc(c, k)])

                            def mk_oc(c):
                                return lambda: em.op("vector", lambda e: e.tensor_tensor(out=oc[c][:], in0=Op[c][:], in1=rs[c][:], op=ALU.mult),
                                                     [f"O{c}"] + [f"rs{c}_{k}" for k in range(4)], [f"oc{c}"])
                            pending.append([i + 4, mk_oc(c)])
                            if c == 1:
                                ab = nq % 2
                                nq += 1

                                def fin_a():
                                    em.op("vector", lambda e: e.scalar_tensor_tensor(out=wt[:], in0=oc[1][:], scalar=l4[:, 5:6], in1=oc[0][:],
                                                                                     op0=ALU.mult, op1=ALU.add), ["oc0", "oc1", "neglam"], ["wt"])
                                    em.op("gpsimd", lambda e: e.tensor_tensor(out=sq[:], in0=wt[:], in1=wt[:], op=ALU.mult), ["wt"], ["sq"])

                                def fin_b(h=h):
                                    q3 = sc_box[0] % NSC
                                    sc_box[0] += 1
                                    em.op("tensor", lambda e: e.matmul(scp[q3][:], lhsT=onesf[:], rhs=sq[:], start=True, stop=True),
                                          ["onesf", "sq"], [f"scp{q3}"])
                                    em.op("scalar", lambda e: e.activation(out=a1[:], in_=scp[q3][:], func=AF.Ln, scale=1.0 / 128, bias=epsb[:, 0:1]),
                                          [f"scp{q3}", "epsb"], ["a1"])
                                    em.op("scalar", lambda e: e.activation(out=a1[:], in_=a1[:], func=AF.Exp, scale=-0.5), ["a1"], ["a1"])

                                def fin_c(h=h, Qb=Qb, ab=ab):
                                    em.op("vector", lambda e: e.tensor_tensor(out=a1[:], in0=wt[:], in1=a1[:], op=ALU.mult), ["wt", "a1"], ["a1"])
                                    em.op("scalar", lambda e: e.activation(out=aT[ab][:], in_=a1[:], func=AF.Copy, scale=gs[:, 1:2]),
                                          ["a1", "gs1"], [f"aT{ab}"])
                                    em.dma("gpsimd", a_s[h * 128:(h + 1) * 128, Qb * 512:(Qb + 1) * 512], aT[ab][:], [f"aT{ab}"], cwrites=["a_s"])
                                pending.append([i + 5, fin_a])
                                pending.append([i + 8, fin_b])
                                pending.append([i + 10, fin_c])
                    due = [p_ for p_ in pending if p_[0] <= i]
                    for p_ in due:
                        pending.remove(p_)
                        p_[1]()
                for p_ in list(pending):
                    p_[1]()
                pending.clear()
            em.barrier()

    def phC(d):
        if True:
            with ExitStack() as st:
                sb = lambda n, s, dt=F32: st.enter_context(nc.sbuf_tensor(n, list(s), dt))
                ps = lambda n, s, dt=F32: st.enter_context(nc.psum_tensor(n, list(s), dt))
                P = f"c{d}_"
                JC = JC0 + d
                JR = JF0 if d == 0 else JB0
                jc = sb(P + "jc", [128, 1])
                em.dma("sync", jc[:], consts[:, JC:JC + 1], [], ["jc"], allow_slow_non_contiguous=True)
                jrow = sb(P + "jrow", [128, 128])
                em.dma("sync", jrow[:], consts[:, JR:JR + 128], [], ["jrow"])
                tri = sb(P + "tri", [128, 128])
                T0 = TL0 if d == 0 else TU0
                em.dma("sync", tri[:], consts[:, T0:T0 + 128], [], ["trif"])
                trib = sb(P + "trib", [128, 128], BF16)
                ntrib = sb(P + "ntrib", [128, 128], BF16)
                em.op("vector", lambda e: e.tensor_copy(out=trib[:], in_=tri[:]), ["trif"], ["trib"])
                em.op("vector", lambda e: e.tensor_scalar(out=ntrib[:], in0=tri[:], scalar1=-1.0, scalar2=None, op0=ALU.mult), ["trif"], ["ntrib"])
                Er = sb(P + "Er", [128, 2048])
                Ei = sb(P + "Ei", [128, 2048])
                Fr = sb(P + "Fr", [128, 2048])
                Fi = sb(P + "Fi", [128, 2048])
                with ExitStack() as st2:
                    sb2 = lambda n, s, dt=F32: st2.enter_context(nc.sbuf_tensor(n, list(s), dt))
                    W = [sb2(P + f"w{i}", [128, 2048]) for i in range(10)]
                    WI = sb2(P + "wi", [128, 2048], I32)
                    wkey = [f"W{i}" for i in range(10)]

                    def vop(fn, r, w):
                        em.op("vector", fn, r, w)

                    def sincos(ph, phk, out_s, out_c, ok_s, ok_c, tmp, tmpk, tmp2, tmp2k):
                        for (off, o, okk) in ((0.0, out_s, ok_s), (math.pi / 2, out_c, ok_c)):
                            vop(lambda e, off=off: e.tensor_scalar(out=tmp[:], in0=ph[:], scalar1=off, scalar2=1.0 / TWO_PI,
                                                                   op0=ALU.add, op1=ALU.mult), [phk], [tmpk])
                            vop(lambda e: e.tensor_copy(out=WI[:], in_=tmp[:]), [tmpk], ["WI"])
                            vop(lambda e: e.tensor_copy(out=tmp[:], in_=WI[:]), ["WI"], [tmpk])
                            vop(lambda e: e.scalar_tensor_tensor(out=tmp2[:], in0=tmp[:], scalar=-TWO_PI, in1=ph[:],
                                                                 op0=ALU.mult, op1=ALU.add), [tmpk, phk], [tmp2k])
                            vop(lambda e, off=off: e.tensor_scalar(out=tmp2[:], in0=tmp2[:], scalar1=off, scalar2=math.pi,
                                                                   op0=ALU.add, op1=ALU.min), [tmp2k], [tmp2k])
                            vop(lambda e: e.tensor_scalar(out=tmp2[:], in0=tmp2[:], scalar1=-math.pi, scalar2=None, op0=ALU.max), [tmp2k], [tmp2k])
                            em.op("scalar", lambda e, o=o: e.activation(out=o[:], in_=tmp2[:], func=AF.Sin), [tmp2k], [okk])

                    AR, AI, LS = W[0], W[1], W[2]
                    em.dma("sync", AR[:], a_row[d, 0:1, :].to_broadcast([128, 2048]), [], [wkey[0]])
                    em.dma("sync", AI[:], a_row[d, 1:2, :].to_broadcast([128, 2048]), [], [wkey[1]])
                    em.dma("sync", LS[:], a_row[d, 2:3, :].to_broadcast([128, 2048]), [], [wkey[2]])
                    em.op("scalar", lambda e: e.activation(out=LS[:], in_=LS[:], func=AF.Exp), [wkey[2]], [wkey[2]])
                    ars, ais = W[3], W[4]
                    vop(lambda e: e.tensor_tensor(out=ars[:], in0=AR[:], in1=LS[:], op=ALU.mult), [wkey[0], wkey[2]], [wkey[3]])
                    vop(lambda e: e.tensor_tensor(out=ais[:], in0=AI[:], in1=LS[:], op=ALU.mult), [wkey[1], wkey[2]], [wkey[4]])
                    sincos(ais, wkey[4], W[5], W[6], wkey[5], wkey[6], W[7], wkey[7], W[8], wkey[8])
                    mag = W[7]
                    em.op("scalar", lambda e: e.activation(out=mag[:], in_=ars[:], func=AF.Exp), [wkey[3]], [wkey[7]])
                    lbr, lbi = W[6], W[5]
                    vop(lambda e: e.tensor_tensor(out=lbr[:], in0=W[6][:], in1=mag[:], op=ALU.mult), [wkey[6], wkey[7]], [wkey[6]])
                    vop(lambda e: e.tensor_tensor(out=lbi[:], in0=W[5][:], in1=mag[:], op=ALU.mult), [wkey[5], wkey[7]], [wkey[5]])
                    vop(lambda e: e.tensor_scalar(out=lbr[:], in0=lbr[:], scalar1=-1.0, scalar2=None, op0=ALU.add), [wkey[6]], [wkey[6]])
                    den = W[7]
                    vop(lambda e: e.tensor_tensor(out=den[:], in0=AR[:], in1=AR[:], op=ALU.mult), [wkey[0]], [wkey[7]])
                    vop(lambda e: e.tensor_tensor(out=W[8][:], in0=AI[:], in1=AI[:], op=ALU.mult), [wkey[1]], [wkey[8]])
                    vop(lambda e: e.tensor_tensor(out=den[:], in0=den[:], in1=W[8][:], op=ALU.add), [wkey[7], wkey[8]], [wkey[7]])
                    vop(lambda e: e.reciprocal(out=den[:], in_=den[:]), [wkey[7]], [wkey[7]])
                    vop(lambda e: e.tensor_tensor(out=W[8][:], in0=lbr[:], in1=AR[:], op=ALU.mult), [wkey[6], wkey[0]], [wkey[8]])
                    vop(lambda e: e.tensor_tensor(out=W[9][:], in0=lbi[:], in1=AI[:], op=ALU.mult), [wkey[5], wkey[1]], [wkey[9]])
                    vop(lambda e: e.tensor_tensor(out=W[8][:], in0=W[8][:], in1=W[9][:], op=ALU.add), [wkey[8], wkey[9]], [wkey[8]])
                    vop(lambda e: e.tensor_tensor(out=W[8][:], in0=W[8][:], in1=den[:], op=ALU.mult), [wkey[8], wkey[7]], [wkey[8]])
                    vop(lambda e: e.tensor_tensor(out=W[9][:], in0=lbi[:], in1=AR[:], op=ALU.mult), [wkey[5], wkey[0]], [wkey[9]])
                    vop(lambda e: e.tensor_tensor(out=W[2][:], in0=lbr[:], in1=AI[:], op=ALU.mult), [wkey[6], wkey[1]], [wkey[2]])
                    vop(lambda e: e.tensor_tensor(out=W[9][:], in0=W[9][:], in1=W[2][:], op=ALU.subtract), [wkey[9], wkey[2]], [wkey[9]])
                    vop(lambda e: e.tensor_tensor(out=W[9][:], in0=W[9][:], in1=den[:], op=ALU.mult), [wkey[9], wkey[7]], [wkey[9]])
                    fre, fim = W[8], W[9]
                    ph = W[0]
                    vop(lambda e: e.tensor_scalar(out=ph[:], in0=ais[:], scalar1=jc[:, 0:1], scalar2=None, op0=ALU.mult), [wkey[4], "jc"], [wkey[0]])
                    emag = W[1]
                    em.op("scalar", lambda e: e.activation(out=emag[:], in_=ars[:], func=AF.Exp, scale=jc[:, 0:1]), [wkey[3], "jc"], [wkey[1]])
                    sincos(ph, wkey[0], W[5], W[6], wkey[5], wkey[6], W[7], wkey[7], W[2], wkey[2])
                    vop(lambda e: e.tensor_tensor(out=W[7][:], in0=fre[:], in1=W[6][:], op=ALU.mult), [wkey[8], wkey[6]], [wkey[7]])
                    vop(lambda e: e.tensor_tensor(out=W[2][:], in0=fim[:], in1=W[5][:], op=ALU.mult), [wkey[9], wkey[5]], [wkey[2]])
                    vop(lambda e: e.tensor_tensor(out=W[7][:], in0=W[7][:], in1=W[2][:], op=ALU.subtract), [wkey[7], wkey[2]], [wkey[7]])
                    vop(lambda e: e.tensor_tensor(out=Er[:], in0=W[7][:], in1=emag[:], op=ALU.mult), [wkey[7], wkey[1]], ["Er"])
                    vop(lambda e: e.tensor_tensor(out=W[7][:], in0=fre[:], in1=W[5][:], op=ALU.mult), [wkey[8], wkey[5]], [wkey[7]])
                    vop(lambda e: e.tensor_tensor(out=W[2][:], in0=fim[:], in1=W[6][:], op=ALU.mult), [wkey[9], wkey[6]], [wkey[2]])
                    vop(lambda e: e.tensor_tensor(out=W[7][:], in0=W[7][:], in1=W[2][:], op=ALU.add), [wkey[7], wkey[2]], [wkey[7]])
                    vop(lambda e: e.tensor_tensor(out=Ei[:], in0=W[7][:], in1=emag[:], op=ALU.mult), [wkey[7], wkey[1]], ["Ei"])
                    asm = sb2(P + "asm", [128, 48])
                    em.dma("sync", asm[:], a_sm[d, :, :], [], ["asm"])
                    em.op("scalar", lambda e: e.activation(out=asm[:, 32:48], in_=asm[:, 32:48], func=AF.Exp), ["asm"], ["asm"])
                    vop(lambda e: e.tensor_tensor(out=asm[:, 0:16], in0=asm[:, 0:16], in1=asm[:, 32:48], op=ALU.mult), ["asm"], ["asm"])
                    vop(lambda e: e.tensor_tensor(out=asm[:, 16:32], in0=asm[:, 16:32], in1=asm[:, 32:48], op=ALU.mult), ["asm"], ["asm"])
                    v3 = lambda t: t[:].rearrange("p (a b) -> p a b", a=16)
                    jb3 = jrow[:, :].unsqueeze(1).to_broadcast([128, 16, 128])
                    fa = W[0]
                    vop(lambda e: e.tensor_tensor(out=v3(fa), in0=asm[:, 0:16].unsqueeze(2).to_broadcast([128, 16, 128]), in1=jb3, op=ALU.mult),
                        ["asm", "jrow"], [wkey[0]])
                    fmag = W[1]
                    em.op("scalar", lambda e: e.activation(out=fmag[:], in_=fa[:], func=AF.Exp), [wkey[0]], [wkey[1]])
                    fph = W[3]
                    vop(lambda e: e.tensor_tensor(out=v3(fph), in0=asm[:, 16:32].unsqueeze(2).to_broadcast([128, 16, 128]), in1=jb3, op=ALU.mult),
                        ["asm", "jrow"], [wkey[3]])
                    sincos(fph, wkey[3], W[5], W[6], wkey[5], wkey[6], W[7], wkey[7], W[2], wkey[2])
                    vop(lambda e: e.tensor_tensor(out=Fr[:], in0=W[6][:], in1=fmag[:], op=ALU.mult), [wkey[6], wkey[1]], ["Fr"])
                    vop(lambda e: e.tensor_tensor(out=Fi[:], in0=W[5][:], in1=fmag[:], op=ALU.mult), [wkey[5], wkey[1]], ["Fi"])
                    em.barrier()
                BA = sb(P + "BA", [128, 4096], BF16)
                BB = sb(P + "BB", [128, 4096], BF16)
                CM = sb(P + "CM", [128, 4096], BF16)
                with ExitStack() as st2:
                    sb2 = lambda n, s, dt=F32: st2.enter_context(nc.sbuf_tensor(n, list(s), dt))
                    wa = sb2(P + "wa", [128, 4096])
                    wb = sb2(P + "wb", [128, 4096])
                    wc = sb2(P + "wc", [128, 4096])
                    em.dma("sync", wa[:], bmA[d, :, :], [], ["wa"])
                    em.dma("sync", wb[:], bmB[d, :, :], [], ["wb"])
                    em.dma("sync", wc[:], cmat[d, :, :], [], ["wc"])
                    em.op("vector", lambda e: e.tensor_copy(out=BA[:], in_=wa[:]), ["wa"], ["BA"])
                    w4 = lambda t: t[:].rearrange("p (a h c) -> p a h c", a=8, h=2)
                    em.op("vector", lambda e: e.tensor_scalar(out=w4(BB)[:, :, 0, :], in0=w4(wb)[:, :, 0, :], scalar1=-1.0, scalar2=None, op0=ALU.mult), ["wb"], ["BB0"])
                    em.op("vector", lambda e: e.tensor_copy(out=w4(BB)[:, :, 1, :], in_=w4(wb)[:, :, 1, :]), ["wb"], ["BB1"])
                    c5 = lambda t: t[:].rearrange("p (a h c) -> p a h c", a=8, h=2)
                    em.op("vector", lambda e: e.tensor_copy(out=c5(CM)[:, :, 0, :], in_=c5(wc)[:, :, 0, :]), ["wc"], ["CM0"])
                    em.op("vector", lambda e: e.tensor_scalar(out=c5(CM)[:, :, 1, :], in0=c5(wc)[:, :, 1, :], scalar1=-1.0, scalar2=None, op0=ALU.mult), ["wc"], ["CM1"])
                    em.barrier()
                dv = sb(P + "dv", [128, 12])
                em.dma("sync", dv[:], ssmv[:, :], [], ["dv"])
                fl = sb(P + "fl", [128, 1])
                em.dma("sync", fl[:], flag[:, :], [], ["fl"])
                uT = [sb(P + f"uT{i}", [128, 4, 128], BF16) for i in range(3)]
                psA = ps(P + "psA", [128, 512])
                psB = ps(P + "psB", [128, 512])
                NG = 3
                psG = [ps(P + f"psG{i}", [128, 512]) for i in range(NG)]
                psY = [ps(P + f"psY{i}", [128, 512]) for i in range(2)]
                T1 = [sb(P + f"T1{i}", [128, 512], BF16) for i in range(2)]
                T2 = [sb(P + f"T2{i}", [128, 512], BF16) for i in range(2)]
                g1 = [sb(P + f"g1{i}", [128, 512], BF16) for i in range(NG)]
                gl = sb(P + "gl", [128, 8, 4])
                t1 = [sb(P + f"t1{i}", [128, 512], BF16) for i in range(NG)]
                t2 = [sb(P + f"t2{i}", [128, 512], BF16) for i in range(NG)]
                hc = [sb(P + f"hc{i}", [128, 8, 4]) for i in range(2)]
                cq = sb(P + "cq", [128, 8, 8])
                yo = [sb(P + f"yo{i}", [128, 4, 128]) for i in range(2)]
                yfl = [sb(P + f"yfl{i}", [128, 4, 128]) for i in range(2)]
                em.op("gpsimd", lambda e: e.memset(hc[0][:], 0.0), [], ["hc0"])
                Er3 = Er[:].rearrange("p (a c) -> p a c", a=8)
                Ei3 = Ei[:].rearrange("p (a c) -> p a c", a=8)
                Fr4 = Fr[:].rearrange("p (a h j) -> p a h j", a=8, h=2)
                Fi4 = Fi[:].rearrange("p (a h j) -> p a h j", a=8, h=2)
                Frb_ = sb(P + "Frb", [128, 2048], BF16)
                Fib_ = sb(P + "Fib", [128, 2048], BF16)
                nFib_ = sb(P + "nFib", [128, 2048], BF16)
                em.op("vector", lambda e: e.tensor_copy(out=Frb_[:], in_=Fr[:]), ["Fr"], ["Frb"])
                em.op("vector", lambda e: e.tensor_copy(out=Fib_[:], in_=Fi[:]), ["Fi"], ["Fib"])
                em.op("vector", lambda e: e.tensor_scalar(out=nFib_[:], in0=Fi[:], scalar1=-1.0, scalar2=None, op0=ALU.mult), ["Fi"], ["nFib"])
                Frb4 = Frb_[:].rearrange("p (a h j) -> p a h j", a=8, h=2)
                Fib4 = Fib_[:].rearrange("p (a h j) -> p a h j", a=8, h=2)
                nFib4 = nFib_[:].rearrange("p (a h j) -> p a h j", a=8, h=2)
                LCOL = 127 if d == 0 else 0
                Fl = sb(P + "Fl", [128, 8, 6])
                em.op("vector", lambda e: e.tensor_copy(out=Fl[:, :, 0:2], in_=Fr4[:, :, :, LCOL]), ["Fr"], ["Fl0"])
                em.op("vector", lambda e: e.tensor_copy(out=Fl[:, :, 2:4], in_=Fi4[:, :, :, LCOL]), ["Fi"], ["Fl1"])
                em.op("vector", lambda e: e.tensor_scalar(out=Fl[:, :, 4:6], in0=Fi4[:, :, :, LCOL], scalar1=-1.0, scalar2=None, op0=ALU.mult), ["Fi"], ["Fl2"])
                em.op("vector", lambda e: e.tensor_copy(out=Fl[:, 0, 0:1], in_=Fl[:, 0, 0:1]), ["Fl0", "Fl1", "Fl2"], ["Fl"])
                order = list(range(NT)) if d == 0 else list(range(NT - 1, -1, -1))
                u_r = u_s.rearrange("(k p) t -> p k t", p=128)
                yf_r = yf_s.rearrange("(k p) t -> p k t", p=128)
                ys_r = ys_s.rearrange("(k p) t -> p k t", p=128)
                NP = NT * 8
                r4 = lambda t_: t_[:].rearrange("p (r h j) -> p r h j", r=2, h=2)
                h3 = lambda a_: a_.rearrange("p (h j) -> p h j", h=2)

                def stage_ab(gi):
                    ci, p = gi // 8, gi % 8
                    n = order[ci]
                    ub, b2 = ci % 3, gi % 2
                    kq, pq = p // 2, p % 2
                    co = kq * 1024 + pq * 512
                    if p == 0:
                        em.dma("sync", uT[ub][:], u_r[:, :, n * 128:(n + 1) * 128], ["u_s"], [f"uT{ub}"])
                    em.op("tensor", lambda e: e.matmul(psA[:], lhsT=uT[ub][:, kq, :], rhs=BA[:, co:co + 512], start=True, stop=True),
                          [f"uT{ub}", "BA"], ["psA"])
                    em.op("vector", lambda e: e.tensor_tensor(
                        out=T1[b2][:].rearrange("p (h c) -> p h c", h=2), in0=psA[:].rearrange("p (h c) -> p h c", h=2),
                        in1=Er3[:, p, :].unsqueeze(1).to_broadcast([128, 2, 256]), op=ALU.mult), ["psA", "Er"], [f"T1{b2}"])
                    em.op("vector", lambda e: e.tensor_tensor(
                        out=T2[b2][:].rearrange("p (h c) -> p h c", h=2), in0=psA[:].rearrange("p (h c) -> p h c", h=2),
                        in1=Ei3[:, p, :].unsqueeze(1).to_broadcast([128, 2, 256]), op=ALU.mult), ["psA", "Ei"], [f"T2{b2}"])

                def stage_cum(gi):
                    b2, g3 = gi % 2, gi % NG
                    for tl in range(4):
                        stl = (tl + 2) % 4
                        tm, tk = (ntrib, "ntrib") if tl < 2 else (trib, "trib")
                        em.op("tensor", lambda e, tl=tl: e.matmul(
                            psG[g3][:, tl * 128:(tl + 1) * 128], lhsT=T1[b2][:, tl * 128:(tl + 1) * 128], rhs=trib[:],
                            start=True, stop=False), [f"T1{b2}", "trib"], [f"psG{g3}"])
                        em.op("tensor", lambda e, tl=tl, stl=stl, tm=tm: e.matmul(
                            psG[g3][:, tl * 128:(tl + 1) * 128], lhsT=T2[b2][:, stl * 128:(stl + 1) * 128], rhs=tm[:],
                            start=False, stop=True), [f"T2{b2}", tk], [f"psG{g3}"])

                def stage_s2(gi):
                    ci, p = gi // 8, gi % 8
                    n = order[ci]
                    g3 = gi % NG
                    cb_, nb_ = ci % 2, (ci + 1) % 2
                    if p == 0 and ci != 0:
                        first_of_seg = ((n * 128) % SEG == 0) if d == 0 else (((n + 1) * 128) % SEG == 0)
                        if first_of_seg:
                            em.op("gpsimd", lambda e: e.tensor_scalar(out=hc[cb_][:], in0=hc[cb_][:], scalar1=fl[:, 0:1], scalar2=None, op0=ALU.mult),
                                  [f"hc{cb_}", "fl"], [f"hc{cb_}"])
                    for tl in range(4):
                        em.op("scalar", lambda e, tl=tl: e.activation(
                            out=g1[g3][:, tl * 128:(tl + 1) * 128], in_=psG[g3][:, tl * 128:(tl + 1) * 128], func=AF.Identity,
                            bias=hc[cb_][:, p, tl:tl + 1]), [f"psG{g3}", f"hc{cb_}"], [f"g1{g3}"])
                    for tl in range(4):
                        cc_ = tl * 128 + LCOL
                        em.op("scalar", lambda e, tl=tl, cc_=cc_: e.activation(
                            out=gl[:, p, tl:tl + 1], in_=psG[g3][:, cc_:cc_ + 1], func=AF.Identity,
                            bias=hc[cb_][:, p, tl:tl + 1]), [f"psG{g3}", f"hc{cb_}"], [f"gl{p}"])
                    em.op("vector", lambda e: e.tensor_tensor(out=h3(t1[g3][:, 0:256]), in0=h3(g1[g3][:, 0:256]), in1=Frb4[:, p, :, :], op=ALU.mult),
                          [f"g1{g3}", "Frb"], [f"t1{g3}a"])
                    em.op("vector", lambda e: e.tensor_tensor(out=h3(t1[g3][:, 256:512]), in0=h3(g1[g3][:, 256:512]), in1=Frb4[:, p, :, :], op=ALU.mult),
                          [f"g1{g3}", "Frb"], [f"t1{g3}b"])
                    em.op("vector", lambda e: e.tensor_tensor(out=h3(t2[g3][:, 0:256]), in0=h3(g1[g3][:, 256:512]), in1=nFib4[:, p, :, :], op=ALU.mult),
                          [f"g1{g3}", "nFib"], [f"t2{g3}a"])
                    em.op("vector", lambda e: e.tensor_tensor(out=h3(t2[g3][:, 256:512]), in0=h3(g1[g3][:, 0:256]), in1=Fib4[:, p, :, :], op=ALU.mult),
                          [f"g1{g3}", "Fib"], [f"t2{g3}b"])
                    em.op("gpsimd", lambda e: e.tensor_tensor(out=cq[:, p, 0:2], in0=gl[:, p, 0:2], in1=Fl[:, p, 0:2], op=ALU.mult), [f"gl{p}", "Fl"], [f"cq{p}a"])
                    em.op("gpsimd", lambda e: e.tensor_tensor(out=cq[:, p, 2:4], in0=gl[:, p, 2:4], in1=Fl[:, p, 0:2], op=ALU.mult), [f"gl{p}", "Fl"], [f"cq{p}a2"])
                    em.op("gpsimd", lambda e: e.tensor_tensor(out=cq[:, p, 4:6], in0=gl[:, p, 2:4], in1=Fl[:, p, 4:6], op=ALU.mult), [f"gl{p}", "Fl"], [f"cq{p}b"])
                    em.op("gpsimd", lambda e: e.tensor_tensor(out=cq[:, p, 6:8], in0=gl[:, p, 0:2], in1=Fl[:, p, 2:4], op=ALU.mult), [f"gl{p}", "Fl"], [f"cq{p}c"])
                    em.op("gpsimd", lambda e: e.tensor_tensor(out=hc[nb_][:, p, :], in0=cq[:, p, 0:4], in1=cq[:, p, 4:8], op=ALU.add),
                          [f"cq{p}a", f"cq{p}a2", f"cq{p}b", f"cq{p}c"], [f"hc{nb_}"])

                def stage_cmm(gi):
                    ci, p = gi // 8, gi % 8
                    n = order[ci]
                    g3 = gi % NG
                    kq, pq = p // 2, p % 2
                    yb, ub = ci % 2, ci % 3
                    for tl in range(4):
                        for (src, sk) in ((t1, "t1"), (t2, "t2")):
                            first = (pq == 0 and tl == 0 and sk == "t1")
                            last = (pq == 1 and tl == 3 and sk == "t2")
                            cc = p * 512 + tl * 128
                            rk = [f"t1{g3}a", f"t1{g3}b"] if sk == "t1" else [f"t2{g3}a", f"t2{g3}b"]
                            em.op("tensor", lambda e, tl=tl, src=src, first=first, last=last, cc=cc: e.matmul(
                                psY[yb][:, kq * 128:(kq + 1) * 128], lhsT=CM[:, cc:cc + 128], rhs=src[g3][:, tl * 128:(tl + 1) * 128],
                                start=first, stop=last), rk + ["CM0", "CM1"], [f"psY{yb}"])
                    if p != 7:
                        return
                    if d == 0:
                        em.op("scalar", lambda e: e.activation(out=yo[yb][:].rearrange("p k j -> p (k j)"), in_=psY[yb][:], func=AF.Copy),
                              [f"psY{yb}"], [f"yo{yb}"])
                        em.dma("gpsimd", yf_r[:, :, n * 128:(n + 1) * 128], yo[yb][:], [f"yo{yb}"], cwrites=["yf_s"])
                    else:
                        em.dma("sync", yfl[yb][:], yf_r[:, :, n * 128:(n + 1) * 128], ["yf_s"], [f"yfl{yb}"])
                        em.op("vector", lambda e: e.tensor_tensor(out=yo[yb][:].rearrange("p k j -> p (k j)"), in0=psY[yb][:],
                                                                  in1=yfl[yb][:].rearrange("p k j -> p (k j)"), op=ALU.add),
                              [f"psY{yb}", f"yfl{yb}"], [f"yo{yb}"])
                        em.op("gpsimd", lambda e: e.tensor_tensor(out=yfl[yb][:], in0=uT[ub][:],
                                                                  in1=dv[:, 0:4].unsqueeze(2).to_broadcast([128, 4, 128]), op=ALU.mult),
                              [f"uT{ub}", "dv", f"yo{yb}"], [f"yfl{yb}"])
                        em.op("gpsimd", lambda e: e.tensor_tensor(out=yo[yb][:], in0=yo[yb][:], in1=yfl[yb][:], op=ALU.add),
                              [f"yo{yb}", f"yfl{yb}"], [f"yo{yb}"])
                        em.dma("gpsimd", ys_r[:, :, n * 128:(n + 1) * 128], yo[yb][:], [f"yo{yb}"], cwrites=["ys_s"])

                for i in range(NP + 2):
                    if i < NP:
                        stage_ab(i)
                    if 0 <= i - 1 < NP:
                        stage_cum(i - 1)
                        stage_s2(i - 1)
                    if 0 <= i - 2 < NP:
                        stage_cmm(i - 2)
                em.barrier()

    def phD():
        with ExitStack() as st:
            sb = lambda n, s, dt=F32: st.enter_context(nc.sbuf_tensor(n, list(s), dt))
            ps = lambda n, s, dt=F32: st.enter_context(nc.psum_tensor(n, list(s), dt))
            wg = sb("d_wg", [128, 4, 512], BF16)
            em.dma("gpsimd", wg[:], w_glu.rearrange("(k p) n -> p k n", p=128), [], ["wg"])
            dv = sb("d_dv", [128, 12])
            em.dma("sync", dv[:], ssmv[:, :], [], ["dv"])
            onesf = sb("d_onesf", [128, 128])
            em.op("vector", lambda e: e.memset(onesf[:], 1.0), [], ["onesf"])
            ysb = [sb(f"d_ys{i}", [128, 4, 512]) for i in range(2)]
            y1 = [sb(f"d_y1{i}", [128, 4, 512]) for i in range(2)]
            y1b = [sb(f"d_y1b{i}", [128, 4, 512], BF16) for i in range(2)]
            sg = [sb(f"d_sg{i}", [128, 512]) for i in range(2)]
            y2 = [sb(f"d_y2{i}", [128, 4, 512]) for i in range(2)]
            sqd = [sb(f"d_sq{i}", [128, 512]) for i in range(2)]
            rsd = sb("d_rsd", [128, 512])
            rsd2 = sb("d_rsd2", [128, 512])
            so = [sb(f"d_so{i}", [128, 4, 512], BF16) for i in range(2)]
            pz = [ps(f"d_pz{i}", [128, 512]) for i in range(4)]
            pq_ = ps("d_pq", [128, 512])
            ys_r = ys_s.rearrange("(k p) t -> p k t", p=128)
            s_r = s_s.rearrange("(k p) t -> p k t", p=128)
            def d_stage1(blk):
                b = blk % 2
                em.dma("sync", ysb[b][:], ys_r[:, :, blk * 512:(blk + 1) * 512], ["ys_s"], [f"ys{b}"])
                em.op("scalar", lambda e: e.activation(out=y1[b][:], in_=ysb[b][:], func=AF.Gelu_apprx_tanh), [f"ys{b}"], [f"y1{b}"])
                em.op("vector", lambda e: e.tensor_copy(out=y1b[b][:], in_=y1[b][:]), [f"y1{b}"], [f"y1b{b}"])
                for mo in range(4):
                    for k in range(4):
                        em.op("tensor", lambda e, k=k, mo=mo: e.matmul(pz[mo][:], lhsT=wg[:, k, mo * 128:(mo + 1) * 128], rhs=y1b[b][:, k, :],
                                                                       start=(k == 0), stop=(k == 3)), ["wg", f"y1b{b}"], [f"pz{mo}"])
                for mo in range(4):
                    zb = mo % 2
                    em.op("scalar", lambda e, mo=mo, zb=zb: e.activation(out=sg[zb][:], in_=pz[mo][:], func=AF.Sigmoid, bias=dv[:, 4 + mo:5 + mo]),
                          [f"pz{mo}", "dv"], [f"sg{zb}"])
                    em.op("vector", lambda e, mo=mo, zb=zb: e.tensor_tensor(out=y2[b][:, mo, :], in0=y1[b][:, mo, :], in1=sg[zb][:], op=ALU.mult),
                          [f"y1{b}", f"sg{zb}"], [f"y2{b}_{mo}"])
                    em.op("gpsimd", lambda e, mo=mo, zb=zb: e.tensor_tensor(out=sqd[zb][:], in0=y2[b][:, mo, :], in1=y2[b][:, mo, :], op=ALU.mult),
                          [f"y2{b}_{mo}"], [f"sqd{zb}"])
                    em.op("tensor", lambda e, mo=mo, zb=zb: e.matmul(pq2[b][:], lhsT=onesf[:], rhs=sqd[zb][:], start=(mo == 0), stop=(mo == 3)),
                          ["onesf", f"sqd{zb}"], [f"pq{b}"])

            def d_stage2(blk):
                b = blk % 2
                em.op("vector", lambda e: e.tensor_scalar(out=rsd[:], in0=pq2[b][:], scalar1=1.0 / 512, scalar2=1e-6, op0=ALU.mult, op1=ALU.add), [f"pq{b}"], ["rsd"])
                em.op("scalar", lambda e: e.activation(out=rsd2[:], in_=rsd[:], func=AF.Sqrt), ["rsd"], ["rsd2"])
                em.op("vector", lambda e: e.reciprocal(out=rsd[:], in_=rsd2[:]), ["rsd2"], ["rsd"])
                for mo in range(4):
                    em.op("vector", lambda e, mo=mo: e.scalar_tensor_tensor(out=so[b][:, mo, :], in0=y2[b][:, mo, :], scalar=dv[:, 8 + mo:9 + mo],
                                                                           in1=rsd[:], op0=ALU.mult, op1=ALU.mult),
                          [f"y2{b}_{mo}", "rsd", "dv"], [f"so{b}"])
                em.dma("gpsimd", s_r[:, :, blk * 512:(blk + 1) * 512], so[b][:], [f"so{b}"], cwrites=["s_s"])

            pq2 = [pq_, ps("d_pq1", [128, 512])]
            d_stage1(0)
            for blk in range(NB):
                if blk + 1 < NB:
                    d_stage1(blk + 1)
                d_stage2(blk)
            em.barrier()

    def phE():
        with ExitStack() as st:
            sb = lambda n, s, dt=F32: st.enter_context(nc.sbuf_tensor(n, list(s), dt))
            ps = lambda n, s, dt=F32: st.enter_context(nc.psum_tensor(n, list(s), dt))
            identf = sb("e_identf", [128, 128])
            identb = sb("e_identb", [128, 128], BF16)
            em.dma("sync", identf[:], consts[:, ID0:ID0 + 128], [], ["identf"])
            em.op("vector", lambda e: e.tensor_copy(out=identb[:], in_=identf[:]), ["identf"], ["identb"])
            wo = sb("e_wo", [128, 8, D], BF16)
            em.dma("gpsimd", wo[:], w_out.rearrange("(k p) n -> p k n", p=128), [], ["wo"])
            fl = sb("e_fl", [128, 1])
            em.dma("sync", fl[:], flag[:, :], [], ["fl"])
            zt = sb("e_zt", [128, 8, 1], BF16)
            em.op("vector", lambda e: e.memset(zt[:], 0.0), [], ["zt"])
            h_r = h_s.rearrange("k p t -> p k t")
            em.dma("gpsimd", h_r[:, :, 0:1], zt[:], ["zt"], cwrites=["h_s"], allow_slow_non_contiguous=True)
            em.dma("gpsimd", h_r[:, :, 2 * SEGP - 1:2 * SEGP], zt[:], ["zt"], cwrites=["h_s"], allow_slow_non_contiguous=True)
            cat = [sb(f"e_cat{i}", [128, 8, 512], BF16) for i in range(2)]
            xt = [sb(f"e_xt{i}", [128, D]) for i in range(3)]
            x1 = [sb(f"e_x1{i}", [128, D]) for i in range(3)]
            junk = sb("e_junk", [128, D], BF16)
            s4 = [sb(f"e_s4{i}", [128, 4]) for i in range(2)]
            hn = [sb(f"e_hn{i}", [128, D], BF16) for i in range(2)]
            hT = [sb(f"e_hT{i}", [128, 8, 512], BF16) for i in range(2)]
            halo = [sb(f"e_halo{i}", [128, 8, 1], BF16) for i in range(2)]
            po = [ps(f"e_po{i}", [128, 512]) for i in range(4)]
            tp = [ps(f"e_tp{i}", [128, D], BF16) for i in range(2)]
            a_r = a_s.rearrange("(k p) t -> p k t", p=128)
            s_r = s_s.rearrange("(k p) t -> p k t", p=128)
            pcc = [0]

            def e_stage1(t):
                blk, i = t // 4, t % 4
                cbf = blk % 2
                a, n2 = t % 3, t % 2
                if i == 0:
                    em.dma("sync", cat[cbf][:, 0:4, :], a_r[:, :, blk * 512:(blk + 1) * 512], ["a_s"], [f"cat{cbf}a"])
                    em.dma("sync", cat[cbf][:, 4:8, :], s_r[:, :, blk * 512:(blk + 1) * 512], ["s_s"], [f"cat{cbf}s"])
                em.dma("sync", xt[a][:], x[t * 128:(t + 1) * 128, :], [], [f"xt{a}"])
                for half in range(2):
                    pb = pcc[0] % 4
                    pcc[0] += 1
                    for k in range(8):
                        em.op("tensor", lambda e, k=k, half=half, pb=pb: e.matmul(
                            po[pb][:], lhsT=cat[cbf][:, k, i * 128:(i + 1) * 128], rhs=wo[:, k, half * 512:(half + 1) * 512],
                            start=(k == 0), stop=(k == 7)), [f"cat{cbf}a", f"cat{cbf}s", "wo"], [f"po{pb}"])
                    em.op("vector", lambda e, half=half, pb=pb: e.tensor_tensor(
                        out=x1[a][:, half * 512:(half + 1) * 512], in0=po[pb][:], in1=xt[a][:, half * 512:(half + 1) * 512], op=ALU.add),
                          [f"po{pb}", f"xt{a}"], [f"x1{a}_{half}"])
                em.dma("gpsimd", x1_s[t * 128:(t + 1) * 128, :], x1[a][:], [f"x1{a}_0", f"x1{a}_1"], cwrites=["x1_s"])
                em.op("scalar", lambda e: e.activation(out=junk[:], in_=x1[a][:], func=AF.Square, accum_out=s4[n2][:, 0:1]),
                      [f"x1{a}_0", f"x1{a}_1"], ["junk", f"s4{n2}_0"])
                rstd_chain(s4[n2], f"s4{n2}", 1.0 / D, 1e-6)
                em.op("scalar", lambda e: e.activation(out=hn[n2][:], in_=x1[a][:], func=AF.Copy, scale=s4[n2][:, 3:4]),
                      [f"x1{a}_0", f"x1{a}_1", f"s4{n2}_3"], [f"hn{n2}"])

            def e_stage2(t):
                blk, i = t // 4, t % 4
                cbf = blk % 2
                n2 = t % 2
                for k in range(8):
                    em.op("tensor", lambda e, k=k: e.transpose(out=tp[n2][:, k * 128:(k + 1) * 128], in_=hn[n2][:, k * 128:(k + 1) * 128],
                                                               identity=identb[:]), [f"hn{n2}", "identb"], [f"tp{n2}"])
                em.op("vector", lambda e: e.tensor_copy(out=hT[cbf][:, :, i * 128:(i + 1) * 128],
                                                        in_=tp[n2][:].rearrange("p (k j) -> p k j", k=8)),
                      [f"tp{n2}"], [f"hT{cbf}"])
                if i != 3:
                    return
                seg = (blk * 512) // SEG
                off = seg * SEGP + 1 + (blk * 512 - seg * SEG)
                em.dma("gpsimd", h_r[:, :, off:off + 512], hT[cbf][:], [f"hT{cbf}"], cwrites=["h_s"])
                if (blk + 1) * 512 == SEG:
                    em.op("vector", lambda e: e.tensor_scalar(out=halo[0][:], in0=hT[cbf][:, :, 511:512], scalar1=fl[:, 0:1], scalar2=None,
                                                              op0=ALU.mult), [f"hT{cbf}", "fl"], ["halo0"])
                    em.dma("gpsimd", h_r[:, :, SEGP:SEGP + 1], halo[0][:], ["halo0"], cwrites=["h_s"], allow_slow_non_contiguous=True)
                if blk * 512 == SEG:
                    em.op("vector", lambda e: e.tensor_scalar(out=halo[1][:], in0=hT[cbf][:, :, 0:1], scalar1=fl[:, 0:1], scalar2=None,
                                                              op0=ALU.mult), [f"hT{cbf}", "fl"], ["halo1"])
                    em.dma("gpsimd", h_r[:, :, SEGP - 1:SEGP], halo[1][:], ["halo1"], cwrites=["h_s"], allow_slow_non_contiguous=True)

            e_stage1(0)
            for t in range(NT):
                if t + 1 < NT:
                    e_stage1(t + 1)
                e_stage2(t)
            em.barrier()

    def phF():
        with ExitStack() as st:
            sb = lambda n, s, dt=F32: st.enter_context(nc.sbuf_tensor(n, list(s), dt))
            ps = lambda n, s, dt=F32: st.enter_context(nc.psum_tensor(n, list(s), dt))
            gf = sb("f_gf", [128, 8])
            em.dma("sync", gf[:], gffn_pk[:, :], [], ["gf"])
            with ExitStack() as st2:
                sb2 = lambda n, s, dt=F32: st2.enter_context(nc.sbuf_tensor(n, list(s), dt))
                wst = [sb2(f"f_wst{i}", [128, 2 * DFF]) for i in range(2)]
                wsb = [sb2(f"f_wsb{i}", [128, 2 * DFF], BF16) for i in range(2)]
                wus_r = wus.rearrange("m p k c -> p k m c")
                for k in range(8):
                    b = k % 2
                    em.dma("sync", wst[b][:], w_up[k * 128:(k + 1) * 128, :], [], [f"wst{b}"])
                    if b == 0:
                        em.op("vector", lambda e, k=k, b=b: e.tensor_scalar(out=wsb[b][:], in0=wst[b][:], scalar1=gf[:, k:k + 1], scalar2=None, op0=ALU.mult),
                              [f"wst{b}", "gf"], [f"wsb{b}"])
                    else:
                        em.op("scalar", lambda e, k=k, b=b: e.activation(out=wsb[b][:], in_=wst[b][:], func=AF.Copy, scale=gf[:, k:k + 1]),
                              [f"wst{b}", "gf"], [f"wsb{b}"])
                    em.dma("gpsimd", wus_r[:, k, :, :], wsb[b][:].rearrange("p (m c) -> p m c", c=128), [f"wsb{b}"], cwrites=["wus"])
                em.barrier()
            wd = sb("f_wd", [128, 22, D], BF16)
            em.dma("gpsimd", wd[:], w_down.rearrange("(m p) n -> p m n", p=128), [], ["wd"])
            cw = sb("f_cw", [128, 44, 3])
            cbv = sb("f_cb", [128, 44])
            em.dma("sync", cw[:], cw_pk[:, :, :], [], ["cw"])
            em.dma("sync", cbv[:], cb_pk[:, :], [], ["cbv"])
            gfin_t = sb("f_gfin", [128, D])
            em.dma("sync", gfin_t[:], gfin[0:1, :].to_broadcast([128, D]), [], ["gfin"])
            hw = [sb(f"f_hw{i}", [128, 8, 512], BF16) for i in range(3)]
            wt_ = [sb(f"f_wt{i}", [128, 8, 128], BF16) for i in range(4)]
            zc = [sb(f"f_zc{i}", [128, 512]) for i in range(4)]
            ga = [sb(f"f_ga{i}", [128, 512]) for i in range(2)]
            act = [sb(f"f_act{i}", [128, 22, 512], BF16) for i in range(2)]
            x1t = [sb(f"f_x1{i}", [128, D]) for i in range(8)]
            x2t = [sb(f"f_x2{i}", [128, D]) for i in range(2)]
            yt = [sb(f"f_yt{i}", [128, D]) for i in range(2)]
            junk = sb("f_junk", [128, D], BF16)
            s4 = [sb(f"f_s4{i}", [128, 4]) for i in range(2)]
            pu = [ps(f"f_pu{i}", [128, 512]) for i in range(3)]
            pd = [ps(f"f_pd{i}", [128, 512]) for i in range(4)]
            h_r = h_s.rearrange("k p t -> p k t")
            cnts = {"uc": 0, "dc": 0, "tc": 0}
            wins = []
            for seg in range(2):
                b = 0
                while 510 * b < SEG:
                    W_ = min(512, SEGP - 510 * b)
                    wins.append(dict(seg=seg, b=b, W=W_, nv=W_ - 2, c0=seg * SEGP + 510 * b, tok0=seg * SEG + 510 * b, idx=len(wins)))
                    b += 1

            def f_load(w):
                hb = w["idx"] % 3
                em.dma("sync", hw[hb][:, :, 0:w["W"]], h_r[:, :, w["c0"]:w["c0"] + w["W"]], ["h_s"], [f"hw{hb}"])

            def f_loadx(w):
                w["xb"] = []
                ntt = (w["nv"] + 127) // 128
                for tt in range(ntt):
                    r0 = tt * 128
                    nr = min(128, w["nv"] - r0)
                    xb_ = cnts["tc"] % 8
                    cnts["tc"] += 1
                    w["xb"].append(xb_)
                    em.dma("sync", x1t[xb_][0:nr, :], x1_s[w["tok0"] + r0:w["tok0"] + r0 + nr, :], ["x1_s"], [f"x1t{xb_}"])

            def f_up(w, mm0, mm1):
                hb, ab = w["idx"] % 3, w["idx"] % 2
                W_, nv = w["W"], w["nv"]
                for mm in range(mm0, mm1):
                    m = (mm // 2) + (22 if mm % 2 else 0)
                    uc = cnts["uc"]
                    cnts["uc"] += 1
                    w4, p3, z4 = uc % 4, uc % 3, uc % 4
                    em.dma("sync", wt_[w4][:], wus[m, :, :, :], ["wus"], [f"wt{w4}"])
                    for k in range(8):
                        em.op("tensor", lambda e, k=k, w4=w4, p3=p3: e.matmul(
                            pu[p3][:, 0:W_], lhsT=wt_[w4][:, k, :], rhs=hw[hb][:, k, 0:W_], start=(k == 0), stop=(k == 7)),
                              [f"wt{w4}", f"hw{hb}"], [f"pu{p3}"])
                    em.op("scalar", lambda e, m=m, p3=p3, z4=z4: e.activation(
                        out=zc[z4][:, 0:nv], in_=pu[p3][:, 1:1 + nv], func=AF.Identity, scale=cw[:, m, 1:2], bias=cbv[:, m:m + 1]),
                          [f"pu{p3}", "cw", "cbv"], [f"zc{z4}"])
                    em.op("vector", lambda e, m=m, p3=p3, z4=z4: e.scalar_tensor_tensor(
                        out=zc[z4][:, 0:nv], in0=pu[p3][:, 0:nv], scalar=cw[:, m, 0:1], in1=zc[z4][:, 0:nv], op0=ALU.mult, op1=ALU.add),
                          [f"pu{p3}", "cw", f"zc{z4}"], [f"zc{z4}"])
                    em.op("vector", lambda e, m=m, p3=p3, z4=z4: e.scalar_tensor_tensor(
                        out=zc[z4][:, 0:nv], in0=pu[p3][:, 2:2 + nv], scalar=cw[:, m, 2:3], in1=zc[z4][:, 0:nv], op0=ALU.mult, op1=ALU.add),
                          [f"pu{p3}", "cw", f"zc{z4}"], [f"zc{z4}"])
                    g2_ = (mm // 2) % 2
                    if mm % 2 == 0:
                        em.op("scalar", lambda e, z4=z4, g2_=g2_: e.activation(out=ga[g2_][:, 0:nv], in_=zc[z4][:, 0:nv], func=AF.Gelu_apprx_tanh),
                              [f"zc{z4}"], [f"ga{g2_}"])
                    else:
                        mg = mm // 2
                        em.op("gpsimd", lambda e, z4=z4, g2_=g2_, mg=mg: e.tensor_tensor(
                            out=act[ab][:, mg, 0:nv], in0=ga[g2_][:, 0:nv], in1=zc[z4][:, 0:nv], op=ALU.mult),
                              [f"ga{g2_}", f"zc{z4}"], [f"act{ab}"])

            def f_down(w):
                ab = w["idx"] % 2
                nv, tok0 = w["nv"], w["tok0"]
                ntt = (nv + 127) // 128
                for tt in range(ntt):
                    r0 = tt * 128
                    nr = min(128, nv - r0)
                    xb_ = w["xb"][tt]
                    ob = xb_ % 2
                    for half in range(2):
                        p4 = cnts["dc"] % 4
                        cnts["dc"] += 1
                        for m in range(22):
                            em.op("tensor", lambda e, m=m, half=half, p4=p4, r0=r0, nr=nr: e.matmul(
                                pd[p4][0:nr, :], lhsT=act[ab][:, m, r0:r0 + nr], rhs=wd[:, m, half * 512:(half + 1) * 512],
                                start=(m == 0), stop=(m == 21)), [f"act{ab}", "wd"], [f"pd{p4}"])
                        em.op("vector", lambda e, half=half, p4=p4, nr=nr, xb_=xb_, ob=ob: e.tensor_tensor(
                            out=x2t[ob][0:nr, half * 512:(half + 1) * 512], in0=pd[p4][0:nr, :], in1=x1t[xb_][0:nr, half * 512:(half + 1) * 512], op=ALU.add),
                              [f"pd{p4}", f"x1t{xb_}"], [f"x2t{ob}_{half}"])
                    em.op("scalar", lambda e, nr=nr, ob=ob: e.activation(out=junk[0:nr, :], in_=x2t[ob][0:nr, :], func=AF.Square, accum_out=s4[ob][0:nr, 0:1]),
                          [f"x2t{ob}_0", f"x2t{ob}_1"], ["junk", f"s4{ob}_0"])
                    rstd_chain(s4[ob], f"s4{ob}", 1.0 / D, 1e-6)
                    em.op("vector", lambda e, nr=nr, ob=ob: e.scalar_tensor_tensor(
                        out=yt[ob][0:nr, :], in0=x2t[ob][0:nr, :], scalar=s4[ob][0:nr, 3:4], in1=gfin_t[0:nr, :], op0=ALU.mult, op1=ALU.mult),
                          [f"x2t{ob}_0", f"x2t{ob}_1", f"s4{ob}_3", "gfin"], [f"yt{ob}"])
                    em.dma("gpsimd", y[tok0 + r0:tok0 + r0 + nr, :], yt[ob][0:nr, :], [f"yt{ob}"], cwrites=["y"])

            NPRE = 8
            f_load(wins[0])
            if len(wins) > 1:
                f_load(wins[1])
            f_loadx(wins[0])
            f_up(wins[0], 0, 44)
            for wi_, w in enumerate(wins):
                nxt = wins[wi_ + 1] if wi_ + 1 < len(wins) else None
                nn = wins[wi_ + 2] if wi_ + 2 < len(wins) else None
                if nxt is not None:
                    f_up(nxt, 0, NPRE)
                if nn is not None:
                    f_load(nn)
                if nxt is not None:
                    f_loadx(nxt)
                f_down(w)
                if nxt is not None:
                    f_up(nxt, NPRE, 44)
            em.barrier()

    if "A" in phases:
        phA()
    if "B" in phases:
        phB()
    if "C" in phases:
        phC(0)
        phC(1)
    if "D" in phases:
        phD()
    if "E" in phases:
        phE()
    if "F" in phases:
        phF()
    em.finish()
    top.close()
    return nc


def make_consts():
    c = np.zeros((128, NCON), np.float32)
    p = np.arange(128, dtype=np.float32)[:, None]
    j = np.arange(128, dtype=np.float32)[None, :]
    c[:, ID0:ID0 + 128] = np.eye(128, dtype=np.float32)
    c[:, TL0:TL0 + 128] = (p <= j).astype(np.float32)
    c[:, TU0:TU0 + 128] = (p >= j).astype(np.float32)
    qf = np.arange(512, dtype=np.float32)[None, :]
    c[:, RP0:RP0 + 512] = qf - p
    for m in range(4):
        c[:, AR0 + 512 * m:AR0 + 512 * (m + 1)] = np.abs(qf - p - 128.0 * m)
    c[:, JF0:JF0 + 128] = j + 1.0
    c[:, JB0:JB0 + 128] = 128.0 - j
    c[:, QF0:QF0 + 512] = qf
    c[:, QR0:QR0 + 512] = 511.0 - qf
    c[:, JC0] = -(p[:, 0] + 1.0)
    c[:, JC0 + 1] = -(128.0 - p[:, 0])
    return c


def make_cbias(T, cross_val):
    NT, NB, SEG = T // 128, T // 512, T // 2
    cb = np.zeros((4, NB, NT), np.float32)
    for h in range(4):
        for Qb in range(NB):
            for kt in range(NT):
                rel = kt - 4 * Qb
                if 0 <= rel < 4:
                    v = 0.0
                elif rel < 0:
                    v = -SLOPES[h] * (512 * Qb - 128 * kt)
                else:
                    v = -SLOPES[h] * (128 * kt - 512 * Qb)
                if (512 * Qb) // SEG != (128 * kt) // SEG:
                    v += cross_val
                cb[h, Qb, kt] = v
    return np.ascontiguousarray(np.broadcast_to(cb.reshape(1, -1), (128, 4 * NB * NT))).astype(np.float32)


def host_weights(inp):
    f = lambda a: np.ascontiguousarray(np.asarray(a, dtype=np.float32))
    w = {}
    w["w_in"] = f(inp["w_in"][0])
    w["w_out"] = f(inp["w_out"][0])
    w["w_up"] = f(inp["w_up"][0])
    w["w_down"] = f(inp["w_down"][0])
    w["w_glu"] = f(inp["w_glu"][0])
    w["gmix_pk"] = f(np.asarray(inp["g_mix_norm"][0]).reshape(8, 128).T)
    w["gffn_pk"] = f(np.asarray(inp["g_ffn_norm"][0]).reshape(8, 128).T)
    w["gfin"] = f(np.asarray(inp["g_final"]).reshape(1, D))
    w["gsub"] = f(np.asarray(inp["g_subln"][0]).reshape(128, 1))
    w["lamv"] = f(np.concatenate([np.asarray(inp[k][0]) for k in ("lambda_q1", "lambda_k1", "lambda_q2", "lambda_k2")]).reshape(1, 256))
    w["cw_pk"] = f(np.asarray(inp["conv_w"][0]).reshape(3, 44, 128).transpose(2, 1, 0))
    w["cb_pk"] = f(np.asarray(inp["conv_b"][0]).reshape(44, 128).T)
    sv = np.zeros((128, 12), np.float32)
    sv[:, 0:4] = np.asarray(inp["ssm_d"][0]).reshape(4, 128).T
    sv[:, 4:8] = np.asarray(inp["b_glu"][0]).reshape(4, 128).T
    sv[:, 8:12] = np.asarray(inp["g_ssm_out"][0]).reshape(4, 128).T
    w["ssmv"] = sv
    a_re = np.asarray(inp["ssm_a_re"][0], np.float32)
    a_im = np.asarray(inp["ssm_a_im"][0], np.float32)
    ls = np.asarray(inp["ssm_log_step"][0], np.float32)
    lsx = np.repeat(ls[:, :, None], 64, axis=2)
    a_row = np.stack([a_re.reshape(2, 2048), a_im.reshape(2, 2048), lsx.reshape(2, 2048)], axis=1)
    w["a_row"] = f(a_row)
    a_sm = np.zeros((2, 128, 48), np.float32)
    for d in range(2):
        a_sm[d, :, 0:16] = a_re[d].reshape(16, 128).T
        a_sm[d, :, 16:32] = a_im[d].reshape(16, 128).T
        a_sm[d, :, 32:48] = lsx[d].reshape(16, 128).T
    w["a_sm"] = a_sm
    b_re = np.asarray(inp["ssm_b_re"][0], np.float32)
    b_im = np.asarray(inp["ssm_b_im"][0], np.float32)
    c_re = np.asarray(inp["ssm_c_re"][0], np.float32)
    c_im = np.asarray(inp["ssm_c_im"][0], np.float32)
    bmA = np.zeros((2, 128, 4, 2, 512), np.float32)
    bmB = np.zeros((2, 128, 4, 2, 512), np.float32)
    cm = np.zeros((2, 128, 8, 4, 128), np.float32)
    for d in range(2):
        for g in range(32):
            kq, gg = g // 8, g % 8
            pq, gl = gg // 4, gg % 4
            rows = slice(gg * 16, gg * 16 + 16)
            bmA[d, rows, kq, pq, gl * 64:gl * 64 + 64] = b_re[d, g].T
            bmA[d, rows, kq, pq, 256 + gl * 64:256 + gl * 64 + 64] = b_im[d, g].T
            bmB[d, rows, kq, pq, gl * 64:gl * 64 + 64] = b_im[d, g].T
            bmB[d, rows, kq, pq, 256 + gl * 64:256 + gl * 64 + 64] = b_re[d, g].T
            p = g // 4
            half, glh = gl // 2, gl % 2
            srows = slice(glh * 64, glh * 64 + 64)
            chc = slice((g % 8) * 16, (g % 8) * 16 + 16)
            cm[d, srows, p, half, chc] = c_re[d, g].T
            cm[d, srows, p, 2 + half, chc] = c_im[d, g].T
    w["bmA"] = bmA.reshape(2, 128, 4096)
    w["bmB"] = bmB.reshape(2, 128, 4096)
    w["cmat"] = cm.reshape(2, 128, 4096)
    w["consts"] = make_consts()
    return w


_NC_CACHE = {}


def kernel(**inputs):
    T = 8192
    w = host_weights(inputs)
    xp = np.asarray(inputs["x_prompt"], np.float32)
    xs = np.asarray(inputs["x_sample"], np.float32)
    cb_p = make_cbias(T, 0.0)
    cb_s = make_cbias(T, -30000.0)
    in_maps = []
    for c in range(8):
        m = dict(w)
        if c < 4:
            m["x"] = np.ascontiguousarray(xp[c])
            m["cbias"] = cb_p
            m["flag"] = np.ones((128, 1), np.float32)
        else:
            m["x"] = np.ascontiguousarray(xs[2 * (c - 4):2 * (c - 4) + 2].reshape(T, D))
            m["cbias"] = cb_s
            m["flag"] = np.zeros((128, 1), np.float32)
        in_maps.append(m)
    if T not in _NC_CACHE:
        _NC_CACHE[T] = build(T)
    res = run_bass_kernel_spmd(_NC_CACHE[T], in_maps, core_ids=list(range(8)))
    ys = [np.asarray(r["y"], np.float32) for r in res.results]
    y_prompt = np.stack(ys[0:4], axis=0)
    y_sample = np.concatenate([ys[c].reshape(2, T // 2, D) for c in range(4, 8)], axis=0)
    return (y_prompt, y_sample)
```

```python
import math
from contextlib import ExitStack
import numpy as np
import ml_dtypes
import concourse.bass as bass
import concourse.mybir as mybir
from concourse.bass_utils import run_bass_kernel_spmd

F32 = mybir.dt.float32
BF16 = mybir.dt.bfloat16
I32 = mybir.dt.int32
AF = mybir.ActivationFunctionType
ALU = mybir.AluOpType
D = 1024
DFF = 2816
SLOPES = [2.0 ** (-2 * (h + 1)) for h in range(4)]
ID0, TL0, TU0, RP0, AR0, JF0, JB0, JC0, QF0, QR0, NCON = 0, 128, 256, 384, 896, 2944, 3072, 3200, 3208, 3720, 4232
TWO_PI = 2.0 * math.pi


class Em:
    def __init__(self, nc, stack):
        self.nc = nc
        self.streams = {k: [] for k in ("sync", "scalar", "vector", "gpsimd", "tensor")}
        self.csem = {e: stack.enter_context(nc.semaphore("c_" + e)) for e in ("scalar", "vector", "gpsimd", "tensor")}
        self.ccnt = {e: 0 for e in self.csem}
        self.NQ = 10
        self.dsem, self.dcnt, self.drr = {}, {}, {}
        for q in ("sync", "gpsimd", "scalar"):
            self.dsem[q] = [stack.enter_context(nc.semaphore(f"d_{q}_{i}")) for i in range(self.NQ)]
            self.dcnt[q] = [0] * self.NQ
            self.drr[q] = 0
        self.waited = {e: {} for e in self.streams}
        self.bw, self.br = {}, {}
        self.nins = 0

    def semh(self, key):
        return self.csem[key] if isinstance(key, str) else self.dsem[key[1]][key[2]]

    def _deps(self, eng, reads, writes):
        deps = {}

        def add(k, v):
            if deps.get(k, 0) < v:
                deps[k] = v

        for b in reads:
            for k, v in self.bw.get(b, {}).items():
                add(k, v)
        for b in writes:
            for k, v in self.bw.get(b, {}).items():
                if k != eng:
                    add(k, v)
            for k, v in self.br.get(b, {}).items():
                if k != eng:
                    add(k, v)
        return deps

    def _waits(self, eng, deps):
        waits = []
        for k, v in deps.items():
            if self.waited[eng].get(k, 0) >= v:
                continue
            self.waited[eng][k] = v
            waits.append((self.semh(k), v))
        return waits

    def _record(self, key, val, reads, writes, cwrites):
        for b in writes:
            self.bw[b] = {key: val}
            self.br[b] = {}
        for b in cwrites:
            d = self.bw.setdefault(b, {})
            d[key] = max(d.get(key, 0), val)
        for b in reads:
            d = self.br.setdefault(b, {})
            d[key] = max(d.get(key, 0), val)

    def op(self, eng, fn, reads=(), writes=()):
        waits = self._waits(eng, self._deps(eng, reads, writes))
        self.ccnt[eng] += 1
        n = self.ccnt[eng]
        sem = self.csem[eng]

        def run(e, waits=waits, fn=fn, sem=sem):
            for s, v in waits:
                e.wait_ge(s, v)
            fn(e).then_inc(sem, 1)

        self.streams[eng].append(run)
        self._record(eng, n, reads, writes, ())
        self.nins += 1

    def dma(self, q, out, in_, reads=(), writes=(), cwrites=(), **kw):
        slot = self.drr[q]
        self.drr[q] = (slot + 1) % self.NQ
        key = ("dma", q, slot)
        deps = self._deps(q, reads, writes)
        prev = self.dcnt[q][slot]
        if prev > 0:
            deps[key] = max(deps.get(key, 0), prev)
        waits = self._waits(q, deps)
        self.dcnt[q][slot] += 16
        v = self.dcnt[q][slot]
        sem = self.dsem[q][slot]

        def run(e, waits=waits, sem=sem, out=out, in_=in_, kw=kw):
            for s, vv in waits:
                e.wait_ge(s, vv)
            e.dma_start(out=out, in_=in_, **kw).then_inc(sem, 16)

        self.streams[q].append(run)
        self._record(key, v, reads, writes, cwrites)
        self.nins += 1

    def barrier(self):
        allk = {e: n for e, n in self.ccnt.items() if n > 0}
        for q in self.dsem:
            for i in range(self.NQ):
                if self.dcnt[q][i] > 0:
                    allk[("dma", q, i)] = self.dcnt[q][i]
        for eng in self.streams:
            waits = self._waits(eng, {k: v for k, v in allk.items() if k != eng})

            def run(e, waits=waits):
                for s, v in waits:
                    e.wait_ge(s, v)

            self.streams[eng].append(run)

    def finish(self):
        self.barrier()
        nc = self.nc
        with nc.Block() as block:
            @block.sync
            def _(e):
                for f in self.streams["sync"]:
                    f(e)

            @block.scalar
            def _(e):
                for f in self.streams["scalar"]:
                    f(e)

            @block.vector
            def _(e):
                for f in self.streams["vector"]:
                    f(e)

            @block.gpsimd
            def _(e):
                for f in self.streams["gpsimd"]:
                    f(e)

            @block.tensor
            def _(e):
                for f in self.streams["tensor"]:
                    f(e)


def build(T, dbg=False, phases="ABCDEF"):
    NT, NB, SEG = T // 128, T // 512, T // 2
    SEGP = SEG + 2
    nc = bass.Bass("TRN2", target_bir_lowering=False)

    def din(name, shape, dt=F32):
        return nc.dram_tensor(name, list(shape), dt, kind="ExternalInput").ap()

    skind = "ExternalOutput" if dbg else "Internal"

    def dscr(name, shape, dt):
        return nc.dram_tensor(name, list(shape), dt, kind=skind).ap()

    x = din("x", [T, D])
    w_in = din("w_in", [D, 2048])
    w_out = din("w_out", [D, D])
    w_up = din("w_up", [D, 2 * DFF])
    w_down = din("w_down", [DFF, D])
    w_glu = din("w_glu", [512, 512])
    gmix_pk = din("gmix_pk", [128, 8])
    gffn_pk = din("gffn_pk", [128, 8])
    gfin = din("gfin", [1, D])
    gsub = din("gsub", [128, 1])
    lamv = din("lamv", [1, 256])
    cw_pk = din("cw_pk", [128, 44, 3])
    cb_pk = din("cb_pk", [128, 44])
    ssmv = din("ssmv", [128, 12])
    a_row = din("a_row", [2, 3, 2048])
    a_sm = din("a_sm", [2, 128, 48])
    bmA = din("bmA", [2, 128, 4096])
    bmB = din("bmB", [2, 128, 4096])
    cmat = din("cmat", [2, 128, 4096])
    consts = din("consts", [128, NCON])
    cbias = din("cbias", [128, 4 * NB * NT])
    flag = din("flag", [128, 1])
    y = nc.dram_tensor("y", [T, D], F32, kind="ExternalOutput").ap()

    q_s = dscr("q_s", [512, T], BF16)
    k_s = dscr("k_s", [512, T], BF16)
    u_s = dscr("u_s", [512, T], BF16)
    v_s = dscr("v_s", [T, 512], BF16)
    a_s = dscr("a_s", [512, T], BF16)
    s_s = dscr("s_s", [512, T], BF16)
    yf_s = dscr("yf_s", [512, T], F32)
    ys_s = dscr("ys_s", [512, T], F32)
    x1_s = dscr("x1_s", [T, D], F32)
    h_s = dscr("h_s", [8, 128, 2 * SEGP], BF16)
    wus = dscr("wus", [44, 128, 8, 128], BF16)

    top = ExitStack()
    em = Em(nc, top)

    def rstd_chain(st4, key, scale, eps):
        em.op("vector", lambda e: e.tensor_scalar(out=st4[:, 1:2], in0=st4[:, 0:1], scalar1=scale, scalar2=eps,
                                                  op0=ALU.mult, op1=ALU.add), [key + "_0"], [key + "_1"])
        em.op("scalar", lambda e: e.activation(out=st4[:, 2:3], in_=st4[:, 1:2], func=AF.Sqrt), [key + "_1"], [key + "_2"])
        em.op("vector", lambda e: e.reciprocal(out=st4[:, 3:4], in_=st4[:, 2:3]), [key + "_2"], [key + "_3"])

    def phA():
        with ExitStack() as st:
            sb = lambda n, s, dt=F32: st.enter_context(nc.sbuf_tensor(n, list(s), dt))
            ps = lambda n, s, dt=F32: st.enter_context(nc.psum_tensor(n, list(s), dt))
            identf = sb("a_identf", [128, 128])
            identb = sb("a_identb", [128, 128], BF16)
            em.dma("sync", identf[:], consts[:, ID0:ID0 + 128], [], ["identf"])
            em.op("vector", lambda e: e.tensor_copy(out=identb[:], in_=identf[:]), ["identf"], ["identb"])
            gm = sb("a_gm", [128, 8])
            em.dma("sync", gm[:], gmix_pk[:, :], [], ["gm"])
            wbf = sb("a_wbf", [128, 8, 2048], BF16)
            wst = [sb(f"a_wst{i}", [128, 2048]) for i in range(2)]
            for k in range(8):
                b = k % 2
                em.dma("sync", wst[b][:], w_in[k * 128:(k + 1) * 128, :], [], [f"wst{b}"])
                if b == 0:
                    em.op("vector", lambda e, k=k, b=b: e.tensor_scalar(out=wbf[:, k, :], in0=wst[b][:], scalar1=gm[:, k:k + 1],
                                                                        scalar2=None, op0=ALU.mult), [f"wst{b}", "gm"], [f"wbf{k}"])
                else:
                    em.op("scalar", lambda e, k=k, b=b: e.activation(out=wbf[:, k, :], in_=wst[b][:], func=AF.Copy,
                                                                     scale=gm[:, k:k + 1]), [f"wst{b}", "gm"], [f"wbf{k}"])
            xt = [sb(f"a_xt{i}", [128, D]) for i in range(3)]
            junk = sb("a_junk", [128, D], BF16)
            xn = [sb(f"a_xn{i}", [128, D], BF16) for i in range(2)]
            s4 = [sb(f"a_s4{i}", [128, 4]) for i in range(2)]
            xT = [sb(f"a_xT{i}", [128, 8, 512], BF16) for i in range(3)]
            tp = [ps(f"a_tp{i}", [128, D], BF16) for i in range(2)]
            pp = [ps(f"a_pp{i}", [128, 512]) for i in range(4)]
            stg = [sb(f"a_stg{i}", [128, 512], BF16) for i in range(4)]
            cntb = [0]
            wk = [f"wbf{k}" for k in range(8)]

            def a_stage1(blk):
                xb = blk % 3
                for i in range(4):
                    t = blk * 4 + i
                    a, n2 = t % 3, t % 2
                    em.dma("sync", xt[a][:], x[t * 128:(t + 1) * 128, :], [], [f"xt{a}"])
                    em.op("scalar", lambda e, a=a, n2=n2: e.activation(out=junk[:], in_=xt[a][:], func=AF.Square,
                                                                       accum_out=s4[n2][:, 0:1]), [f"xt{a}"], ["junk", f"s4{n2}_0"])
                    rstd_chain(s4[n2], f"s4{n2}", 1.0 / D, 1e-6)
                    em.op("scalar", lambda e, a=a, n2=n2: e.activation(out=xn[n2][:], in_=xt[a][:], func=AF.Copy,
                                                                       scale=s4[n2][:, 3:4]), [f"xt{a}", f"s4{n2}_3"], [f"xn{n2}"])
                    for k in range(8):
                        em.op("tensor", lambda e, k=k, n2=n2: e.transpose(out=tp[n2][:, k * 128:(k + 1) * 128],
                                                                          in_=xn[n2][:, k * 128:(k + 1) * 128], identity=identb[:]),
                              [f"xn{n2}", "identb"], [f"tp{n2}"])
                    em.op("vector", lambda e, i=i, n2=n2, xb=xb: e.tensor_copy(
                        out=xT[xb][:, :, i * 128:(i + 1) * 128], in_=tp[n2][:].rearrange("p (k j) -> p k j", k=8)),
                          [f"tp{n2}"], [f"xT{xb}"])

            def a_stage2(blk):
                xb = blk % 3
                for m in range(16):
                    pb = cntb[0] % 4
                    sg = cntb[0] % 4
                    cntb[0] += 1
                    if m < 12:
                        col = m * 128 if m < 8 else 1536 + (m - 8) * 128
                        for k in range(8):
                            em.op("tensor", lambda e, k=k, col=col, pb=pb, xb=xb: e.matmul(
                                pp[pb][:], lhsT=wbf[:, k, col:col + 128], rhs=xT[xb][:, k, :], start=(k == 0), stop=(k == 7)),
                                  [wk[k], f"xT{xb}"], [f"pp{pb}"])
                        dst = (q_s if m < 4 else k_s if m < 8 else u_s)[(m % 4) * 128:(m % 4 + 1) * 128, blk * 512:(blk + 1) * 512]
                        dk = "q_s" if m < 4 else "k_s" if m < 8 else "u_s"
                    else:
                        i = m - 12
                        for k in range(8):
                            em.op("tensor", lambda e, k=k, i=i, pb=pb, xb=xb: e.matmul(
                                pp[pb][:], lhsT=xT[xb][:, k, i * 128:(i + 1) * 128], rhs=wbf[:, k, 1024:1536],
                                start=(k == 0), stop=(k == 7)), [wk[k], f"xT{xb}"], [f"pp{pb}"])
                        dst = v_s[(blk * 4 + i) * 128:(blk * 4 + i + 1) * 128, :]
                        dk = "v_s"
                    if m < 4:
                        em.op("scalar", lambda e, pb=pb, sg=sg: e.activation(out=stg[sg][:], in_=pp[pb][:], func=AF.Copy, scale=0.125),
                              [f"pp{pb}"], [f"stg{sg}"])
                    elif m % 2 == 0:
                        em.op("vector", lambda e, pb=pb, sg=sg: e.tensor_copy(out=stg[sg][:], in_=pp[pb][:]), [f"pp{pb}"], [f"stg{sg}"])
                    else:
                        em.op("scalar", lambda e, pb=pb, sg=sg: e.activation(out=stg[sg][:], in_=pp[pb][:], func=AF.Copy),
                              [f"pp{pb}"], [f"stg{sg}"])
                    em.dma("gpsimd", dst, stg[sg][:], [f"stg{sg}"], cwrites=[dk])
            a_stage1(0)
            if NB > 1:
                a_stage1(1)
            for blk in range(NB):
                if blk + 2 < NB:
                    a_stage1(blk + 2)
                a_stage2(blk)
            em.barrier()

    def phB():
        with ExitStack() as st:
            sb = lambda n, s, dt=F32: st.enter_context(nc.sbuf_tensor(n, list(s), dt))
            ps = lambda n, s, dt=F32: st.enter_context(nc.psum_tensor(n, list(s), dt))
            ramp = sb("b_ramp", [128, 512])
            absr = sb("b_absr", [128, 2048])
            em.dma("sync", ramp[:], consts[:, RP0:RP0 + 512], [], ["ramps"])
            em.dma("sync", absr[:], consts[:, AR0:AR0 + 2048], [], ["absr"])
            cb = sb("b_cb", [128, 4 * NB * NT])
            em.dma("sync", cb[:], cbias[:, :], [], ["cb"])
            onesf = sb("b_onesf", [128, 128])
            onesb = sb("b_onesb", [128, 128], BF16)
            em.op("vector", lambda e: e.memset(onesf[:], 1.0), [], ["onesf"])
            em.op("vector", lambda e: e.memset(onesb[:], 1.0), [], ["onesb"])
            lv = sb("b_lv", [128, 256])
            em.dma("sync", lv[:], lamv[0:1, :].to_broadcast([128, 256]), [], ["lv"])
            lp = sb("b_lp", [128, 128])
            l4 = sb("b_l4", [128, 8])
            em.op("vector", lambda e: e.tensor_tensor(out=lp[:, 0:64], in0=lv[:, 0:64], in1=lv[:, 64:128], op=ALU.mult), ["lv"], ["lp0"])
            em.op("vector", lambda e: e.tensor_tensor(out=lp[:, 64:128], in0=lv[:, 128:192], in1=lv[:, 192:256], op=ALU.mult), ["lv"], ["lp1"])
            em.op("scalar", lambda e: e.activation(out=lv[:, 0:64], in_=lp[:, 0:64], func=AF.Copy, accum_out=l4[:, 0:1]), ["lp0"], ["l40", "lvj"])
            em.op("scalar", lambda e: e.activation(out=lv[:, 64:128], in_=lp[:, 64:128], func=AF.Copy, accum_out=l4[:, 1:2]), ["lp1"], ["l41", "lvj2"])
            em.op("scalar", lambda e: e.activation(out=l4[:, 2:4], in_=l4[:, 0:2], func=AF.Exp), ["l40", "l41"], ["l42"])
            em.op("vector", lambda e: e.tensor_tensor(out=l4[:, 4:5], in0=l4[:, 3:4], in1=l4[:, 2:3], op=ALU.subtract), ["l42"], ["l44"])
            em.op("vector", lambda e: e.tensor_scalar(out=l4[:, 5:6], in0=l4[:, 4:5], scalar1=-0.2, scalar2=None, op0=ALU.add), ["l44"], ["neglam"])
            epsb = sb("b_epsb", [128, 1])
            em.op("vector", lambda e: e.memset(epsb[:], 1e-5), [], ["epsb"])
            gs = sb("b_gs", [128, 2])
            em.dma("sync", gs[:, 0:1], gsub[:, :], [], ["gs0"])
            em.op("vector", lambda e: e.tensor_scalar(out=gs[:, 1:2], in0=gs[:, 0:1], scalar1=0.8, scalar2=None, op0=ALU.mult), ["gs0"], ["gs1"])

            KT = [sb(f"b_KT{i}", [128, T], BF16) for i in range(2)]
            QT = [[sb(f"b_QT{i}_{c}", [128, T], BF16) for c in range(2)] for i in range(2)]
            for i in range(2):
                em.op("gpsimd", lambda e, i=i: e.memset(QT[i][0][64:128, :], 0.0), [], [f"QTz{i}0"])
                em.op("gpsimd", lambda e, i=i: e.memset(QT[i][1][0:64, :], 0.0), [], [f"QTz{i}1"])
            VV = [sb(f"b_VV{i}", [128, NT, 128], BF16) for i in range(2)]
            NSC = 4
            scp = [ps(f"b_scp{i}", [128, 512]) for i in range(NSC)]
            Op = [ps(f"b_O{i}", [128, 512]) for i in range(2)]
            Sp = [ps(f"b_S{i}", [128, 512]) for i in range(2)]
            NSB = 4
            sbt = [sb(f"b_sbt{i}", [128, 512]) for i in range(NSB)]
            NPT = 6
            pT = [sb(f"b_pT{i}", [128, 512], BF16) for i in range(NPT)]
            rs = [sb(f"b_rs{i}", [128, 512]) for i in range(2)]
            oc = [sb(f"b_oc{i}", [128, 512]) for i in range(2)]
            wt = sb("b_wt", [128, 512])
            sq = sb("b_sq", [128, 512])
            a1 = sb("b_a1", [128, 512])
            aT = [sb(f"b_aT{i}", [128, 512], BF16) for i in range(2)]
            LA = 3
            THR = 40.0
            v_r = v_s.rearrange("(kt kp) e -> kp kt e", kp=128)
            nq = 0
            sc_box = [0]
            pending = []
            for h in range(4):
                hb = h % 2
                em.dma("sync", KT[hb][:], k_s[h * 128:(h + 1) * 128, :], ["k_s"], [f"KT{hb}"])
                em.dma("sync", QT[hb][0][0:64, :], q_s[h * 128:h * 128 + 64, :], ["q_s", f"QTz{hb}0"], [f"QT{hb}_0"])
                em.dma("sync", QT[hb][1][64:128, :], q_s[h * 128 + 64:(h + 1) * 128, :], ["q_s", f"QTz{hb}1"], [f"QT{hb}_1"])
                VCH = max(1, NT // 4)
                for j in range(0, NT, VCH):
                    em.dma("sync", VV[hb][:, j:j + VCH, :], v_r[:, j:j + VCH, h * 128:(h + 1) * 128], ["v_s"], cwrites=[f"VV{hb}"],
                           writes=([f"VV{hb}"] if j == 0 else []))
                units = []
                for Qb in range(NB):
                    kept = []
                    for kt in range(NT):
                        rel = kt - 4 * Qb
                        if rel < 0:
                            dmin = 512 * Qb - (128 * kt + 127)
                        elif rel >= 4:
                            dmin = 128 * kt - (512 * Qb + 511)
                        else:
                            dmin = 0
                        if SLOPES[h] * dmin <= THR:
                            kept.append(kt)
                    for c in range(2):
                        for ii, kt in enumerate(kept):
                            units.append((Qb, c, kt, ii == 0, ii == len(kept) - 1))
                N = len(units)
                slot = {}
                for i in range(N + LA):
                    if i < N:
                        Qb, c, kt, first, last = units[i]
                        s3 = sc_box[0] % NSC
                        sc_box[0] += 1
                        s4_, p5 = i % NSB, i % NPT
                        slot[i] = p5
                        em.op("tensor", lambda e, c=c, kt=kt, Qb=Qb, s3=s3, hb=hb: e.matmul(
                            scp[s3][:], lhsT=KT[hb][:, kt * 128:(kt + 1) * 128],
                            rhs=QT[hb][c][:, Qb * 512:(Qb + 1) * 512], start=True, stop=True),
                              [f"KT{hb}", f"QT{hb}_{c}"], [f"scp{s3}"])
                        rel = kt - 4 * Qb
                        if 0 <= rel < 4:
                            tab, sc, tk = absr[:, rel * 512:(rel + 1) * 512], -SLOPES[h], "absr"
                        elif rel < 0:
                            tab, sc, tk = ramp[:], -SLOPES[h], "ramps"
                        else:
                            tab, sc, tk = ramp[:], SLOPES[h], "ramps"
                        em.op("vector", lambda e, tab=tab, sc=sc, s3=s3, s4_=s4_: e.scalar_tensor_tensor(
                            out=sbt[s4_][:], in0=tab, scalar=sc, in1=scp[s3][:], op0=ALU.mult, op1=ALU.add),
                              [f"scp{s3}", tk], [f"sbt{s4_}"])
                        col = (h * NB + Qb) * NT + kt
                        em.op("scalar", lambda e, s4_=s4_, p5=p5, col=col: e.activation(
                            out=pT[p5][:], in_=sbt[s4_][:], func=AF.Exp, bias=cb[:, col:col + 1]),
                              [f"sbt{s4_}", "cb"], [f"pT{p5}"])
                    j = i - LA
                    if j >= 0:
                        Qb, c, kt, first, last = units[j]
                        p5 = slot[j]
                        em.op("tensor", lambda e, c=c, kt=kt, p5=p5, hb=hb, first=first, last=last: e.matmul(
                            Op[c][:], lhsT=VV[hb][:, kt, :], rhs=pT[p5][:], start=first, stop=last),
                              [f"VV{hb}", f"pT{p5}"], [f"O{c}"])
                        em.op("tensor", lambda e, c=c, p5=p5, first=first, last=last: e.matmul(
                            Sp[c][:], lhsT=onesb[:], rhs=pT[p5][:], start=first, stop=last),
                              ["onesb", f"pT{p5}"], [f"S{c}"])
                        if last:
                            def mk_rc(c, k):
                                return lambda: em.op("vector", lambda e: e.reciprocal(out=rs[c][:, k * 128:(k + 1) * 128], in_=Sp[c][:, k * 128:(k + 1) * 128]),
                                                     [f"S{c}"], [f"rs{c}_{k}"])
                            for k in range(4):
                                pending.append([i + k, mk_rc(c, k)])

                            def mk_oc(c):
                                return lambda: em.op("vector", lambda e: e.tensor_tensor(out=oc[c][:], in0=Op[c][:], in1=rs[c][:], op=ALU.mult),
                                                     [f"O{c}"] + [f"rs{c}_{k}" for k in range(4)], [f"oc{c}"])
                            pending.append([i + 4, mk_oc(c)])
                            if c == 1:
                                ab = nq % 2
                                nq += 1

                                def fin_a():
                                    em.op("vector", lambda e: e.scalar_tensor_tensor(out=wt[:], in0=oc[1][:], scalar=l4[:, 5:6], in1=oc[0][:],
                                                                                     op0=ALU.mult, op1=ALU.add), ["oc0", "oc1", "neglam"], ["wt"])
                                    em.op("gpsimd", lambda e: e.tensor_tensor(out=sq[:], in0=wt[:], in1=wt[:], op=ALU.mult), ["wt"], ["sq"])

                                def fin_b(h=h):
                                    q3 = sc_box[0] % NSC
                                    sc_box[0] += 1
                                    em.op("tensor", lambda e: e.matmul(scp[q3][:], lhsT=onesf[:], rhs=sq[:], start=True, stop=True),
                                          ["onesf", "sq"], [f"scp{q3}"])
                                    em.op("scalar", lambda e: e.activation(out=a1[:], in_=scp[q3][:], func=AF.Ln, scale=1.0 / 128, bias=epsb[:, 0:1]),
                                          [f"scp{q3}", "epsb"], ["a1"])
                                    em.op("scalar", lambda e: e.activation(out=a1[:], in_=a1[:], func=AF.Exp, scale=-0.5), ["a1"], ["a1"])

                                def fin_c(h=h, Qb=Qb, ab=ab):
                                    em.op("vector", lambda e: e.tensor_tensor(out=a1[:], in0=wt[:], in1=a1[:], op=ALU.mult), ["wt", "a1"], ["a1"])
                                    em.op("scalar", lambda e: e.activation(out=aT[ab][:], in_=a1[:], func=AF.Copy, scale=gs[:, 1:2]),
                                          ["a1", "gs1"], [f"aT{ab}"])
                                    em.dma("gpsimd", a_s[h * 128:(h + 1) * 128, Qb * 512:(Qb + 1) * 512], aT[ab][:], [f"aT{ab}"], cwrites=["a_s"])
                                pending.append([i + 5, fin_a])
                                pending.append([i + 8, fin_b])
                                pending.append([i + 10, fin_c])
                    due = [p_ for p_ in pending if p_[0] <= i]
                    for p_ in due:
                        pending.remove(p_)
                        p_[1]()
                for p_ in list(pending):
                    p_[1]()
                pending.clear()
            em.barrier()

    def phC(d):
        if True:
            with ExitStack() as st:
                sb = lambda n, s, dt=F32: st.enter_context(nc.sbuf_tensor(n, list(s), dt))
                ps = lambda n, s, dt=F32: st.enter_context(nc.psum_tensor(n, list(s), dt))
                P = f"c{d}_"
                JC = JC0 + d
                JR = JF0 if d == 0 else JB0
                jc = sb(P + "jc", [128, 1])
                em.dma("sync", jc[:], consts[:, JC:JC + 1], [], ["jc"], allow_slow_non_contiguous=True)
                jrow = sb(P + "jrow", [128, 128])
                em.dma("sync", jrow[:], consts[:, JR:JR + 128], [], ["jrow"])
                tri = sb(P + "tri", [128, 128])
                T0 = TL0 if d == 0 else TU0
                em.dma("sync", tri[:], consts[:, T0:T0 + 128], [], ["trif"])
                trib = sb(P + "trib", [128, 128], BF16)
                ntrib = sb(P + "ntrib", [128, 128], BF16)
                em.op("vector", lambda e: e.tensor_copy(out=trib[:], in_=tri[:]), ["trif"], ["trib"])
                em.op("vector", lambda e: e.tensor_scalar(out=ntrib[:], in0=tri[:], scalar1=-1.0, scalar2=None, op0=ALU.mult), ["trif"], ["ntrib"])
                Er = sb(P + "Er", [128, 2048])
                Ei = sb(P + "Ei", [128, 2048])
                Fr = sb(P + "Fr", [128, 2048])
                Fi = sb(P + "Fi", [128, 2048])
                with ExitStack() as st2:
                    sb2 = lambda n, s, dt=F32: st2.enter_context(nc.sbuf_tensor(n, list(s), dt))
                    W = [sb2(P + f"w{i}", [128, 2048]) for i in range(10)]
                    WI = sb2(P + "wi", [128, 2048], I32)
                    wkey = [f"W{i}" for i in range(10)]

                    def vop(fn, r, w):
                        em.op("vector", fn, r, w)

                    def sincos(ph, phk, out_s, out_c, ok_s, ok_c, tmp, tmpk, tmp2, tmp2k):
                        for (off, o, okk) in ((0.0, out_s, ok_s), (math.pi / 2, out_c, ok_c)):
                            vop(lambda e, off=off: e.tensor_scalar(out=tmp[:], in0=ph[:], scalar1=off, scalar2=1.0 / TWO_PI,
                                                                   op0=ALU.add, op1=ALU.mult), [phk], [tmpk])
                            vop(lambda e: e.tensor_copy(out=WI[:], in_=tmp[:]), [tmpk], ["WI"])
                            vop(lambda e: e.tensor_copy(out=tmp[:], in_=WI[:]), ["WI"], [tmpk])
                            vop(lambda e: e.scalar_tensor_tensor(out=tmp2[:], in0=tmp[:], scalar=-TWO_PI, in1=ph[:],
                                                                 op0=ALU.mult, op1=ALU.add), [tmpk, phk], [tmp2k])
                            vop(lambda e, off=off: e.tensor_scalar(out=tmp2[:], in0=tmp2[:], scalar1=off, scalar2=math.pi,
                                                                   op0=ALU.add, op1=ALU.min), [tmp2k], [tmp2k])
                            vop(lambda e: e.tensor_scalar(out=tmp2[:], in0=tmp2[:], scalar1=-math.pi, scalar2=None, op0=ALU.max), [tmp2k], [tmp2k])
                            em.op("scalar", lambda e, o=o: e.activation(out=o[:], in_=tmp2[:], func=AF.Sin), [tmp2k], [okk])

                    AR, AI, LS = W[0], W[1], W[2]
                    em.dma("sync", AR[:], a_row[d, 0:1, :].to_broadcast([128, 2048]), [], [wkey[0]])
                    em.dma("sync", AI[:], a_row[d, 1:2, :].to_broadcast([128, 2048]), [], [wkey[1]])
                    em.dma("sync", LS[:], a_row[d, 2:3, :].to_broadcast([128, 2048]), [], [wkey[2]])
                    em.op("scalar", lambda e: e.activation(out=LS[:], in_=LS[:], func=AF.Exp), [wkey[2]], [wkey[2]])
                    ars, ais = W[3], W[4]
                    vop(lambda e: e.tensor_tensor(out=ars[:], in0=AR[:], in1=LS[:], op=ALU.mult), [wkey[0], wkey[2]], [wkey[3]])
                    vop(lambda e: e.tensor_tensor(out=ais[:], in0=AI[:], in1=LS[:], op=ALU.mult), [wkey[1], wkey[2]], [wkey[4]])
                    sincos(ais, wkey[4], W[5], W[6], wkey[5], wkey[6], W[7], wkey[7], W[8], wkey[8])
                    mag = W[7]
                    em.op("scalar", lambda e: e.activation(out=mag[:], in_=ars[:], func=AF.Exp), [wkey[3]], [wkey[7]])
                    lbr, lbi = W[6], W[5]
                    vop(lambda e: e.tensor_tensor(out=lbr[:], in0=W[6][:], in1=mag[:], op=ALU.mult), [wkey[6], wkey[7]], [wkey[6]])
                    vop(lambda e: e.tensor_tensor(out=lbi[:], in0=W[5][:], in1=mag[:], op=ALU.mult), [wkey[5], wkey[7]], [wkey[5]])
                    vop(lambda e: e.tensor_scalar(out=lbr[:], in0=lbr[:], scalar1=-1.0, scalar2=None, op0=ALU.add), [wkey[6]], [wkey[6]])
                    den = W[7]
                    vop(lambda e: e.tensor_tensor(out=den[:], in0=AR[:], in1=AR[:], op=ALU.mult), [wkey[0]], [wkey[7]])
                    vop(lambda e: e.tensor_tensor(out=W[8][:], in0=AI[:], in1=AI[:], op=ALU.mult), [wkey[1]], [wkey[8]])
                    vop(lambda e: e.tensor_tensor(out=den[:], in0=den[:], in1=W[8][:], op=ALU.add), [wkey[7], wkey[8]], [wkey[7]])
                    vop(lambda e: e.reciprocal(out=den[:], in_=den[:]), [wkey[7]], [wkey[7]])
                    vop(lambda e: e.tensor_tensor(out=W[8][:], in0=lbr[:], in1=AR[:], op=ALU.mult), [wkey[6], wkey[0]], [wkey[8]])
                    vop(lambda e: e.tensor_tensor(out=W[9][:], in0=lbi[:], in1=AI[:], op=ALU.mult), [wkey[5], wkey[1]], [wkey[9]])
                    vop(lambda e: e.tensor_tensor(out=W[8][:], in0=W[8][:], in1=W[9][:], op=ALU.add), [wkey[8], wkey[9]], [wkey[8]])
                    vop(lambda e: e.tensor_tensor(out=W[8][:], in0=W[8][:], in1=den[:], op=ALU.mult), [wkey[8], wkey[7]], [wkey[8]])
                    vop(lambda e: e.tensor_tensor(out=W[9][:], in0=lbi[:], in1=AR[:], op=ALU.mult), [wkey[5], wkey[0]], [wkey[9]])
                    vop(lambda e: e.tensor_tensor(out=W[2][:], in0=lbr[:], in1=AI[:], op=ALU.mult), [wkey[6], wkey[1]], [wkey[2]])
                    vop(lambda e: e.tensor_tensor(out=W[9][:], in0=W[9][:], in1=W[2][:], op=ALU.subtract), [wkey[9], wkey[2]], [wkey[9]])
                    vop(lambda e: e.tensor_tensor(out=W[9][:], in0=W[9][:], in1=den[:], op=ALU.mult), [wkey[9], wkey[7]], [wkey[9]])
                    fre, fim = W[8], W[9]
                    ph = W[0]
                    vop(lambda e: e.tensor_scalar(out=ph[:], in0=ais[:], scalar1=jc[:, 0:1], scalar2=None, op0=ALU.mult), [wkey[4], "jc"], [wkey[0]])
                    emag = W[1]
                    em.op("scalar", lambda e: e.activation(out=emag[:], in_=ars[:], func=AF.Exp, scale=jc[:, 0:1]), [wkey[3], "jc"], [wkey[1]])
                    sincos(ph, wkey[0], W[5], W[6], wkey[5], wkey[6], W[7], wkey[7], W[2], wkey[2])
                    vop(lambda e: e.tensor_tensor(out=W[7][:], in0=fre[:], in1=W[6][:], op=ALU.mult), [wkey[8], wkey[6]], [wkey[7]])
                    vop(lambda e: e.tensor_tensor(out=W[2][:], in0=fim[:], in1=W[5][:], op=ALU.mult), [wkey[9], wkey[5]], [wkey[2]])
                    vop(lambda e: e.tensor_tensor(out=W[7][:], in0=W[7][:], in1=W[2][:], op=ALU.subtract), [wkey[7], wkey[2]], [wkey[7]])
                    vop(lambda e: e.tensor_tensor(out=Er[:], in0=W[7][:], in1=emag[:], op=ALU.mult), [wkey[7], wkey[1]], ["Er"])
                    vop(lambda e: e.tensor_tensor(out=W[7][:], in0=fre[:], in1=W[5][:], op=ALU.mult), [wkey[8], wkey[5]], [wkey[7]])
                    vop(lambda e: e.tensor_tensor(out=W[2][:], in0=fim[:], in1=W[6][:], op=ALU.mult), [wkey[9], wkey[6]], [wkey[2]])
                    vop(lambda e: e.tensor_tensor(out=W[7][:], in0=W[7][:], in1=W[2][:], op=ALU.add), [wkey[7], wkey[2]], [wkey[7]])
                    vop(lambda e: e.tensor_tensor(out=Ei[:], in0=W[7][:], in1=emag[:], op=ALU.mult), [wkey[7], wkey[1]], ["Ei"])
                    asm = sb2(P + "asm", [128, 48])
                    em.dma("sync", asm[:], a_sm[d, :, :], [], ["asm"])
                    em.op("scalar", lambda e: e.activation(out=asm[:, 32:48], in_=asm[:, 32:48], func=AF.Exp), ["asm"], ["asm"])
                    vop(lambda e: e.tensor_tensor(out=asm[:, 0:16], in0=asm[:, 0:16], in1=asm[:, 32:48], op=ALU.mult), ["asm"], ["asm"])
                    vop(lambda e: e.tensor_tensor(out=asm[:, 16:32], in0=asm[:, 16:32], in1=asm[:, 32:48], op=ALU.mult), ["asm"], ["asm"])
                    v3 = lambda t: t[:].rearrange("p (a b) -> p a b", a=16)
                    jb3 = jrow[:, :].unsqueeze(1).to_broadcast([128, 16, 128])
                    fa = W[0]
                    vop(lambda e: e.tensor_tensor(out=v3(fa), in0=asm[:, 0:16].unsqueeze(2).to_broadcast([128, 16, 128]), in1=jb3, op=ALU.mult),
                        ["asm", "jrow"], [wkey[0]])
                    fmag = W[1]
                    em.op("scalar", lambda e: e.activation(out=fmag[:], in_=fa[:], func=AF.Exp), [wkey[0]], [wkey[1]])
                    fph = W[3]
                    vop(lambda e: e.tensor_tensor(out=v3(fph), in0=asm[:, 16:32].unsqueeze(2).to_broadcast([128, 16, 128]), in1=jb3, op=ALU.mult),
                        ["asm", "jrow"], [wkey[3]])
                    sincos(fph, wkey[3], W[5], W[6], wkey[5], wkey[6], W[7], wkey[7], W[2], wkey[2])
                    vop(lambda e: e.tensor_tensor(out=Fr[:], in0=W[6][:], in1=fmag[:], op=ALU.mult), [wkey[6], wkey[1]], ["Fr"])
                    vop(lambda e: e.tensor_tensor(out=Fi[:], in0=W[5][:], in1=fmag[:], op=ALU.mult), [wkey[5], wkey[1]], ["Fi"])
                    em.barrier()
                BA = sb(P + "BA", [128, 4096], BF16)
                BB = sb(P + "BB", [128, 4096], BF16)
                CM = sb(P + "CM", [128, 4096], BF16)
                with ExitStack() as st2:
                    sb2 = lambda n, s, dt=F32: st2.enter_context(nc.sbuf_tensor(n, list(s), dt))
                    wa = sb2(P + "wa", [128, 4096])
                    wb = sb2(P + "wb", [128, 4096])
                    wc = sb2(P + "wc", [128, 4096])
                    em.dma("sync", wa[:], bmA[d, :, :], [], ["wa"])
                    em.dma("sync", wb[:], bmB[d, :, :], [], ["wb"])
                    em.dma("sync", wc[:], cmat[d, :, :], [], ["wc"])
                    em.op("vector", lambda e: e.tensor_copy(out=BA[:], in_=wa[:]), ["wa"], ["BA"])
                    w4 = lambda t: t[:].rearrange("p (a h c) -> p a h c", a=8, h=2)
                    em.op("vector", lambda e: e.tensor_scalar(out=w4(BB)[:, :, 0, :], in0=w4(wb)[:, :, 0, :], scalar1=-1.0, scalar2=None, op0=ALU.mult), ["wb"], ["BB0"])
                    em.op("vector", lambda e: e.tensor_copy(out=w4(BB)[:, :, 1, :], in_=w4(wb)[:, :, 1, :]), ["wb"], ["BB1"])
                    c5 = lambda t: t[:].rearrange("p (a h c) -> p a h c", a=8, h=2)
                    em.op("vector", lambda e: e.tensor_copy(out=c5(CM)[:, :, 0, :], in_=c5(wc)[:, :, 0, :]), ["wc"], ["CM0"])
                    em.op("vector", lambda e: e.tensor_scalar(out=c5(CM)[:, :, 1, :], in0=c5(wc)[:, :, 1, :], scalar1=-1.0, scalar2=None, op0=ALU.mult), ["wc"], ["CM1"])
                    em.barrier()
                dv = sb(P + "dv", [128, 12])
                em.dma("sync", dv[:], ssmv[:, :], [], ["dv"])
                fl = sb(P + "fl", [128, 1])
                em.dma("sync", fl[:], flag[:, :], [], ["fl"])
                uT = [sb(P + f"uT{i}", [128, 4, 128], BF16) for i in range(3)]
                psA = ps(P + "psA", [128, 512])
                psB = ps(P + "psB", [128, 512])
                NG = 4
                psG = [ps(P + f"psG{i}", [128, 512]) for i in range(NG)]
                psY = [ps(P + f"psY{i}", [128, 512]) for i in range(2)]
                T1 = [sb(P + f"T1{i}", [128, 512], BF16) for i in range(2)]
                T2 = [sb(P + f"T2{i}", [128, 512], BF16) for i in range(2)]
                g1 = [sb(P + f"g1{i}", [128, 512], BF16) for i in range(NG)]
                gl = sb(P + "gl", [128, 8, 4])
                t1 = [sb(P + f"t1{i}", [128, 512], BF16) for i in range(NG)]
                t2 = [sb(P + f"t2{i}", [128, 512], BF16) for i in range(NG)]
                hc = [sb(P + f"hc{i}", [128, 8, 4]) for i in range(2)]
                cq = sb(P + "cq", [128, 8, 8])
                yo = [sb(P + f"yo{i}", [128, 4, 128]) for i in range(2)]
                yfl = [sb(P + f"yfl{i}", [128, 4, 128]) for i in range(2)]
                em.op("gpsimd", lambda e: e.memset(hc[0][:], 0.0), [], ["hc0"])
                Er3 = Er[:].rearrange("p (a c) -> p a c", a=8)
                Ei3 = Ei[:].rearrange("p (a c) -> p a c", a=8)
                Fr4 = Fr[:].rearrange("p (a h j) -> p a h j", a=8, h=2)
                Fi4 = Fi[:].rearrange("p (a h j) -> p a h j", a=8, h=2)
                Frb_ = sb(P + "Frb", [128, 2048], BF16)
                Fib_ = sb(P + "Fib", [128, 2048], BF16)
                nFib_ = sb(P + "nFib", [128, 2048], BF16)
                em.op("vector", lambda e: e.tensor_copy(out=Frb_[:], in_=Fr[:]), ["Fr"], ["Frb"])
                em.op("vector", lambda e: e.tensor_copy(out=Fib_[:], in_=Fi[:]), ["Fi"], ["Fib"])
                em.op("vector", lambda e: e.tensor_scalar(out=nFib_[:], in0=Fi[:], scalar1=-1.0, scalar2=None, op0=ALU.mult), ["Fi"], ["nFib"])
                Frb4 = Frb_[:].rearrange("p (a h j) -> p a h j", a=8, h=2)
                Fib4 = Fib_[:].rearrange("p (a h j) -> p a h j", a=8, h=2)
                nFib4 = nFib_[:].rearrange("p (a h j) -> p a h j", a=8, h=2)
                LCOL = 127 if d == 0 else 0
                Fl = sb(P + "Fl", [128, 8, 6])
                em.op("vector", lambda e: e.tensor_copy(out=Fl[:, :, 0:2], in_=Fr4[:, :, :, LCOL]), ["Fr"], ["Fl0"])
                em.op("vector", lambda e: e.tensor_copy(out=Fl[:, :, 2:4], in_=Fi4[:, :, :, LCOL]), ["Fi"], ["Fl1"])
                em.op("vector", lambda e: e.tensor_scalar(out=Fl[:, :, 4:6], in0=Fi4[:, :, :, LCOL], scalar1=-1.0, scalar2=None, op0=ALU.mult), ["Fi"], ["Fl2"])
                em.op("vector", lambda e: e.tensor_copy(out=Fl[:, 0, 0:1], in_=Fl[:, 0, 0:1]), ["Fl0", "Fl1", "Fl2"], ["Fl"])
                order = list(range(NT)) if d == 0 else list(range(NT - 1, -1, -1))
                u_r = u_s.rearrange("(k p) t -> p k t", p=128)
                yf_r = yf_s.rearrange("(k p) t -> p k t", p=128)
                ys_r = ys_s.rearrange("(k p) t -> p k t", p=128)
                NP = NT * 8
                r4 = lambda t_: t_[:].rearrange("p (r h j) -> p r h j", r=2, h=2)
                h3 = lambda a_: a_.rearrange("p (h j) -> p h j", h=2)

                def stage_ab(gi):
                    ci, p = gi // 8, gi % 8
                    n = order[ci]
                    ub, b2 = ci % 3, gi % 2
                    kq, pq = p // 2, p % 2
                    co = kq * 1024 + pq * 512
                    if p == 0:
                        em.dma("sync", uT[ub][:], u_r[:, :, n * 128:(n + 1) * 128], ["u_s"], [f"uT{ub}"])
                    em.op("tensor", lambda e: e.matmul(psA[:], lhsT=uT[ub][:, kq, :], rhs=BA[:, co:co + 512], start=True, stop=True),
                          [f"uT{ub}", "BA"], ["psA"])
                    em.op("vector", lambda e: e.tensor_tensor(
                        out=T1[b2][:].rearrange("p (h c) -> p h c", h=2), in0=psA[:].rearrange("p (h c) -> p h c", h=2),
                        in1=Er3[:, p, :].unsqueeze(1).to_broadcast([128, 2, 256]), op=ALU.mult), ["psA", "Er"], [f"T1{b2}"])
                    em.op("vector", lambda e: e.tensor_tensor(
                        out=T2[b2][:].rearrange("p (h c) -> p h c", h=2), in0=psA[:].rearrange("p (h c) -> p h c", h=2),
                        in1=Ei3[:, p, :].unsqueeze(1).to_broadcast([128, 2, 256]), op=ALU.mult), ["psA", "Ei"], [f"T2{b2}"])

                def stage_cum(gi):
                    b2, g3 = gi % 2, gi % NG
                    for tl in range(4):
                        stl = (tl + 2) % 4
                        tm, tk = (ntrib, "ntrib") if tl < 2 else (trib, "trib")
                        em.op("tensor", lambda e, tl=tl: e.matmul(
                            psG[g3][:, tl * 128:(tl + 1) * 128], lhsT=T1[b2][:, tl * 128:(tl + 1) * 128], rhs=trib[:],
                            start=True, stop=False), [f"T1{b2}", "trib"], [f"psG{g3}"])
                        em.op("tensor", lambda e, tl=tl, stl=stl, tm=tm: e.matmul(
                            psG[g3][:, tl * 128:(tl + 1) * 128], lhsT=T2[b2][:, stl * 128:(stl + 1) * 128], rhs=tm[:],
                            start=False, stop=True), [f"T2{b2}", tk], [f"psG{g3}"])

                def stage_s2(gi):
                    ci, p = gi // 8, gi % 8
                    n = order[ci]
                    g3 = gi % NG
                    cb_, nb_ = ci % 2, (ci + 1) % 2
                    if p == 0 and ci != 0:
                        first_of_seg = ((n * 128) % SEG == 0) if d == 0 else (((n + 1) * 128) % SEG == 0)
                        if first_of_seg:
                            em.op("gpsimd", lambda e: e.tensor_scalar(out=hc[cb_][:], in0=hc[cb_][:], scalar1=fl[:, 0:1], scalar2=None, op0=ALU.mult),
                                  [f"hc{cb_}", "fl"], [f"hc{cb_}"])
                    for tl in range(4):
                        em.op("scalar", lambda e, tl=tl: e.activation(
                            out=g1[g3][:, tl * 128:(tl + 1) * 128], in_=psG[g3][:, tl * 128:(tl + 1) * 128], func=AF.Identity,
                            bias=hc[cb_][:, p, tl:tl + 1]), [f"psG{g3}", f"hc{cb_}"], [f"g1{g3}"])
                    for tl in range(4):
                        cc_ = tl * 128 + LCOL
                        em.op("scalar", lambda e, tl=tl, cc_=cc_: e.activation(
                            out=gl[:, p, tl:tl + 1], in_=psG[g3][:, cc_:cc_ + 1], func=AF.Identity,
                            bias=hc[cb_][:, p, tl:tl + 1]), [f"psG{g3}", f"hc{cb_}"], [f"gl{p}"])
                    em.op("vector", lambda e: e.tensor_tensor(out=h3(t1[g3][:, 0:256]), in0=h3(g1[g3][:, 0:256]), in1=Frb4[:, p, :, :], op=ALU.mult),
                          [f"g1{g3}", "Frb"], [f"t1{g3}a"])
                    em.op("vector", lambda e: e.tensor_tensor(out=h3(t1[g3][:, 256:512]), in0=h3(g1[g3][:, 256:512]), in1=Frb4[:, p, :, :], op=ALU.mult),
                          [f"g1{g3}", "Frb"], [f"t1{g3}b"])
                    em.op("vector", lambda e: e.tensor_tensor(out=h3(t2[g3][:, 0:256]), in0=h3(g1[g3][:, 256:512]), in1=nFib4[:, p, :, :], op=ALU.mult),
                          [f"g1{g3}", "nFib"], [f"t2{g3}a"])
                    em.op("vector", lambda e: e.tensor_tensor(out=h3(t2[g3][:, 256:512]), in0=h3(g1[g3][:, 0:256]), in1=Fib4[:, p, :, :], op=ALU.mult),
                          [f"g1{g3}", "Fib"], [f"t2{g3}b"])
                    em.op("gpsimd", lambda e: e.tensor_tensor(out=cq[:, p, 0:2], in0=gl[:, p, 0:2], in1=Fl[:, p, 0:2], op=ALU.mult), [f"gl{p}", "Fl"], [f"cq{p}a"])
                    em.op("gpsimd", lambda e: e.tensor_tensor(out=cq[:, p, 2:4], in0=gl[:, p, 2:4], in1=Fl[:, p, 0:2], op=ALU.mult), [f"gl{p}", "Fl"], [f"cq{p}a2"])
                    em.op("gpsimd", lambda e: e.tensor_tensor(out=cq[:, p, 4:6], in0=gl[:, p, 2:4], in1=Fl[:, p, 4:6], op=ALU.mult), [f"gl{p}", "Fl"], [f"cq{p}b"])
                    em.op("gpsimd", lambda e: e.tensor_tensor(out=cq[:, p, 6:8], in0=gl[:, p, 0:2], in1=Fl[:, p, 2:4], op=ALU.mult), [f"gl{p}", "Fl"], [f"cq{p}c"])
                    em.op("gpsimd", lambda e: e.tensor_tensor(out=hc[nb_][:, p, :], in0=cq[:, p, 0:4], in1=cq[:, p, 4:8], op=ALU.add),
                          [f"cq{p}a", f"cq{p}a2", f"cq{p}b", f"cq{p}c"], [f"hc{nb_}"])

                def stage_cmm(gi):
                    ci, p = gi // 8, gi % 8
                    n = order[ci]
                    g3 = gi % NG
                    kq, pq = p // 2, p % 2
                    yb, ub = ci % 2, ci % 3
                    for tl in range(4):
                        for (src, sk) in ((t1, "t1"), (t2, "t2")):
                            first = (pq == 0 and tl == 0 and sk == "t1")
                            last = (pq == 1 and tl == 3 and sk == "t2")
                            cc = p * 512 + tl * 128
                            rk = [f"t1{g3}a", f"t1{g3}b"] if sk == "t1" else [f"t2{g3}a", f"t2{g3}b"]
                            em.op("tensor", lambda e, tl=tl, src=src, first=first, last=last, cc=cc: e.matmul(
                                psY[yb][:, kq * 128:(kq + 1) * 128], lhsT=CM[:, cc:cc + 128], rhs=src[g3][:, tl * 128:(tl + 1) * 128],
                                start=first, stop=last), rk + ["CM0", "CM1"], [f"psY{yb}"])
                    if p != 7:
                        return
                    if d == 0:
                        em.op("scalar", lambda e: e.activation(out=yo[yb][:].rearrange("p k j -> p (k j)"), in_=psY[yb][:], func=AF.Copy),
                              [f"psY{yb}"], [f"yo{yb}"])
                        em.dma("gpsimd", yf_r[:, :, n * 128:(n + 1) * 128], yo[yb][:], [f"yo{yb}"], cwrites=["yf_s"])
                    else:
                        em.dma("sync", yfl[yb][:], yf_r[:, :, n * 128:(n + 1) * 128], ["yf_s"], [f"yfl{yb}"])
                        em.op("vector", lambda e: e.tensor_tensor(out=yo[yb][:].rearrange("p k j -> p (k j)"), in0=psY[yb][:],
                                                                  in1=yfl[yb][:].rearrange("p k j -> p (k j)"), op=ALU.add),
                              [f"psY{yb}", f"yfl{yb}"], [f"yo{yb}"])
                        em.op("gpsimd", lambda e: e.tensor_tensor(out=yfl[yb][:], in0=uT[ub][:],
                                                                  in1=dv[:, 0:4].unsqueeze(2).to_broadcast([128, 4, 128]), op=ALU.mult),
                              [f"uT{ub}", "dv", f"yo{yb}"], [f"yfl{yb}"])
                        em.op("gpsimd", lambda e: e.tensor_tensor(out=yo[yb][:], in0=yo[yb][:], in1=yfl[yb][:], op=ALU.add),
                              [f"yo{yb}", f"yfl{yb}"], [f"yo{yb}"])
                        em.dma("gpsimd", ys_r[:, :, n * 128:(n + 1) * 128], yo[yb][:], [f"yo{yb}"], cwrites=["ys_s"])

                for i in range(NP + 2):
                    if i < NP:
                        stage_ab(i)
                    if 0 <= i - 1 < NP:
                        stage_cum(i - 1)
                        stage_s2(i - 1)
                    if 0 <= i - 2 < NP:
                        stage_cmm(i - 2)
                em.barrier()

    def phD():
        with ExitStack() as st:
            sb = lambda n, s, dt=F32: st.enter_context(nc.sbuf_tensor(n, list(s), dt))
            ps = lambda n, s, dt=F32: st.enter_context(nc.psum_tensor(n, list(s), dt))
            wg = sb("d_wg", [128, 4, 512], BF16)
            em.dma("gpsimd", wg[:], w_glu.rearrange("(k p) n -> p k n", p=128), [], ["wg"])
            dv = sb("d_dv", [128, 12])
            em.dma("sync", dv[:], ssmv[:, :], [], ["dv"])
            onesf = sb("d_onesf", [128, 128])
            em.op("vector", lambda e: e.memset(onesf[:], 1.0), [], ["onesf"])
            ysb = [sb(f"d_ys{i}", [128, 4, 512]) for i in range(2)]
            y1 = [sb(f"d_y1{i}", [128, 4, 512]) for i in range(2)]
            y1b = [sb(f"d_y1b{i}", [128, 4, 512], BF16) for i in range(2)]
            sg = [sb(f"d_sg{i}", [128, 512]) for i in range(2)]
            y2 = [sb(f"d_y2{i}", [128, 4, 512]) for i in range(2)]
            sqd = [sb(f"d_sq{i}", [128, 512]) for i in range(2)]
            rsd = sb("d_rsd", [128, 512])
            rsd2 = sb("d_rsd2", [128, 512])
            so = [sb(f"d_so{i}", [128, 4, 512], BF16) for i in range(2)]
            pz = [ps(f"d_pz{i}", [128, 512]) for i in range(4)]
            pq_ = ps("d_pq", [128, 512])
            ys_r = ys_s.rearrange("(k p) t -> p k t", p=128)
            s_r = s_s.rearrange("(k p) t -> p k t", p=128)
            def d_stage1(blk):
                b = blk % 2
                em.dma("sync", ysb[b][:], ys_r[:, :, blk * 512:(blk + 1) * 512], ["ys_s"], [f"ys{b}"])
                em.op("scalar", lambda e: e.activation(out=y1[b][:], in_=ysb[b][:], func=AF.Gelu_apprx_tanh), [f"ys{b}"], [f"y1{b}"])
                em.op("vector", lambda e: e.tensor_copy(out=y1b[b][:], in_=y1[b][:]), [f"y1{b}"], [f"y1b{b}"])
                for mo in range(4):
                    for k in range(4):
                        em.op("tensor", lambda e, k=k, mo=mo: e.matmul(pz[mo][:], lhsT=wg[:, k, mo * 128:(mo + 1) * 128], rhs=y1b[b][:, k, :],
                                                                       start=(k == 0), stop=(k == 3)), ["wg", f"y1b{b}"], [f"pz{mo}"])
                for mo in range(4):
                    zb = mo % 2
                    em.op("scalar", lambda e, mo=mo, zb=zb: e.activation(out=sg[zb][:], in_=pz[mo][:], func=AF.Sigmoid, bias=dv[:, 4 + mo:5 + mo]),
                          [f"pz{mo}", "dv"], [f"sg{zb}"])
                    em.op("vector", lambda e, mo=mo, zb=zb: e.tensor_tensor(out=y2[b][:, mo, :], in0=y1[b][:, mo, :], in1=sg[zb][:], op=ALU.mult),
                          [f"y1{b}", f"sg{zb}"], [f"y2{b}_{mo}"])
                    em.op("gpsimd", lambda e, mo=mo, zb=zb: e.tensor_tensor(out=sqd[zb][:], in0=y2[b][:, mo, :], in1=y2[b][:, mo, :], op=ALU.mult),
                          [f"y2{b}_{mo}"], [f"sqd{zb}"])
                    em.op("tensor", lambda e, mo=mo, zb=zb: e.matmul(pq2[b][:], lhsT=onesf[:], rhs=sqd[zb][:], start=(mo == 0), stop=(mo == 3)),
                          ["onesf", f"sqd{zb}"], [f"pq{b}"])

            def d_stage2(blk):
                b = blk % 2
                em.op("vector", lambda e: e.tensor_scalar(out=rsd[:], in0=pq2[b][:], scalar1=1.0 / 512, scalar2=1e-6, op0=ALU.mult, op1=ALU.add), [f"pq{b}"], ["rsd"])
                em.op("scalar", lambda e: e.activation(out=rsd2[:], in_=rsd[:], func=AF.Sqrt), ["rsd"], ["rsd2"])
                em.op("vector", lambda e: e.reciprocal(out=rsd[:], in_=rsd2[:]), ["rsd2"], ["rsd"])
                for mo in range(4):
                    em.op("vector", lambda e, mo=mo: e.scalar_tensor_tensor(out=so[b][:, mo, :], in0=y2[b][:, mo, :], scalar=dv[:, 8 + mo:9 + mo],
                                                                           in1=rsd[:], op0=ALU.mult, op1=ALU.mult),
                          [f"y2{b}_{mo}", "rsd", "dv"], [f"so{b}"])
                em.dma("gpsimd", s_r[:, :, blk * 512:(blk + 1) * 512], so[b][:], [f"so{b}"], cwrites=["s_s"])

            pq2 = [pq_, ps("d_pq1", [128, 512])]
            d_stage1(0)
            for blk in range(NB):
                if blk + 1 < NB:
                    d_stage1(blk + 1)
                d_stage2(blk)
            em.barrier()

    def phE():
        with ExitStack() as st:
            sb = lambda n, s, dt=F32: st.enter_context(nc.sbuf_tensor(n, list(s), dt))
            ps = lambda n, s, dt=F32: st.enter_context(nc.psum_tensor(n, list(s), dt))
            identf = sb("e_identf", [128, 128])
            identb = sb("e_identb", [128, 128], BF16)
            em.dma("sync", identf[:], consts[:, ID0:ID0 + 128], [], ["identf"])
            em.op("vector", lambda e: e.tensor_copy(out=identb[:], in_=identf[:]), ["identf"], ["identb"])
            wo = sb("e_wo", [128, 8, D], BF16)
            em.dma("gpsimd", wo[:], w_out.rearrange("(k p) n -> p k n", p=128), [], ["wo"])
            fl = sb("e_fl", [128, 1])
            em.dma("sync", fl[:], flag[:, :], [], ["fl"])
            zt = sb("e_zt", [128, 8, 1], BF16)
            em.op("vector", lambda e: e.memset(zt[:], 0.0), [], ["zt"])
            h_r = h_s.rearrange("k p t -> p k t")
            em.dma("gpsimd", h_r[:, :, 0:1], zt[:], ["zt"], cwrites=["h_s"], allow_slow_non_contiguous=True)
            em.dma("gpsimd", h_r[:, :, 2 * SEGP - 1:2 * SEGP], zt[:], ["zt"], cwrites=["h_s"], allow_slow_non_contiguous=True)
            cat = [sb(f"e_cat{i}", [128, 8, 512], BF16) for i in range(2)]
            xt = [sb(f"e_xt{i}", [128, D]) for i in range(3)]
            x1 = [sb(f"e_x1{i}", [128, D]) for i in range(3)]
            junk = sb("e_junk", [128, D], BF16)
            s4 = [sb(f"e_s4{i}", [128, 4]) for i in range(2)]
            hn = [sb(f"e_hn{i}", [128, D], BF16) for i in range(2)]
            hT = [sb(f"e_hT{i}", [128, 8, 512], BF16) for i in range(2)]
            halo = [sb(f"e_halo{i}", [128, 8, 1], BF16) for i in range(2)]
            po = [ps(f"e_po{i}", [128, 512]) for i in range(4)]
            tp = [ps(f"e_tp{i}", [128, D], BF16) for i in range(2)]
            a_r = a_s.rearrange("(k p) t -> p k t", p=128)
            s_r = s_s.rearrange("(k p) t -> p k t", p=128)
            pcc = [0]

            def e_stage1(t):
                blk, i = t // 4, t % 4
                cbf = blk % 2
                a, n2 = t % 3, t % 2
                if i == 0:
                    em.dma("sync", cat[cbf][:, 0:4, :], a_r[:, :, blk * 512:(blk + 1) * 512], ["a_s"], [f"cat{cbf}a"])
                    em.dma("sync", cat[cbf][:, 4:8, :], s_r[:, :, blk * 512:(blk + 1) * 512], ["s_s"], [f"cat{cbf}s"])
                em.dma("sync", xt[a][:], x[t * 128:(t + 1) * 128, :], [], [f"xt{a}"])
                for half in range(2):
                    pb = pcc[0] % 4
                    pcc[0] += 1
                    for k in range(8):
                        em.op("tensor", lambda e, k=k, half=half, pb=pb: e.matmul(
                            po[pb][:], lhsT=cat[cbf][:, k, i * 128:(i + 1) * 128], rhs=wo[:, k, half * 512:(half + 1) * 512],
                            start=(k == 0), stop=(k == 7)), [f"cat{cbf}a", f"cat{cbf}s", "wo"], [f"po{pb}"])
                    em.op("vector", lambda e, half=half, pb=pb: e.tensor_tensor(
                        out=x1[a][:, half * 512:(half + 1) * 512], in0=po[pb][:], in1=xt[a][:, half * 512:(half + 1) * 512], op=ALU.add),
                          [f"po{pb}", f"xt{a}"], [f"x1{a}_{half}"])
                em.dma("gpsimd", x1_s[t * 128:(t + 1) * 128, :], x1[a][:], [f"x1{a}_0", f"x1{a}_1"], cwrites=["x1_s"])
                em.op("scalar", lambda e: e.activation(out=junk[:], in_=x1[a][:], func=AF.Square, accum_out=s4[n2][:, 0:1]),
                      [f"x1{a}_0", f"x1{a}_1"], ["junk", f"s4{n2}_0"])
                rstd_chain(s4[n2], f"s4{n2}", 1.0 / D, 1e-6)
                em.op("scalar", lambda e: e.activation(out=hn[n2][:], in_=x1[a][:], func=AF.Copy, scale=s4[n2][:, 3:4]),
                      [f"x1{a}_0", f"x1{a}_1", f"s4{n2}_3"], [f"hn{n2}"])

            def e_stage2(t):
                blk, i = t // 4, t % 4
                cbf = blk % 2
                n2 = t % 2
                for k in range(8):
                    em.op("tensor", lambda e, k=k: e.transpose(out=tp[n2][:, k * 128:(k + 1) * 128], in_=hn[n2][:, k * 128:(k + 1) * 128],
                                                               identity=identb[:]), [f"hn{n2}", "identb"], [f"tp{n2}"])
                em.op("vector", lambda e: e.tensor_copy(out=hT[cbf][:, :, i * 128:(i + 1) * 128],
                                                        in_=tp[n2][:].rearrange("p (k j) -> p k j", k=8)),
                      [f"tp{n2}"], [f"hT{cbf}"])
                if i != 3:
                    return
                seg = (blk * 512) // SEG
                off = seg * SEGP + 1 + (blk * 512 - seg * SEG)
                em.dma("gpsimd", h_r[:, :, off:off + 512], hT[cbf][:], [f"hT{cbf}"], cwrites=["h_s"])
                if (blk + 1) * 512 == SEG:
                    em.op("vector", lambda e: e.tensor_scalar(out=halo[0][:], in0=hT[cbf][:, :, 511:512], scalar1=fl[:, 0:1], scalar2=None,
                                                              op0=ALU.mult), [f"hT{cbf}", "fl"], ["halo0"])
                    em.dma("gpsimd", h_r[:, :, SEGP:SEGP + 1], halo[0][:], ["halo0"], cwrites=["h_s"], allow_slow_non_contiguous=True)
                if blk * 512 == SEG:
                    em.op("vector", lambda e: e.tensor_scalar(out=halo[1][:], in0=hT[cbf][:, :, 0:1], scalar1=fl[:, 0:1], scalar2=None,
                                                              op0=ALU.mult), [f"hT{cbf}", "fl"], ["halo1"])
                    em.dma("gpsimd", h_r[:, :, SEGP - 1:SEGP], halo[1][:], ["halo1"], cwrites=["h_s"], allow_slow_non_contiguous=True)

            e_stage1(0)
            for t in range(NT):
                if t + 1 < NT:
                    e_stage1(t + 1)
                e_stage2(t)
            em.barrier()

    def phF():
        with ExitStack() as st:
            sb = lambda n, s, dt=F32: st.enter_context(nc.sbuf_tensor(n, list(s), dt))
            ps = lambda n, s, dt=F32: st.enter_context(nc.psum_tensor(n, list(s), dt))
            gf = sb("f_gf", [128, 8])
            em.dma("sync", gf[:], gffn_pk[:, :], [], ["gf"])
            with ExitStack() as st2:
                sb2 = lambda n, s, dt=F32: st2.enter_context(nc.sbuf_tensor(n, list(s), dt))
                wst = [sb2(f"f_wst{i}", [128, 2 * DFF]) for i in range(2)]
                wsb = [sb2(f"f_wsb{i}", [128, 2 * DFF], BF16) for i in range(2)]
                wus_r = wus.rearrange("m p k c -> p k m c")
                for k in range(8):
                    b = k % 2
                    em.dma("sync", wst[b][:], w_up[k * 128:(k + 1) * 128, :], [], [f"wst{b}"])
                    if b == 0:
                        em.op("vector", lambda e, k=k, b=b: e.tensor_scalar(out=wsb[b][:], in0=wst[b][:], scalar1=gf[:, k:k + 1], scalar2=None, op0=ALU.mult),
                              [f"wst{b}", "gf"], [f"wsb{b}"])
                    else:
                        em.op("scalar", lambda e, k=k, b=b: e.activation(out=wsb[b][:], in_=wst[b][:], func=AF.Copy, scale=gf[:, k:k + 1]),
                              [f"wst{b}", "gf"], [f"wsb{b}"])
                    em.dma("gpsimd", wus_r[:, k, :, :], wsb[b][:].rearrange("p (m c) -> p m c", c=128), [f"wsb{b}"], cwrites=["wus"])
                em.barrier()
            wd = sb("f_wd", [128, 22, D], BF16)
            em.dma("gpsimd", wd[:], w_down.rearrange("(m p) n -> p m n", p=128), [], ["wd"])
            cw = sb("f_cw", [128, 44, 3])
            cbv = sb("f_cb", [128, 44])
            em.dma("sync", cw[:], cw_pk[:, :, :], [], ["cw"])
            em.dma("sync", cbv[:], cb_pk[:, :], [], ["cbv"])
            gfin_t = sb("f_gfin", [128, D])
            em.dma("sync", gfin_t[:], gfin[0:1, :].to_broadcast([128, D]), [], ["gfin"])
            hw = [sb(f"f_hw{i}", [128, 8, 512], BF16) for i in range(3)]
            wt_ = [sb(f"f_wt{i}", [128, 8, 128], BF16) for i in range(4)]
            zc = [sb(f"f_zc{i}", [128, 512]) for i in range(4)]
            ga = [sb(f"f_ga{i}", [128, 512]) for i in range(2)]
            act = [sb(f"f_act{i}", [128, 22, 512], BF16) for i in range(2)]
            x1t = [sb(f"f_x1{i}", [128, D]) for i in range(8)]
            x2t = [sb(f"f_x2{i}", [128, D]) for i in range(2)]
            yt = [sb(f"f_yt{i}", [128, D]) for i in range(2)]
            junk = sb("f_junk", [128, D], BF16)
            s4 = [sb(f"f_s4{i}", [128, 4]) for i in range(2)]
            pu = [ps(f"f_pu{i}", [128, 512]) for i in range(3)]
            pd = [ps(f"f_pd{i}", [128, 512]) for i in range(4)]
            h_r = h_s.rearrange("k p t -> p k t")
            cnts = {"uc": 0, "dc": 0, "tc": 0}
            wins = []
            for seg in range(2):
                b = 0
                while 510 * b < SEG:
                    W_ = min(512, SEGP - 510 * b)
                    wins.append(dict(seg=seg, b=b, W=W_, nv=W_ - 2, c0=seg * SEGP + 510 * b, tok0=seg * SEG + 510 * b, idx=len(wins)))
                    b += 1

            def f_load(w):
                hb = w["idx"] % 3
                em.dma("sync", hw[hb][:, :, 0:w["W"]], h_r[:, :, w["c0"]:w["c0"] + w["W"]], ["h_s"], [f"hw{hb}"])

            def f_loadx(w):
                w["xb"] = []
                ntt = (w["nv"] + 127) // 128
                for tt in range(ntt):
                    r0 = tt * 128
                    nr = min(128, w["nv"] - r0)
                    xb_ = cnts["tc"] % 8
                    cnts["tc"] += 1
                    w["xb"].append(xb_)
                    em.dma("sync", x1t[xb_][0:nr, :], x1_s[w["tok0"] + r0:w["tok0"] + r0 + nr, :], ["x1_s"], [f"x1t{xb_}"])

            def f_up(w, mm0, mm1):
                hb, ab = w["idx"] % 3, w["idx"] % 2
                W_, nv = w["W"], w["nv"]
                for mm in range(mm0, mm1):
                    m = (mm // 2) + (22 if mm % 2 else 0)
                    uc = cnts["uc"]
                    cnts["uc"] += 1
                    w4, p3, z4 = uc % 4, uc % 3, uc % 4
                    em.dma("sync", wt_[w4][:], wus[m, :, :, :], ["wus"], [f"wt{w4}"])
                    for k in range(8):
                        em.op("tensor", lambda e, k=k, w4=w4, p3=p3: e.matmul(
                            pu[p3][:, 0:W_], lhsT=wt_[w4][:, k, :], rhs=hw[hb][:, k, 0:W_], start=(k == 0), stop=(k == 7)),
                              [f"wt{w4}", f"hw{hb}"], [f"pu{p3}"])
                    em.op("scalar", lambda e, m=m, p3=p3, z4=z4: e.activation(
                        out=zc[z4][:, 0:nv], in_=pu[p3][:, 1:1 + nv], func=AF.Identity, scale=cw[:, m, 1:2], bias=cbv[:, m:m + 1]),
                          [f"pu{p3}", "cw", "cbv"], [f"zc{z4}"])
                    em.op("vector", lambda e, m=m, p3=p3, z4=z4: e.scalar_tensor_tensor(
                        out=zc[z4][:, 0:nv], in0=pu[p3][:, 0:nv], scalar=cw[:, m, 0:1], in1=zc[z4][:, 0:nv], op0=ALU.mult, op1=ALU.add),
                          [f"pu{p3}", "cw", f"zc{z4}"], [f"zc{z4}"])
                    em.op("vector", lambda e, m=m, p3=p3, z4=z4: e.scalar_tensor_tensor(
                        out=zc[z4][:, 0:nv], in0=pu[p3][:, 2:2 + nv], scalar=cw[:, m, 2:3], in1=zc[z4][:, 0:nv], op0=ALU.mult, op1=ALU.add),
                          [f"pu{p3}", "cw", f"zc{z4}"], [f"zc{z4}"])
                    g2_ = (mm // 2) % 2
                    if mm % 2 == 0:
                        em.op("scalar", lambda e, z4=z4, g2_=g2_: e.activation(out=ga[g2_][:, 0:nv], in_=zc[z4][:, 0:nv], func=AF.Gelu_apprx_tanh),
                              [f"zc{z4}"], [f"ga{g2_}"])
                    else:
                        mg = mm // 2
                        em.op("gpsimd", lambda e, z4=z4, g2_=g2_, mg=mg: e.tensor_tensor(
                            out=act[ab][:, mg, 0:nv], in0=ga[g2_][:, 0:nv], in1=zc[z4][:, 0:nv], op=ALU.mult),
                              [f"ga{g2_}", f"zc{z4}"], [f"act{ab}"])

            def f_down(w):
                ab = w["idx"] % 2
                nv, tok0 = w["nv"], w["tok0"]
                ntt = (nv + 127) // 128
                for tt in range(ntt):
                    r0 = tt * 128
                    nr = min(128, nv - r0)
                    xb_ = w["xb"][tt]
                    ob = xb_ % 2
                    for half in range(2):
                        p4 = cnts["dc"] % 4
                        cnts["dc"] += 1
                        for m in range(22):
                            em.op("tensor", lambda e, m=m, half=half, p4=p4, r0=r0, nr=nr: e.matmul(
                                pd[p4][0:nr, :], lhsT=act[ab][:, m, r0:r0 + nr], rhs=wd[:, m, half * 512:(half + 1) * 512],
                                start=(m == 0), stop=(m == 21)), [f"act{ab}", "wd"], [f"pd{p4}"])
                        em.op("vector", lambda e, half=half, p4=p4, nr=nr, xb_=xb_, ob=ob: e.tensor_tensor(
                            out=x2t[ob][0:nr, half * 512:(half + 1) * 512], in0=pd[p4][0:nr, :], in1=x1t[xb_][0:nr, half * 512:(half + 1) * 512], op=ALU.add),
                              [f"pd{p4}", f"x1t{xb_}"], [f"x2t{ob}_{half}"])
                    em.op("scalar", lambda e, nr=nr, ob=ob: e.activation(out=junk[0:nr, :], in_=x2t[ob][0:nr, :], func=AF.Square, accum_out=s4[ob][0:nr, 0:1]),
                          [f"x2t{ob}_0", f"x2t{ob}_1"], ["junk", f"s4{ob}_0"])
                    rstd_chain(s4[ob], f"s4{ob}", 1.0 / D, 1e-6)
                    em.op("vector", lambda e, nr=nr, ob=ob: e.scalar_tensor_tensor(
                        out=yt[ob][0:nr, :], in0=x2t[ob][0:nr, :], scalar=s4[ob][0:nr, 3:4], in1=gfin_t[0:nr, :], op0=ALU.mult, op1=ALU.mult),
                          [f"x2t{ob}_0", f"x2t{ob}_1", f"s4{ob}_3", "gfin"], [f"yt{ob}"])
                    em.dma("gpsimd", y[tok0 + r0:tok0 + r0 + nr, :], yt[ob][0:nr, :], [f"yt{ob}"], cwrites=["y"])

            NPRE = 8
            f_load(wins[0])
            if len(wins) > 1:
                f_load(wins[1])
            f_loadx(wins[0])
            f_up(wins[0], 0, 44)
            for wi_, w in enumerate(wins):
                nxt = wins[wi_ + 1] if wi_ + 1 < len(wins) else None
                nn = wins[wi_ + 2] if wi_ + 2 < len(wins) else None
                if nxt is not None:
                    f_up(nxt, 0, NPRE)
                if nn is not None:
                    f_load(nn)
                if nxt is not None:
                    f_loadx(nxt)
                f_down(w)
                if nxt is not None:
                    f_up(nxt, NPRE, 44)
            em.barrier()

    if "A" in phases:
        phA()
    if "B" in phases:
        phB()
    if "C" in phases:
        phC(0)
        phC(1)
    if "D" in phases:
        phD()
    if "E" in phases:
        phE()
    if "F" in phases:
        phF()
    em.finish()
    top.close()
    return nc


def make_consts():
    c = np.zeros((128, NCON), np.float32)
    p = np.arange(128, dtype=np.float32)[:, None]
    j = np.arange(128, dtype=np.float32)[None, :]
    c[:, ID0:ID0 + 128] = np.eye(128, dtype=np.float32)
    c[:, TL0:TL0 + 128] = (p <= j).astype(np.float32)
    c[:, TU0:TU0 + 128] = (p >= j).astype(np.float32)
    qf = np.arange(512, dtype=np.float32)[None, :]
    c[:, RP0:RP0 + 512] = qf - p
    for m in range(4):
        c[:, AR0 + 512 * m:AR0 + 512 * (m + 1)] = np.abs(qf - p - 128.0 * m)
    c[:, JF0:JF0 + 128] = j + 1.0
    c[:, JB0:JB0 + 128] = 128.0 - j
    c[:, QF0:QF0 + 512] = qf
    c[:, QR0:QR0 + 512] = 511.0 - qf
    c[:, JC0] = -(p[:, 0] + 1.0)
    c[:, JC0 + 1] = -(128.0 - p[:, 0])
    return c


def make_cbias(T, cross_val):
    NT, NB, SEG = T // 128, T // 512, T // 2
    cb = np.zeros((4, NB, NT), np.float32)
    for h in range(4):
        for Qb in range(NB):
            for kt in range(NT):
                rel = kt - 4 * Qb
                if 0 <= rel < 4:
                    v = 0.0
                elif rel < 0:
                    v = -SLOPES[h] * (512 * Qb - 128 * kt)
                else:
                    v = -SLOPES[h] * (128 * kt - 512 * Qb)
                if (512 * Qb) // SEG != (128 * kt) // SEG:
                    v += cross_val
                cb[h, Qb, kt] = v
    return np.ascontiguousarray(np.broadcast_to(cb.reshape(1, -1), (128, 4 * NB * NT))).astype(np.float32)


def host_weights(inp):
    f = lambda a: np.ascontiguousarray(np.asarray(a, dtype=np.float32))
    w = {}
    w["w_in"] = f(inp["w_in"][0])
    w["w_out"] = f(inp["w_out"][0])
    w["w_up"] = f(inp["w_up"][0])
    w["w_down"] = f(inp["w_down"][0])
    w["w_glu"] = f(inp["w_glu"][0])
    w["gmix_pk"] = f(np.asarray(inp["g_mix_norm"][0]).reshape(8, 128).T)
    w["gffn_pk"] = f(np.asarray(inp["g_ffn_norm"][0]).reshape(8, 128).T)
    w["gfin"] = f(np.asarray(inp["g_final"]).reshape(1, D))
    w["gsub"] = f(np.asarray(inp["g_subln"][0]).reshape(128, 1))
    w["lamv"] = f(np.concatenate([np.asarray(inp[k][0]) for k in ("lambda_q1", "lambda_k1", "lambda_q2", "lambda_k2")]).reshape(1, 256))
    w["cw_pk"] = f(np.asarray(inp["conv_w"][0]).reshape(3, 44, 128).transpose(2, 1, 0))
    w["cb_pk"] = f(np.asarray(inp["conv_b"][0]).reshape(44, 128).T)
    sv = np.zeros((128, 12), np.float32)
    sv[:, 0:4] = np.asarray(inp["ssm_d"][0]).reshape(4, 128).T
    sv[:, 4:8] = np.asarray(inp["b_glu"][0]).reshape(4, 128).T
    sv[:, 8:12] = np.asarray(inp["g_ssm_out"][0]).reshape(4, 128).T
    w["ssmv"] = sv
    a_re = np.asarray(inp["ssm_a_re"][0], np.float32)
    a_im = np.asarray(inp["ssm_a_im"][0], np.float32)
    ls = np.asarray(inp["ssm_log_step"][0], np.float32)
    lsx = np.repeat(ls[:, :, None], 64, axis=2)
    a_row = np.stack([a_re.reshape(2, 2048), a_im.reshape(2, 2048), lsx.reshape(2, 2048)], axis=1)
    w["a_row"] = f(a_row)
    a_sm = np.zeros((2, 128, 48), np.float32)
    for d in range(2):
        a_sm[d, :, 0:16] = a_re[d].reshape(16, 128).T
        a_sm[d, :, 16:32] = a_im[d].reshape(16, 128).T
        a_sm[d, :, 32:48] = lsx[d].reshape(16, 128).T
    w["a_sm"] = a_sm
    b_re = np.asarray(inp["ssm_b_re"][0], np.float32)
    b_im = np.asarray(inp["ssm_b_im"][0], np.float32)
    c_re = np.asarray(inp["ssm_c_re"][0], np.float32)
    c_im = np.asarray(inp["ssm_c_im"][0], np.float32)
    bmA = np.zeros((2, 128, 4, 2, 512), np.float32)
    bmB = np.zeros((2, 128, 4, 2, 512), np.float32)
    cm = np.zeros((2, 128, 8, 4, 128), np.float32)
    for d in range(2):
        for g in range(32):
            kq, gg = g // 8, g % 8
            pq, gl = gg // 4, gg % 4
            rows = slice(gg * 16, gg * 16 + 16)
            bmA[d, rows, kq, pq, gl * 64:gl * 64 + 64] = b_re[d, g].T
            bmA[d, rows, kq, pq, 256 + gl * 64:256 + gl * 64 + 64] = b_im[d, g].T
            bmB[d, rows, kq, pq, gl * 64:gl * 64 + 64] = b_im[d, g].T
            bmB[d, rows, kq, pq, 256 + gl * 64:256 + gl * 64 + 64] = b_re[d, g].T
            p = g // 4
            half, glh = gl // 2, gl % 2
            srows = slice(glh * 64, glh * 64 + 64)
            chc = slice((g % 8) * 16, (g % 8) * 16 + 16)
            cm[d, srows, p, half, chc] = c_re[d, g].T
            cm[d, srows, p, 2 + half, chc] = c_im[d, g].T
    w["bmA"] = bmA.reshape(2, 128, 4096)
    w["bmB"] = bmB.reshape(2, 128, 4096)
    w["cmat"] = cm.reshape(2, 128, 4096)
    w["consts"] = make_consts()
    return w


_NC_CACHE = {}


def kernel(**inputs):
    T = 8192
    w = host_weights(inputs)
    xp = np.asarray(inputs["x_prompt"], np.float32)
    xs = np.asarray(inputs["x_sample"], np.float32)
    cb_p = make_cbias(T, 0.0)
    cb_s = make_cbias(T, -30000.0)
    in_maps = []
    for c in range(8):
        m = dict(w)
        if c < 4:
            m["x"] = np.ascontiguousarray(xp[c])
            m["cbias"] = cb_p
            m["flag"] = np.ones((128, 1), np.float32)
        else:
            m["x"] = np.ascontiguousarray(xs[2 * (c - 4):2 * (c - 4) + 2].reshape(T, D))
            m["cbias"] = cb_s
            m["flag"] = np.zeros((128, 1), np.float32)
        in_maps.append(m)
    if T not in _NC_CACHE:
        _NC_CACHE[T] = build(T)
    res = run_bass_kernel_spmd(_NC_CACHE[T], in_maps, core_ids=list(range(8)))
    ys = [np.asarray(r["y"], np.float32) for r in res.results]
    y_prompt = np.stack(ys[0:4], axis=0)
    y_sample = np.concatenate([ys[c].reshape(2, T // 2, D) for c in range(4, 8)], axis=0)
    return (y_prompt, y_sample)
```

```python
import math
from contextlib import ExitStack
import numpy as np
import ml_dtypes
import concourse.bass as bass
import concourse.mybir as mybir
from concourse.bass_utils import run_bass_kernel_spmd

F32 = mybir.dt.float32
BF16 = mybir.dt.bfloat16
I32 = mybir.dt.int32
AF = mybir.ActivationFunctionType
ALU = mybir.AluOpType
D = 1024
DFF = 2816
SLOPES = [2.0 ** (-2 * (h + 1)) for h in range(4)]
ID0, TL0, TU0, RP0, AR0, JF0, JB0, JC0, QF0, QR0, NCON = 0, 128, 256, 384, 896, 2944, 3072, 3200, 3208, 3720, 4232
TWO_PI = 2.0 * math.pi


class Em:
    def __init__(self, nc, stack):
        self.nc = nc
        self.streams = {k: [] for k in ("sync", "scalar", "vector", "gpsimd", "tensor")}
        self.csem = {e: stack.enter_context(nc.semaphore("c_" + e)) for e in ("scalar", "vector", "gpsimd", "tensor")}
        self.ccnt = {e: 0 for e in self.csem}
        self.NQ = 10
        self.dsem, self.dcnt, self.drr = {}, {}, {}
        for q in ("sync", "gpsimd", "scalar"):
            self.dsem[q] = [stack.enter_context(nc.semaphore(f"d_{q}_{i}")) for i in range(self.NQ)]
            self.dcnt[q] = [0] * self.NQ
            self.drr[q] = 0
        self.waited = {e: {} for e in self.streams}
        self.bw, self.br = {}, {}
        self.nins = 0

    def semh(self, key):
        return self.csem[key] if isinstance(key, str) else self.dsem[key[1]][key[2]]

    def _deps(self, eng, reads, writes):
        deps = {}

        def add(k, v):
            if deps.get(k, 0) < v:
                deps[k] = v

        for b in reads:
            for k, v in self.bw.get(b, {}).items():
                add(k, v)
        for b in writes:
            for k, v in self.bw.get(b, {}).items():
                if k != eng:
                    add(k, v)
            for k, v in self.br.get(b, {}).items():
                if k != eng:
                    add(k, v)
        return deps

    def _waits(self, eng, deps):
        waits = []
        for k, v in deps.items():
            if self.waited[eng].get(k, 0) >= v:
                continue
            self.waited[eng][k] = v
            waits.append((self.semh(k), v))
        return waits

    def _record(self, key, val, reads, writes, cwrites):
        for b in writes:
            self.bw[b] = {key: val}
            self.br[b] = {}
        for b in cwrites:
            d = self.bw.setdefault(b, {})
            d[key] = max(d.get(key, 0), val)
        for b in reads:
            d = self.br.setdefault(b, {})
            d[key] = max(d.get(key, 0), val)

    def op(self, eng, fn, reads=(), writes=()):
        waits = self._waits(eng, self._deps(eng, reads, writes))
        self.ccnt[eng] += 1
        n = self.ccnt[eng]
        sem = self.csem[eng]

        def run(e, waits=waits, fn=fn, sem=sem):
            for s, v in waits:
                e.wait_ge(s, v)
            fn(e).then_inc(sem, 1)

        self.streams[eng].append(run)
        self._record(eng, n, reads, writes, ())
        self.nins += 1

    def dma(self, q, out, in_, reads=(), writes=(), cwrites=(), **kw):
        slot = self.drr[q]
        self.drr[q] = (slot + 1) % self.NQ
        key = ("dma", q, slot)
        deps = self._deps(q, reads, writes)
        prev = self.dcnt[q][slot]
        if prev > 0:
            deps[key] = max(deps.get(key, 0), prev)
        waits = self._waits(q, deps)
        self.dcnt[q][slot] += 16
        v = self.dcnt[q][slot]
        sem = self.dsem[q][slot]

        def run(e, waits=waits, sem=sem, out=out, in_=in_, kw=kw):
            for s, vv in waits:
                e.wait_ge(s, vv)
            e.dma_start(out=out, in_=in_, **kw).then_inc(sem, 16)

        self.streams[q].append(run)
        self._record(key, v, reads, writes, cwrites)
        self.nins += 1

    def barrier(self):
        allk = {e: n for e, n in self.ccnt.items() if n > 0}
        for q in self.dsem:
            for i in range(self.NQ):
                if self.dcnt[q][i] > 0:
                    allk[("dma", q, i)] = self.dcnt[q][i]
        for eng in self.streams:
            waits = self._waits(eng, {k: v for k, v in allk.items() if k != eng})

            def run(e, waits=waits):
                for s, v in waits:
                    e.wait_ge(s, v)

            self.streams[eng].append(run)

    def finish(self):
        self.barrier()
        nc = self.nc
        with nc.Block() as block:
            @block.sync
            def _(e):
                for f in self.streams["sync"]:
                    f(e)

            @block.scalar
            def _(e):
                for f in self.streams["scalar"]:
                    f(e)

            @block.vector
            def _(e):
                for f in self.streams["vector"]:
                    f(e)

            @block.gpsimd
            def _(e):
                for f in self.streams["gpsimd"]:
                    f(e)

            @block.tensor
            def _(e):
                for f in self.streams["tensor"]:
                    f(e)


def build(T, dbg=False, phases="ABCDEF"):
    NT, NB, SEG = T // 128, T // 512, T // 2
    SEGP = SEG + 2
    nc = bass.Bass("TRN2", target_bir_lowering=False)

    def din(name, shape, dt=F32):
        return nc.dram_tensor(name, list(shape), dt, kind="ExternalInput").ap()

    skind = "ExternalOutput" if dbg else "Internal"

    def dscr(name, shape, dt):
        return nc.dram_tensor(name, list(shape), dt, kind=skind).ap()

    x = din("x", [T, D])
    w_in = din("w_in", [D, 2048])
    w_out = din("w_out", [D, D])
    w_up = din("w_up", [D, 2 * DFF])
    w_down = din("w_down", [DFF, D])
    w_glu = din("w_glu", [512, 512])
    gmix_pk = din("gmix_pk", [128, 8])
    gffn_pk = din("gffn_pk", [128, 8])
    gfin = din("gfin", [1, D])
    gsub = din("gsub", [128, 1])
    lamv = din("lamv", [1, 256])
    cw_pk = din("cw_pk", [128, 44, 3])
    cb_pk = din("cb_pk", [128, 44])
    ssmv = din("ssmv", [128, 12])
    a_row = din("a_row", [2, 3, 2048])
    a_sm = din("a_sm", [2, 128, 48])
    bmA = din("bmA", [2, 128, 4096])
    bmB = din("bmB", [2, 128, 4096])
    cmat = din("cmat", [2, 128, 4096])
    consts = din("consts", [128, NCON])
    cbias = din("cbias", [128, 4 * NB * NT])
    flag = din("flag", [128, 1])
    y = nc.dram_tensor("y", [T, D], F32, kind="ExternalOutput").ap()

    q_s = dscr("q_s", [512, T], BF16)
    k_s = dscr("k_s", [512, T], BF16)
    u_s = dscr("u_s", [512, T], BF16)
    v_s = dscr("v_s", [T, 512], BF16)
    a_s = dscr("a_s", [512, T], BF16)
    s_s = dscr("s_s", [512, T], BF16)
    yf_s = dscr("yf_s", [512, T], F32)
    ys_s = dscr("ys_s", [512, T], F32)
    x1_s = dscr("x1_s", [T, D], F32)
    h_s = dscr("h_s", [8, 128, 2 * SEGP], BF16)
    wus = dscr("wus", [44, 128, 8, 128], BF16)

    top = ExitStack()
    em = Em(nc, top)

    def rstd_chain(st4, key, scale, eps):
        em.op("vector", lambda e: e.tensor_scalar(out=st4[:, 1:2], in0=st4[:, 0:1], scalar1=scale, scalar2=eps,
                                                  op0=ALU.mult, op1=ALU.add), [key + "_0"], [key + "_1"])
        em.op("scalar", lambda e: e.activation(out=st4[:, 2:3], in_=st4[:, 1:2], func=AF.Sqrt), [key + "_1"], [key + "_2"])
        em.op("vector", lambda e: e.reciprocal(out=st4[:, 3:4], in_=st4[:, 2:3]), [key + "_2"], [key + "_3"])

    def phA():
        with ExitStack() as st:
            sb = lambda n, s, dt=F32: st.enter_context(nc.sbuf_tensor(n, list(s), dt))
            ps = lambda n, s, dt=F32: st.enter_context(nc.psum_tensor(n, list(s), dt))
            identf = sb("a_identf", [128, 128])
            identb = sb("a_identb", [128, 128], BF16)
            em.dma("sync", identf[:], consts[:, ID0:ID0 + 128], [], ["identf"])
            em.op("vector", lambda e: e.tensor_copy(out=identb[:], in_=identf[:]), ["identf"], ["identb"])
            gm = sb("a_gm", [128, 8])
            em.dma("sync", gm[:], gmix_pk[:, :], [], ["gm"])
            wbf = sb("a_wbf", [128, 8, 2048], BF16)
            wst = [sb(f"a_wst{i}", [128, 2048]) for i in range(2)]
            for k in range(8):
                b = k % 2
                em.dma("sync", wst[b][:], w_in[k * 128:(k + 1) * 128, :], [], [f"wst{b}"])
                if b == 0:
                    em.op("vector", lambda e, k=k, b=b: e.tensor_scalar(out=wbf[:, k, :], in0=wst[b][:], scalar1=gm[:, k:k + 1],
                                                                        scalar2=None, op0=ALU.mult), [f"wst{b}", "gm"], [f"wbf{k}"])
                else:
                    em.op("scalar", lambda e, k=k, b=b: e.activation(out=wbf[:, k, :], in_=wst[b][:], func=AF.Copy,
                                                                     scale=gm[:, k:k + 1]), [f"wst{b}", "gm"], [f"wbf{k}"])
            xt = [sb(f"a_xt{i}", [128, D]) for i in range(3)]
            junk = sb("a_junk", [128, D], BF16)
            xn = [sb(f"a_xn{i}", [128, D], BF16) for i in range(2)]
            s4 = [sb(f"a_s4{i}", [128, 4]) for i in range(2)]
            xT = [sb(f"a_xT{i}", [128, 8, 512], BF16) for i in range(3)]
            tp = [ps(f"a_tp{i}", [128, D], BF16) for i in range(2)]
            pp = [ps(f"a_pp{i}", [128, 512]) for i in range(4)]
            stg = [sb(f"a_stg{i}", [128, 512], BF16) for i in range(4)]
            cntb = [0]
            wk = [f"wbf{k}" for k in range(8)]

            def a_stage1(blk):
                xb = blk % 3
                for i in range(4):
                    t = blk * 4 + i
                    a, n2 = t % 3, t % 2
                    em.dma("sync", xt[a][:], x[t * 128:(t + 1) * 128, :], [], [f"xt{a}"])
                    em.op("scalar", lambda e, a=a, n2=n2: e.activation(out=junk[:], in_=xt[a][:], func=AF.Square,
                                                                       accum_out=s4[n2][:, 0:1]), [f"xt{a}"], ["junk", f"s4{n2}_0"])
                    rstd_chain(s4[n2], f"s4{n2}", 1.0 / D, 1e-6)
                    em.op("scalar", lambda e, a=a, n2=n2: e.activation(out=xn[n2][:], in_=xt[a][:], func=AF.Copy,
                                                                       scale=s4[n2][:, 3:4]), [f"xt{a}", f"s4{n2}_3"], [f"xn{n2}"])
                    for k in range(8):
                        em.op("tensor", lambda e, k=k, n2=n2: e.transpose(out=tp[n2][:, k * 128:(k + 1) * 128],
                                                                          in_=xn[n2][:, k * 128:(k + 1) * 128], identity=identb[:]),
                              [f"xn{n2}", "identb"], [f"tp{n2}"])
                    em.op("vector", lambda e, i=i, n2=n2, xb=xb: e.tensor_copy(
                        out=xT[xb][:, :, i * 128:(i + 1) * 128], in_=tp[n2][:].rearrange("p (k j) -> p k j", k=8)),
                          [f"tp{n2}"], [f"xT{xb}"])

            def a_stage2(blk):
                xb = blk % 3
                for m in range(16):
                    pb = cntb[0] % 4
                    sg = cntb[0] % 4
                    cntb[0] += 1
                    if m < 12:
                        col = m * 128 if m < 8 else 1536 + (m - 8) * 128
                        for k in range(8):
                            em.op("tensor", lambda e, k=k, col=col, pb=pb, xb=xb: e.matmul(
                                pp[pb][:], lhsT=wbf[:, k, col:col + 128], rhs=xT[xb][:, k, :], start=(k == 0), stop=(k == 7)),
                                  [wk[k], f"xT{xb}"], [f"pp{pb}"])
                        dst = (q_s if m < 4 else k_s if m < 8 else u_s)[(m % 4) * 128:(m % 4 + 1) * 128, blk * 512:(blk + 1) * 512]
                        dk = "q_s" if m < 4 else "k_s" if m < 8 else "u_s"
                    else:
                        i = m - 12
                        for k in range(8):
                            em.op("tensor", lambda e, k=k, i=i, pb=pb, xb=xb: e.matmul(
                                pp[pb][:], lhsT=xT[xb][:, k, i * 128:(i + 1) * 128], rhs=wbf[:, k, 1024:1536],
                                start=(k == 0), stop=(k == 7)), [wk[k], f"xT{xb}"], [f"pp{pb}"])
                        dst = v_s[(blk * 4 + i) * 128:(blk * 4 + i + 1) * 128, :]
                        dk = "v_s"
                    if m < 4:
                        em.op("scalar", lambda e, pb=pb, sg=sg: e.activation(out=stg[sg][:], in_=pp[pb][:], func=AF.Copy, scale=0.125),
                              [f"pp{pb}"], [f"stg{sg}"])
                    elif m % 2 == 0:
                        em.op("vector", lambda e, pb=pb, sg=sg: e.tensor_copy(out=stg[sg][:], in_=pp[pb][:]), [f"pp{pb}"], [f"stg{sg}"])
                    else:
                        em.op("scalar", lambda e, pb=pb, sg=sg: e.activation(out=stg[sg][:], in_=pp[pb][:], func=AF.Copy),
                              [f"pp{pb}"], [f"stg{sg}"])
                    em.dma("gpsimd", dst, stg[sg][:], [f"stg{sg}"], cwrites=[dk])
            a_stage1(0)
            if NB > 1:
                a_stage1(1)
            for blk in range(NB):
                if blk + 2 < NB:
                    a_stage1(blk + 2)
                a_stage2(blk)
            em.barrier()

    def phB():
        with ExitStack() as st:
            sb = lambda n, s, dt=F32: st.enter_context(nc.sbuf_tensor(n, list(s), dt))
            ps = lambda n, s, dt=F32: st.enter_context(nc.psum_tensor(n, list(s), dt))
            ramp = sb("b_ramp", [128, 512])
            absr = sb("b_absr", [128, 2048])
            em.dma("sync", ramp[:], consts[:, RP0:RP0 + 512], [], ["ramps"])
            em.dma("sync", absr[:], consts[:, AR0:AR0 + 2048], [], ["absr"])
            cb = sb("b_cb", [128, 4 * NB * NT])
            em.dma("sync", cb[:], cbias[:, :], [], ["cb"])
            onesf = sb("b_onesf", [128, 128])
            onesb = sb("b_onesb", [128, 128], BF16)
            em.op("vector", lambda e: e.memset(onesf[:], 1.0), [], ["onesf"])
            em.op("vector", lambda e: e.memset(onesb[:], 1.0), [], ["onesb"])
            lv = sb("b_lv", [128, 256])
            em.dma("sync", lv[:], lamv[0:1, :].to_broadcast([128, 256]), [], ["lv"])
            lp = sb("b_lp", [128, 128])
            l4 = sb("b_l4", [128, 8])
            em.op("vector", lambda e: e.tensor_tensor(out=lp[:, 0:64], in0=lv[:, 0:64], in1=lv[:, 64:128], op=ALU.mult), ["lv"], ["lp0"])
            em.op("vector", lambda e: e.tensor_tensor(out=lp[:, 64:128], in0=lv[:, 128:192], in1=lv[:, 192:256], op=ALU.mult), ["lv"], ["lp1"])
            em.op("scalar", lambda e: e.activation(out=lv[:, 0:64], in_=lp[:, 0:64], func=AF.Copy, accum_out=l4[:, 0:1]), ["lp0"], ["l40", "lvj"])
            em.op("scalar", lambda e: e.activation(out=lv[:, 64:128], in_=lp[:, 64:128], func=AF.Copy, accum_out=l4[:, 1:2]), ["lp1"], ["l41", "lvj2"])
            em.op("scalar", lambda e: e.activation(out=l4[:, 2:4], in_=l4[:, 0:2], func=AF.Exp), ["l40", "l41"], ["l42"])
            em.op("vector", lambda e: e.tensor_tensor(out=l4[:, 4:5], in0=l4[:, 3:4], in1=l4[:, 2:3], op=ALU.subtract), ["l42"], ["l44"])
            em.op("vector", lambda e: e.tensor_scalar(out=l4[:, 5:6], in0=l4[:, 4:5], scalar1=-0.2, scalar2=None, op0=ALU.add), ["l44"], ["neglam"])
            epsb = sb("b_epsb", [128, 1])
            em.op("vector", lambda e: e.memset(epsb[:], 1e-5), [], ["epsb"])
            gs = sb("b_gs", [128, 2])
            em.dma("sync", gs[:, 0:1], gsub[:, :], [], ["gs0"])
            em.op("vector", lambda e: e.tensor_scalar(out=gs[:, 1:2], in0=gs[:, 0:1], scalar1=0.8, scalar2=None, op0=ALU.mult), ["gs0"], ["gs1"])

            KT = [sb(f"b_KT{i}", [128, T], BF16) for i in range(2)]
            QT = [[sb(f"b_QT{i}_{c}", [128, T], BF16) for c in range(2)] for i in range(2)]
            for i in range(2):
                em.op("gpsimd", lambda e, i=i: e.memset(QT[i][0][64:128, :], 0.0), [], [f"QTz{i}0"])
                em.op("gpsimd", lambda e, i=i: e.memset(QT[i][1][0:64, :], 0.0), [], [f"QTz{i}1"])
            VV = [sb(f"b_VV{i}", [128, NT, 128], BF16) for i in range(2)]
            NSC = 4
            scp = [ps(f"b_scp{i}", [128, 512]) for i in range(NSC)]
            Op = [ps(f"b_O{i}", [128, 512]) for i in range(2)]
            Sp = [ps(f"b_S{i}", [128, 512]) for i in range(2)]
            NSB = 4
            sbt = [sb(f"b_sbt{i}", [128, 512]) for i in range(NSB)]
            NPT = 6
            pT = [sb(f"b_pT{i}", [128, 512], BF16) for i in range(NPT)]
            rs = [sb(f"b_rs{i}", [128, 512]) for i in range(2)]
            oc = [sb(f"b_oc{i}", [128, 512]) for i in range(2)]
            wt = sb("b_wt", [128, 512])
            sq = sb("b_sq", [128, 512])
            a1 = sb("b_a1", [128, 512])
            aT = [sb(f"b_aT{i}", [128, 512], BF16) for i in range(2)]
            LA = 3
            THR = 40.0
            v_r = v_s.rearrange("(kt kp) e -> kp kt e", kp=128)
            nq = 0
            sc_box = [0]
            pending = []
            for h in range(4):
                hb = h % 2
                em.dma("sync", KT[hb][:], k_s[h * 128:(h + 1) * 128, :], ["k_s"], [f"KT{hb}"])
                em.dma("sync", QT[hb][0][0:64, :], q_s[h * 128:h * 128 + 64, :], ["q_s", f"QTz{hb}0"], [f"QT{hb}_0"])
                em.dma("sync", QT[hb][1][64:128, :], q_s[h * 128 + 64:(h + 1) * 128, :], ["q_s", f"QTz{hb}1"], [f"QT{hb}_1"])
                VCH = max(1, NT // 4)
                for j in range(0, NT, VCH):
                    em.dma("sync", VV[hb][:, j:j + VCH, :], v_r[:, j:j + VCH, h * 128:(h + 1) * 128], ["v_s"], cwrites=[f"VV{hb}"],
                           writes=([f"VV{hb}"] if j == 0 else []))
                units = []
                for Qb in range(NB):
                    kept = []
                    for kt in range(NT):
                        rel = kt - 4 * Qb
                        if rel < 0:
                            dmin = 512 * Qb - (128 * kt + 127)
                        elif rel >= 4:
                            dmin = 128 * kt - (512 * Qb + 511)
                        else:
                            dmin = 0
                        if SLOPES[h] * dmin <= THR:
                            kept.append(kt)
                    for c in range(2):
                        for ii, kt in enumerate(kept):
                            units.append((Qb, c, kt, ii == 0, ii == len(kept) - 1))
                N = len(units)
                slot = {}
                for i in range(N + LA):
                    if i < N:
                        Qb, c, kt, first, last = units[i]
                        s3 = sc_box[0] % NSC
                        sc_box[0] += 1
                        s4_, p5 = i % NSB, i % NPT
                        slot[i] = p5
                        em.op("tensor", lambda e, c=c, kt=kt, Qb=Qb, s3=s3, hb=hb: e.matmul(
                            scp[s3][:], lhsT=KT[hb][:, kt * 128:(kt + 1) * 128],
                            rhs=QT[hb][c][:, Qb * 512:(Qb + 1) * 512], start=True, stop=True),
                              [f"KT{hb}", f"QT{hb}_{c}"], [f"scp{s3}"])
                        rel = kt - 4 * Qb
                        if 0 <= rel < 4:
                            tab, sc, tk = absr[:, rel * 512:(rel + 1) * 512], -SLOPES[h], "absr"
                        elif rel < 0:
                            tab, sc, tk = ramp[:], -SLOPES[h], "ramps"
                        else:
                            tab, sc, tk = ramp[:], SLOPES[h], "ramps"
                        em.op("vector", lambda e, tab=tab, sc=sc, s3=s3, s4_=s4_: e.scalar_tensor_tensor(
                            out=sbt[s4_][:], in0=tab, scalar=sc, in1=scp[s3][:], op0=ALU.mult, op1=ALU.add),
                              [f"scp{s3}", tk], [f"sbt{s4_}"])
                        col = (h * NB + Qb) * NT + kt
                        em.op("scalar", lambda e, s4_=s4_, p5=p5, col=col: e.activation(
                            out=pT[p5][:], in_=sbt[s4_][:], func=AF.Exp, bias=cb[:, col:col + 1]),
                              [f"sbt{s4_}", "cb"], [f"pT{p5}"])
                    j = i - LA
                    if j >= 0:
                        Qb, c, kt, first, last = units[j]
                        p5 = slot[j]
                        em.op("tensor", lambda e, c=c, kt=kt, p5=p5, hb=hb, first=first, last=last: e.matmul(
                            Op[c][:], lhsT=VV[hb][:, kt, :], rhs=pT[p5][:], start=first, stop=last),
                              [f"VV{hb}", f"pT{p5}"], [f"O{c}"])
                        em.op("tensor", lambda e, c=c, p5=p5, first=first, last=last: e.matmul(
                            Sp[c][:], lhsT=onesb[:], rhs=pT[p5][:], start=first, stop=last),
                              ["onesb", f"pT{p5}"], [f"S{c}"])
                        if last:
                            def mk_rc(c, k):
                                return lambda: em.op("vector", lambda e: e.reciprocal(out=rs[c][:, k * 128:(k + 1) * 128], in_=Sp[c][:, k * 128:(k + 1) * 128]),
                                                     [f"S{c}"], [f"rs{c}_{k}"])
                            for k in range(4):
                                pending.append([i + k, mk_rc(c, k)])

                            def mk_oc(c):
                                return lambda: em.op("vector", lambda e: e.tensor_tensor(out=oc[c][:], in0=Op[c][:], in1=rs[c][:], op=ALU.mult),
                                                     [f"O{c}"] + [f"rs{c}_{k}" for k in range(4)], [f"oc{c}"])
                            pending.append([i + 4, mk_oc(c)])
                            if c == 1:
                                ab = nq % 2
                                nq += 1

                                def fin_a():
                                    em.op("vector", lambda e: e.scalar_tensor_tensor(out=wt[:], in0=oc[1][:], scalar=l4[:, 5:6], in1=oc[0][:],
                                                                                     op0=ALU.mult, op1=ALU.add), ["oc0", "oc1", "neglam"], ["wt"])
                                    em.op("gpsimd", lambda e: e.tensor_tensor(out=sq[:], in0=wt[:], in1=wt[:], op=ALU.mult), ["wt"], ["sq"])

                                def fin_b(h=h):
                                    q3 = sc_box[0] % NSC
                                    sc_box[0] += 1
                                    em.op("tensor", lambda e: e.matmul(scp[q3][:], lhsT=onesf[:], rhs=sq[:], start=True, stop=True),
                                          ["onesf", "sq"], [f"scp{q3}"])
                                    em.op("scalar", lambda e: e.activation(out=a1[:], in_=scp[q3][:], func=AF.Ln, scale=1.0 / 128, bias=epsb[:, 0:1]),
                                          [f"scp{q3}", "epsb"], ["a1"])
                                    em.op("scalar", lambda e: e.activation(out=a1[:], in_=a1[:], func=AF.Exp, scale=-0.5), ["a1"], ["a1"])

                                def fin_c(h=h, Qb=Qb, ab=ab):
                                    em.op("vector", lambda e: e.tensor_tensor(out=a1[:], in0=wt[:], in1=a1[:], op=ALU.mult), ["wt", "a1"], ["a1"])
                                    em.op("scalar", lambda e: e.activation(out=aT[ab][:], in_=a1[:], func=AF.Copy, scale=gs[:, 1:2]),
                                          ["a1", "gs1"], [f"aT{ab}"])
                                    em.dma("gpsimd", a_s[h * 128:(h + 1) * 128, Qb * 512:(Qb + 1) * 512], aT[ab][:], [f"aT{ab}"], cwrites=["a_s"])
                                pending.append([i + 5, fin_a])
                                pending.append([i + 8, fin_b])
                                pending.append([i + 10, fin_c])
                    due = [p_ for p_ in pending if p_[0] <= i]
                    for p_ in due:
                        pending.remove(p_)
                        p_[1]()
                for p_ in list(pending):
                    p_[1]()
                pending.clear()
            em.barrier()

    def phC(d):
        if True:
            with ExitStack() as st:
                sb = lambda n, s, dt=F32: st.enter_context(nc.sbuf_tensor(n, list(s), dt))
                ps = lambda n, s, dt=F32: st.enter_context(nc.psum_tensor(n, list(s), dt))
                P = f"c{d}_"
                JC = JC0 + d
                JR = JF0 if d == 0 else JB0
                jc = sb(P + "jc", [128, 1])
                em.dma("sync", jc[:], consts[:, JC:JC + 1], [], ["jc"], allow_slow_non_contiguous=True)
                jrow = sb(P + "jrow", [128, 128])
                em.dma("sync", jrow[:], consts[:, JR:JR + 128], [], ["jrow"])
                tri = sb(P + "tri", [128, 128])
                T0 = TL0 if d == 0 else TU0
                em.dma("sync", tri[:], consts[:, T0:T0 + 128], [], ["trif"])
                trib = sb(P + "trib", [128, 128], BF16)
                ntrib = sb(P + "ntrib", [128, 128], BF16)
                em.op("vector", lambda e: e.tensor_copy(out=trib[:], in_=tri[:]), ["trif"], ["trib"])
                em.op("vector", lambda e: e.tensor_scalar(out=ntrib[:], in0=tri[:], scalar1=-1.0, scalar2=None, op0=ALU.mult), ["trif"], ["ntrib"])
                Er = sb(P + "Er", [128, 2048])
                Ei = sb(P + "Ei", [128, 2048])
                Fr = sb(P + "Fr", [128, 2048])
                Fi = sb(P + "Fi", [128, 2048])
                with ExitStack() as st2:
                    sb2 = lambda n, s, dt=F32: st2.enter_context(nc.sbuf_tensor(n, list(s), dt))
                    W = [sb2(P + f"w{i}", [128, 2048]) for i in range(10)]
                    WI = sb2(P + "wi", [128, 2048], I32)
                    wkey = [f"W{i}" for i in range(10)]

                    def vop(fn, r, w):
                        em.op("vector", fn, r, w)

                    def sincos(ph, phk, out_s, out_c, ok_s, ok_c, tmp, tmpk, tmp2, tmp2k):
                        for (off, o, okk) in ((0.0, out_s, ok_s), (math.pi / 2, out_c, ok_c)):
                            vop(lambda e, off=off: e.tensor_scalar(out=tmp[:], in0=ph[:], scalar1=off, scalar2=1.0 / TWO_PI,
                                                                   op0=ALU.add, op1=ALU.mult), [phk], [tmpk])
                            vop(lambda e: e.tensor_copy(out=WI[:], in_=tmp[:]), [tmpk], ["WI"])
                            vop(lambda e: e.tensor_copy(out=tmp[:], in_=WI[:]), ["WI"], [tmpk])
                            vop(lambda e: e.scalar_tensor_tensor(out=tmp2[:], in0=tmp[:], scalar=-TWO_PI, in1=ph[:],
                                                                 op0=ALU.mult, op1=ALU.add), [tmpk, phk], [tmp2k])
                            vop(lambda e, off=off: e.tensor_scalar(out=tmp2[:], in0=tmp2[:], scalar1=off, scalar2=math.pi,
                                                                   op0=ALU.add, op1=ALU.min), [tmp2k], [tmp2k])
                            vop(lambda e: e.tensor_scalar(out=tmp2[:], in0=tmp2[:], scalar1=-math.pi, scalar2=None, op0=ALU.max), [tmp2k], [tmp2k])
                            em.op("scalar", lambda e, o=o: e.activation(out=o[:], in_=tmp2[:], func=AF.Sin), [tmp2k], [okk])

                    AR, AI, LS = W[0], W[1], W[2]
                    em.dma("sync", AR[:], a_row[d, 0:1, :].to_broadcast([128, 2048]), [], [wkey[0]])
                    em.dma("sync", AI[:], a_row[d, 1:2, :].to_broadcast([128, 2048]), [], [wkey[1]])
                    em.dma("sync", LS[:], a_row[d, 2:3, :].to_broadcast([128, 2048]), [], [wkey[2]])
                    em.op("scalar", lambda e: e.activation(out=LS[:], in_=LS[:], func=AF.Exp), [wkey[2]], [wkey[2]])
                    ars, ais = W[3], W[4]
                    vop(lambda e: e.tensor_tensor(out=ars[:], in0=AR[:], in1=LS[:], op=ALU.mult), [wkey[0], wkey[2]], [wkey[3]])
                    vop(lambda e: e.tensor_tensor(out=ais[:], in0=AI[:], in1=LS[:], op=ALU.mult), [wkey[1], wkey[2]], [wkey[4]])
                    sincos(ais, wkey[4], W[5], W[6], wkey[5], wkey[6], W[7], wkey[7], W[8], wkey[8])
                    mag = W[7]
                    em.op("scalar", lambda e: e.activation(out=mag[:], in_=ars[:], func=AF.Exp), [wkey[3]], [wkey[7]])
                    lbr, lbi = W[6], W[5]
                    vop(lambda e: e.tensor_tensor(out=lbr[:], in0=W[6][:], in1=mag[:], op=ALU.mult), [wkey[6], wkey[7]], [wkey[6]])
                    vop(lambda e: e.tensor_tensor(out=lbi[:], in0=W[5][:], in1=mag[:], op=ALU.mult), [wkey[5], wkey[7]], [wkey[5]])
                    vop(lambda e: e.tensor_scalar(out=lbr[:], in0=lbr[:], scalar1=-1.0, scalar2=None, op0=ALU.add), [wkey[6]], [wkey[6]])
                    den = W[7]
                    vop(lambda e: e.tensor_tensor(out=den[:], in0=AR[:], in1=AR[:], op=ALU.mult), [wkey[0]], [wkey[7]])
                    vop(lambda e: e.tensor_tensor(out=W[8][:], in0=AI[:], in1=AI[:], op=ALU.mult), [wkey[1]], [wkey[8]])
                    vop(lambda e: e.tensor_tensor(out=den[:], in0=den[:], in1=W[8][:], op=ALU.add), [wkey[7], wkey[8]], [wkey[7]])
                    vop(lambda e: e.reciprocal(out=den[:], in_=den[:]), [wkey[7]], [wkey[7]])
                    vop(lambda e: e.tensor_tensor(out=W[8][:], in0=lbr[:], in1=AR[:], op=ALU.mult), [wkey[6], wkey[0]], [wkey[8]])
                    vop(lambda e: e.tensor_tensor(out=W[9][:], in0=lbi[:], in1=AI[:], op=ALU.mult), [wkey[5], wkey[1]], [wkey[9]])
                    vop(lambda e: e.tensor_tensor(out=W[8][:], in0=W[8][:], in1=W[9][:], op=ALU.add), [wkey[8], wkey[9]], [wkey[8]])
                    vop(lambda e: e.tensor_tensor(out=W[8][:], in0=W[8][:], in1=den[:], op=ALU.mult), [wkey[8], wkey[7]], [wkey[8]])
                    vop(lambda e: e.tensor_tensor(out=W[9][:], in0=lbi[:], in1=AR[:], op=ALU.mult), [wkey[5], wkey[0]], [wkey[9]])
                    vop(lambda e: e.tensor_tensor(out=W[2][:], in0=lbr[:], in1=AI[:], op=ALU.mult), [wkey[6], wkey[1]], [wkey[2]])
                    vop(lambda e: e.tensor_tensor(out=W[9][:], in0=W[9][:], in1=W[2][:], op=ALU.subtract), [wkey[9], wkey[2]], [wkey[9]])
                    vop(lambda e: e.tensor_tensor(out=W[9][:], in0=W[9][:], in1=den[:], op=ALU.mult), [wkey[9], wkey[7]], [wkey[9]])
                    fre, fim = W[8], W[9]
                    ph = W[0]
                    vop(lambda e: e.tensor_scalar(out=ph[:], in0=ais[:], scalar1=jc[:, 0:1], scalar2=None, op0=ALU.mult), [wkey[4], "jc"], [wkey[0]])
                    emag = W[1]
                    em.op("scalar", lambda e: e.activation(out=emag[:], in_=ars[:], func=AF.Exp, scale=jc[:, 0:1]), [wkey[3], "jc"], [wkey[1]])
                    sincos(ph, wkey[0], W[5], W[6], wkey[5], wkey[6], W[7], wkey[7], W[2], wkey[2])
                    vop(lambda e: e.tensor_tensor(out=W[7][:], in0=fre[:], in1=W[6][:], op=ALU.mult), [wkey[8], wkey[6]], [wkey[7]])
                    vop(lambda e: e.tensor_tensor(out=W[2][:], in0=fim[:], in1=W[5][:], op=ALU.mult), [wkey[9], wkey[5]], [wkey[2]])
                    vop(lambda e: e.tensor_tensor(out=W[7][:], in0=W[7][:], in1=W[2][:], op=ALU.subtract), [wkey[7], wkey[2]], [wkey[7]])
                    vop(lambda e: e.tensor_tensor(out=Er[:], in0=W[7][:], in1=emag[:], op=ALU.mult), [wkey[7], wkey[1]], ["Er"])
                    vop(lambda e: e.tensor_tensor(out=W[7][:], in0=fre[:], in1=W[5][:], op=ALU.mult), [wkey[8], wkey[5]], [wkey[7]])
                    vop(lambda e: e.tensor_tensor(out=W[2][:], in0=fim[:], in1=W[6][:], op=ALU.mult), [wkey[9], wkey[6]], [wkey[2]])
                    vop(lambda e: e.tensor_tensor(out=W[7][:], in0=W[7][:], in1=W[2][:], op=ALU.add), [wkey[7], wkey[2]], [wkey[7]])
                    vop(lambda e: e.tensor_tensor(out=Ei[:], in0=W[7][:], in1=emag[:], op=ALU.mult), [wkey[7], wkey[1]], ["Ei"])
                    asm = sb2(P + "asm", [128, 48])
                    em.dma("sync", asm[:], a_sm[d, :, :], [], ["asm"])
                    em.op("scalar", lambda e: e.activation(out=asm[:, 32:48], in_=asm[:, 32:48], func=AF.Exp), ["asm"], ["asm"])
                    vop(lambda e: e.tensor_tensor(out=asm[:, 0:16], in0=asm[:, 0:16], in1=asm[:, 32:48], op=ALU.mult), ["asm"], ["asm"])
                    vop(lambda e: e.tensor_tensor(out=asm[:, 16:32], in0=asm[:, 16:32], in1=asm[:, 32:48], op=ALU.mult), ["asm"], ["asm"])
                    v3 = lambda t: t[:].rearrange("p (a b) -> p a b", a=16)
                    jb3 = jrow[:, :].unsqueeze(1).to_broadcast([128, 16, 128])
                    fa = W[0]
                    vop(lambda e: e.tensor_tensor(out=v3(fa), in0=asm[:, 0:16].unsqueeze(2).to_broadcast([128, 16, 128]), in1=jb3, op=ALU.mult),
                        ["asm", "jrow"], [wkey[0]])
                    fmag = W[1]
                    em.op("scalar", lambda e: e.activation(out=fmag[:], in_=fa[:], func=AF.Exp), [wkey[0]], [wkey[1]])
                    fph = W[3]
                    vop(lambda e: e.tensor_tensor(out=v3(fph), in0=asm[:, 16:32].unsqueeze(2).to_broadcast([128, 16, 128]), in1=jb3, op=ALU.mult),
                        ["asm", "jrow"], [wkey[3]])
                    sincos(fph, wkey[3], W[5], W[6], wkey[5], wkey[6], W[7], wkey[7], W[2], wkey[2])
                    vop(lambda e: e.tensor_tensor(out=Fr[:], in0=W[6][:], in1=fmag[:], op=ALU.mult), [wkey[6], wkey[1]], ["Fr"])
                    vop(lambda e: e.tensor_tensor(out=Fi[:], in0=W[5][:], in1=fmag[:], op=ALU.mult), [wkey[5], wkey[1]], ["Fi"])
                    em.barrier()
                BA = sb(P + "BA", [128, 4096], BF16)
                BB = sb(P + "BB", [128, 4096], BF16)
                CM = sb(P + "CM", [128, 4096], BF16)
                with ExitStack() as st2:
                    sb2 = lambda n, s, dt=F32: st2.enter_context(nc.sbuf_tensor(n, list(s), dt))
                    wa = sb2(P + "wa", [128, 4096])
                    wb = sb2(P + "wb", [128, 4096])
                    wc = sb2(P + "wc", [128, 4096])
                    em.dma("sync", wa[:], bmA[d, :, :], [], ["wa"])
                    em.dma("sync", wb[:], bmB[d, :, :], [], ["wb"])
                    em.dma("sync", wc[:], cmat[d, :, :], [], ["wc"])
                    em.op("vector", lambda e: e.tensor_copy(out=BA[:], in_=wa[:]), ["wa"], ["BA"])
                    w4 = lambda t: t[:].rearrange("p (a h c) -> p a h c", a=8, h=2)
                    em.op("vector", lambda e: e.tensor_scalar(out=w4(BB)[:, :, 0, :], in0=w4(wb)[:, :, 0, :], scalar1=-1.0, scalar2=None, op0=ALU.mult), ["wb"], ["BB0"])
                    em.op("vector", lambda e: e.tensor_copy(out=w4(BB)[:, :, 1, :], in_=w4(wb)[:, :, 1, :]), ["wb"], ["BB1"])
                    c5 = lambda t: t[:].rearrange("p (a h c) -> p a h c", a=8, h=2)
                    em.op("vector", lambda e: e.tensor_copy(out=c5(CM)[:, :, 0, :], in_=c5(wc)[:, :, 0, :]), ["wc"], ["CM0"])
                    em.op("vector", lambda e: e.tensor_scalar(out=c5(CM)[:, :, 1, :], in0=c5(wc)[:, :, 1, :], scalar1=-1.0, scalar2=None, op0=ALU.mult), ["wc"], ["CM1"])
                    em.barrier()
                dv = sb(P + "dv", [128, 12])
                em.dma("sync", dv[:], ssmv[:, :], [], ["dv"])
                fl = sb(P + "fl", [128, 1])
                em.dma("sync", fl[:], flag[:, :], [], ["fl"])
                uT = [sb(P + f"uT{i}", [128, 4, 128], BF16) for i in range(3)]
                psA = ps(P + "psA", [128, 512])
                psB = ps(P + "psB", [128, 512])
                NG = 4
                psG = [ps(P + f"psG{i}", [128, 512]) for i in range(NG)]
                psY = [ps(P + f"psY{i}", [128, 512]) for i in range(2)]
                T1 = [sb(P + f"T1{i}", [128, 512], BF16) for i in range(3)]
                T2 = [sb(P + f"T2{i}", [128, 512], BF16) for i in range(3)]
                g1 = [sb(P + f"g1{i}", [128, 512], BF16) for i in range(NG)]
                gl = sb(P + "gl", [128, 8, 4])
                t1 = [sb(P + f"t1{i}", [128, 512], BF16) for i in range(NG)]
                t2 = [sb(P + f"t2{i}", [128, 512], BF16) for i in range(NG)]
                hc = [sb(P + f"hc{i}", [128, 8, 4]) for i in range(2)]
                cq = sb(P + "cq", [128, 8, 8])
                yo = [sb(P + f"yo{i}", [128, 4, 128]) for i in range(2)]
                yfl = [sb(P + f"yfl{i}", [128, 4, 128]) for i in range(2)]
                em.op("gpsimd", lambda e: e.memset(hc[0][:], 0.0), [], ["hc0"])
                Er3 = Er[:].rearrange("p (a c) -> p a c", a=8)
                Ei3 = Ei[:].rearrange("p (a c) -> p a c", a=8)
                Fr4 = Fr[:].rearrange("p (a h j) -> p a h j", a=8, h=2)
                Fi4 = Fi[:].rearrange("p (a h j) -> p a h j", a=8, h=2)
                Frb_ = sb(P + "Frb", [128, 2048], BF16)
                Fib_ = sb(P + "Fib", [128, 2048], BF16)
                nFib_ = sb(P + "nFib", [128, 2048], BF16)
                em.op("vector", lambda e: e.tensor_copy(out=Frb_[:], in_=Fr[:]), ["Fr"], ["Frb"])
                em.op("vector", lambda e: e.tensor_copy(out=Fib_[:], in_=Fi[:]), ["Fi"], ["Fib"])
                em.op("vector", lambda e: e.tensor_scalar(out=nFib_[:], in0=Fi[:], scalar1=-1.0, scalar2=None, op0=ALU.mult), ["Fi"], ["nFib"])
                Frb4 = Frb_[:].rearrange("p (a h j) -> p a h j", a=8, h=2)
                Fib4 = Fib_[:].rearrange("p (a h j) -> p a h j", a=8, h=2)
                nFib4 = nFib_[:].rearrange("p (a h j) -> p a h j", a=8, h=2)
                LCOL = 127 if d == 0 else 0
                Fl = sb(P + "Fl", [128, 8, 6])
                em.op("vector", lambda e: e.tensor_copy(out=Fl[:, :, 0:2], in_=Fr4[:, :, :, LCOL]), ["Fr"], ["Fl0"])
                em.op("vector", lambda e: e.tensor_copy(out=Fl[:, :, 2:4], in_=Fi4[:, :, :, LCOL]), ["Fi"], ["Fl1"])
                em.op("vector", lambda e: e.tensor_scalar(out=Fl[:, :, 4:6], in0=Fi4[:, :, :, LCOL], scalar1=-1.0, scalar2=None, op0=ALU.mult), ["Fi"], ["Fl2"])
                em.op("vector", lambda e: e.tensor_copy(out=Fl[:, 0, 0:1], in_=Fl[:, 0, 0:1]), ["Fl0", "Fl1", "Fl2"], ["Fl"])
                order = list(range(NT)) if d == 0 else list(range(NT - 1, -1, -1))
                u_r = u_s.rearrange("(k p) t -> p k t", p=128)
                yf_r = yf_s.rearrange("(k p) t -> p k t", p=128)
                ys_r = ys_s.rearrange("(k p) t -> p k t", p=128)
                NP = NT * 8
                r4 = lambda t_: t_[:].rearrange("p (r h j) -> p r h j", r=2, h=2)
                h3 = lambda a_: a_.rearrange("p (h j) -> p h j", h=2)

                def stage_ab(gi):
                    ci, p = gi // 8, gi % 8
                    n = order[ci]
                    ub, b2 = ci % 3, gi % 3
                    kq, pq = p // 2, p % 2
                    co = kq * 1024 + pq * 512
                    if p == 0:
                        em.dma("sync", uT[ub][:], u_r[:, :, n * 128:(n + 1) * 128], ["u_s"], [f"uT{ub}"])
                    em.op("tensor", lambda e: e.matmul(psA[:], lhsT=uT[ub][:, kq, :], rhs=BA[:, co:co + 512], start=True, stop=True),
                          [f"uT{ub}", "BA"], ["psA"])
                    em.op("vector", lambda e: e.tensor_tensor(
                        out=T1[b2][:].rearrange("p (h c) -> p h c", h=2), in0=psA[:].rearrange("p (h c) -> p h c", h=2),
                        in1=Er3[:, p, :].unsqueeze(1).to_broadcast([128, 2, 256]), op=ALU.mult), ["psA", "Er"], [f"T1{b2}"])
                    em.op("vector", lambda e: e.tensor_tensor(
                        out=T2[b2][:].rearrange("p (h c) -> p h c", h=2), in0=psA[:].rearrange("p (h c) -> p h c", h=2),
                        in1=Ei3[:, p, :].unsqueeze(1).to_broadcast([128, 2, 256]), op=ALU.mult), ["psA", "Ei"], [f"T2{b2}"])

                def stage_cum(gi):
                    b2, g3 = gi % 3, gi % NG
                    for tl in range(4):
                        stl = (tl + 2) % 4
                        tm, tk = (ntrib, "ntrib") if tl < 2 else (trib, "trib")
                        em.op("tensor", lambda e, tl=tl: e.matmul(
                            psG[g3][:, tl * 128:(tl + 1) * 128], lhsT=T1[b2][:, tl * 128:(tl + 1) * 128], rhs=trib[:],
                            start=True, stop=False), [f"T1{b2}", "trib"], [f"psG{g3}"])
                        em.op("tensor", lambda e, tl=tl, stl=stl, tm=tm: e.matmul(
                            psG[g3][:, tl * 128:(tl + 1) * 128], lhsT=T2[b2][:, stl * 128:(stl + 1) * 128], rhs=tm[:],
                            start=False, stop=True), [f"T2{b2}", tk], [f"psG{g3}"])

                def stage_s2(gi):
                    ci, p = gi // 8, gi % 8
                    n = order[ci]
                    g3 = gi % NG
                    cb_, nb_ = ci % 2, (ci + 1) % 2
                    if p == 0 and ci != 0:
                        first_of_seg = ((n * 128) % SEG == 0) if d == 0 else (((n + 1) * 128) % SEG == 0)
                        if first_of_seg:
                            em.op("gpsimd", lambda e: e.tensor_scalar(out=hc[cb_][:], in0=hc[cb_][:], scalar1=fl[:, 0:1], scalar2=None, op0=ALU.mult),
                                  [f"hc{cb_}", "fl"], [f"hc{cb_}"])
                    for tl in range(4):
                        em.op("scalar", lambda e, tl=tl: e.activation(
                            out=g1[g3][:, tl * 128:(tl + 1) * 128], in_=psG[g3][:, tl * 128:(tl + 1) * 128], func=AF.Identity,
                            bias=hc[cb_][:, p, tl:tl + 1]), [f"psG{g3}", f"hc{cb_}"], [f"g1{g3}"])
                    for tl in range(4):
                        cc_ = tl * 128 + LCOL
                        em.op("scalar", lambda e, tl=tl, cc_=cc_: e.activation(
                            out=gl[:, p, tl:tl + 1], in_=psG[g3][:, cc_:cc_ + 1], func=AF.Identity,
                            bias=hc[cb_][:, p, tl:tl + 1]), [f"psG{g3}", f"hc{cb_}"], [f"gl{p}"])
                    em.op("vector", lambda e: e.tensor_tensor(out=h3(t1[g3][:, 0:256]), in0=h3(g1[g3][:, 0:256]), in1=Frb4[:, p, :, :], op=ALU.mult),
                          [f"g1{g3}", "Frb"], [f"t1{g3}a"])
                    em.op("vector", lambda e: e.tensor_tensor(out=h3(t1[g3][:, 256:512]), in0=h3(g1[g3][:, 256:512]), in1=Frb4[:, p, :, :], op=ALU.mult),
                          [f"g1{g3}", "Frb"], [f"t1{g3}b"])
                    em.op("vector", lambda e: e.tensor_tensor(out=h3(t2[g3][:, 0:256]), in0=h3(g1[g3][:, 256:512]), in1=nFib4[:, p, :, :], op=ALU.mult),
                          [f"g1{g3}", "nFib"], [f"t2{g3}a"])
                    em.op("vector", lambda e: e.tensor_tensor(out=h3(t2[g3][:, 256:512]), in0=h3(g1[g3][:, 0:256]), in1=Fib4[:, p, :, :], op=ALU.mult),
                          [f"g1{g3}", "Fib"], [f"t2{g3}b"])
                    em.op("gpsimd", lambda e: e.tensor_tensor(out=cq[:, p, 0:2], in0=gl[:, p, 0:2], in1=Fl[:, p, 0:2], op=ALU.mult), [f"gl{p}", "Fl"], [f"cq{p}a"])
                    em.op("gpsimd", lambda e: e.tensor_tensor(out=cq[:, p, 2:4], in0=gl[:, p, 2:4], in1=Fl[:, p, 0:2], op=ALU.mult), [f"gl{p}", "Fl"], [f"cq{p}a2"])
                    em.op("gpsimd", lambda e: e.tensor_tensor(out=cq[:, p, 4:6], in0=gl[:, p, 2:4], in1=Fl[:, p, 4:6], op=ALU.mult), [f"gl{p}", "Fl"], [f"cq{p}b"])
                    em.op("gpsimd", lambda e: e.tensor_tensor(out=cq[:, p, 6:8], in0=gl[:, p, 0:2], in1=Fl[:, p, 2:4], op=ALU.mult), [f"gl{p}", "Fl"], [f"cq{p}c"])
                    em.op("gpsimd", lambda e: e.tensor_tensor(out=hc[nb_][:, p, :], in0=cq[:, p, 0:4], in1=cq[:, p, 4:8], op=ALU.add),
                          [f"cq{p}a", f"cq{p}a2", f"cq{p}b", f"cq{p}c"], [f"hc{nb_}"])

                def stage_cmm(gi):
                    ci, p = gi // 8, gi % 8
                    n = order[ci]
                    g3 = gi % NG
                    kq, pq = p // 2, p % 2
                    yb, ub = ci % 2, ci % 3
                    for tl in range(4):
                        for (src, sk) in ((t1, "t1"), (t2, "t2")):
                            first = (pq == 0 and tl == 0 and sk == "t1")
                            last = (pq == 1 and tl == 3 and sk == "t2")
                            cc = p * 512 + tl * 128
                            rk = [f"t1{g3}a", f"t1{g3}b"] if sk == "t1" else [f"t2{g3}a", f"t2{g3}b"]
                            em.op("tensor", lambda e, tl=tl, src=src, first=first, last=last, cc=cc: e.matmul(
                                psY[yb][:, kq * 128:(kq + 1) * 128], lhsT=CM[:, cc:cc + 128], rhs=src[g3][:, tl * 128:(tl + 1) * 128],
                                start=first, stop=last), rk + ["CM0", "CM1"], [f"psY{yb}"])
                    if p != 7:
                        return
                    if d == 0:
                        em.op("scalar", lambda e: e.activation(out=yo[yb][:].rearrange("p k j -> p (k j)"), in_=psY[yb][:], func=AF.Copy),
                              [f"psY{yb}"], [f"yo{yb}"])
                        em.dma("gpsimd", yf_r[:, :, n * 128:(n + 1) * 128], yo[yb][:], [f"yo{yb}"], cwrites=["yf_s"])
                    else:
                        em.dma("sync", yfl[yb][:], yf_r[:, :, n * 128:(n + 1) * 128], ["yf_s"], [f"yfl{yb}"])
                        em.op("vector", lambda e: e.tensor_tensor(out=yo[yb][:].rearrange("p k j -> p (k j)"), in0=psY[yb][:],
                                                                  in1=yfl[yb][:].rearrange("p k j -> p (k j)"), op=ALU.add),
                              [f"psY{yb}", f"yfl{yb}"], [f"yo{yb}"])
                        em.op("gpsimd", lambda e: e.tensor_tensor(out=yfl[yb][:], in0=uT[ub][:],
                                                                  in1=dv[:, 0:4].unsqueeze(2).to_broadcast([128, 4, 128]), op=ALU.mult),
                              [f"uT{ub}", "dv", f"yo{yb}"], [f"yfl{yb}"])
                        em.op("gpsimd", lambda e: e.tensor_tensor(out=yo[yb][:], in0=yo[yb][:], in1=yfl[yb][:], op=ALU.add),
                              [f"yo{yb}", f"yfl{yb}"], [f"yo{yb}"])
                        em.dma("gpsimd", ys_r[:, :, n * 128:(n + 1) * 128], yo[yb][:], [f"yo{yb}"], cwrites=["ys_s"])

                for i in range(NP + 2):
                    if i < NP:
                        stage_ab(i)
                    if 0 <= i - 1 < NP:
                        stage_cum(i - 1)
                        stage_s2(i - 1)
                    if 0 <= i - 2 < NP:
                        stage_cmm(i - 2)
                em.barrier()

    def phD():
        with ExitStack() as st:
            sb = lambda n, s, dt=F32: st.enter_context(nc.sbuf_tensor(n, list(s), dt))
            ps = lambda n, s, dt=F32: st.enter_context(nc.psum_tensor(n, list(s), dt))
            wg = sb("d_wg", [128, 4, 512], BF16)
            em.dma("gpsimd", wg[:], w_glu.rearrange("(k p) n -> p k n", p=128), [], ["wg"])
            dv = sb("d_dv", [128, 12])
            em.dma("sync", dv[:], ssmv[:, :], [], ["dv"])
            onesf = sb("d_onesf", [128, 128])
            em.op("vector", lambda e: e.memset(onesf[:], 1.0), [], ["onesf"])
            ysb = [sb(f"d_ys{i}", [128, 4, 512]) for i in range(2)]
            y1 = [sb(f"d_y1{i}", [128, 4, 512]) for i in range(2)]
            y1b = [sb(f"d_y1b{i}", [128, 4, 512], BF16) for i in range(2)]
            sg = [sb(f"d_sg{i}", [128, 512]) for i in range(2)]
            y2 = [sb(f"d_y2{i}", [128, 4, 512]) for i in range(2)]
            sqd = [sb(f"d_sq{i}", [128, 512]) for i in range(2)]
            rsd = sb("d_rsd", [128, 512])
            rsd2 = sb("d_rsd2", [128, 512])
            so = [sb(f"d_so{i}", [128, 4, 512], BF16) for i in range(2)]
            pz = [ps(f"d_pz{i}", [128, 512]) for i in range(4)]
            pq_ = ps("d_pq", [128, 512])
            ys_r = ys_s.rearrange("(k p) t -> p k t", p=128)
            s_r = s_s.rearrange("(k p) t -> p k t", p=128)
            def d_stage1(blk):
                b = blk % 2
                em.dma("sync", ysb[b][:], ys_r[:, :, blk * 512:(blk + 1) * 512], ["ys_s"], [f"ys{b}"])
                em.op("scalar", lambda e: e.activation(out=y1[b][:], in_=ysb[b][:], func=AF.Gelu_apprx_tanh), [f"ys{b}"], [f"y1{b}"])
                em.op("vector", lambda e: e.tensor_copy(out=y1b[b][:], in_=y1[b][:]), [f"y1{b}"], [f"y1b{b}"])
                for mo in range(4):
                    for k in range(4):
                        em.op("tensor", lambda e, k=k, mo=mo: e.matmul(pz[mo][:], lhsT=wg[:, k, mo * 128:(mo + 1) * 128], rhs=y1b[b][:, k, :],
                                                                       start=(k == 0), stop=(k == 3)), ["wg", f"y1b{b}"], [f"pz{mo}"])
                for mo in range(4):
                    zb = mo % 2
                    em.op("scalar", lambda e, mo=mo, zb=zb: e.activation(out=sg[zb][:], in_=pz[mo][:], func=AF.Sigmoid, bias=dv[:, 4 + mo:5 + mo]),
                          [f"pz{mo}", "dv"], [f"sg{zb}"])
                    em.op("vector", lambda e, mo=mo, zb=zb: e.tensor_tensor(out=y2[b][:, mo, :], in0=y1[b][:, mo, :], in1=sg[zb][:], op=ALU.mult),
                          [f"y1{b}", f"sg{zb}"], [f"y2{b}_{mo}"])
                    em.op("gpsimd", lambda e, mo=mo, zb=zb: e.tensor_tensor(out=sqd[zb][:], in0=y2[b][:, mo, :], in1=y2[b][:, mo, :], op=ALU.mult),
                          [f"y2{b}_{mo}"], [f"sqd{zb}"])
                    em.op("tensor", lambda e, mo=mo, zb=zb: e.matmul(pq2[b][:], lhsT=onesf[:], rhs=sqd[zb][:], start=(mo == 0), stop=(mo == 3)),
                          ["onesf", f"sqd{zb}"], [f"pq{b}"])

            def d_stage2(blk):
                b = blk % 2
                em.op("vector", lambda e: e.tensor_scalar(out=rsd[:], in0=pq2[b][:], scalar1=1.0 / 512, scalar2=1e-6, op0=ALU.mult, op1=ALU.add), [f"pq{b}"], ["rsd"])
                em.op("scalar", lambda e: e.activation(out=rsd2[:], in_=rsd[:], func=AF.Sqrt), ["rsd"], ["rsd2"])
                em.op("vector", lambda e: e.reciprocal(out=rsd[:], in_=rsd2[:]), ["rsd2"], ["rsd"])
                for mo in range(4):
                    em.op("vector", lambda e, mo=mo: e.scalar_tensor_tensor(out=so[b][:, mo, :], in0=y2[b][:, mo, :], scalar=dv[:, 8 + mo:9 + mo],
                                                                           in1=rsd[:], op0=ALU.mult, op1=ALU.mult),
                          [f"y2{b}_{mo}", "rsd", "dv"], [f"so{b}"])
                em.dma("gpsimd", s_r[:, :, blk * 512:(blk + 1) * 512], so[b][:], [f"so{b}"], cwrites=["s_s"])

            pq2 = [pq_, ps("d_pq1", [128, 512])]
            d_stage1(0)
            for blk in range(NB):
                if blk + 1 < NB:
                    d_stage1(blk + 1)
                d_stage2(blk)
            em.barrier()

    def phE():
        with ExitStack() as st:
            sb = lambda n, s, dt=F32: st.enter_context(nc.sbuf_tensor(n, list(s), dt))
            ps = lambda n, s, dt=F32: st.enter_context(nc.psum_tensor(n, list(s), dt))
            identf = sb("e_identf", [128, 128])
            identb = sb("e_identb", [128, 128], BF16)
            em.dma("sync", identf[:], consts[:, ID0:ID0 + 128], [], ["identf"])
            em.op("vector", lambda e: e.tensor_copy(out=identb[:], in_=identf[:]), ["identf"], ["identb"])
            wo = sb("e_wo", [128, 8, D], BF16)
            em.dma("gpsimd", wo[:], w_out.rearrange("(k p) n -> p k n", p=128), [], ["wo"])
            fl = sb("e_fl", [128, 1])
            em.dma("sync", fl[:], flag[:, :], [], ["fl"])
            zt = sb("e_zt", [128, 8, 1], BF16)
            em.op("vector", lambda e: e.memset(zt[:], 0.0), [], ["zt"])
            h_r = h_s.rearrange("k p t -> p k t")
            em.dma("gpsimd", h_r[:, :, 0:1], zt[:], ["zt"], cwrites=["h_s"], allow_slow_non_contiguous=True)
            em.dma("gpsimd", h_r[:, :, 2 * SEGP - 1:2 * SEGP], zt[:], ["zt"], cwrites=["h_s"], allow_slow_non_contiguous=True)
            cat = [sb(f"e_cat{i}", [128, 8, 512], BF16) for i in range(2)]
            xt = [sb(f"e_xt{i}", [128, D]) for i in range(3)]
            x1 = [sb(f"e_x1{i}", [128, D]) for i in range(3)]
            junk = sb("e_junk", [128, D], BF16)
            s4 = [sb(f"e_s4{i}", [128, 4]) for i in range(2)]
            hn = [sb(f"e_hn{i}", [128, D], BF16) for i in range(2)]
            hT = [sb(f"e_hT{i}", [128, 8, 512], BF16) for i in range(2)]
            halo = [sb(f"e_halo{i}", [128, 8, 1], BF16) for i in range(2)]
            po = [ps(f"e_po{i}", [128, 512]) for i in range(4)]
            tp = [ps(f"e_tp{i}", [128, D], BF16) for i in range(2)]
            a_r = a_s.rearrange("(k p) t -> p k t", p=128)
            s_r = s_s.rearrange("(k p) t -> p k t", p=128)
            pcc = [0]

            def e_stage1(t):
                blk, i = t // 4, t % 4
                cbf = blk % 2
                a, n2 = t % 3, t % 2
                if i == 0:
                    em.dma("sync", cat[cbf][:, 0:4, :], a_r[:, :, blk * 512:(blk + 1) * 512], ["a_s"], [f"cat{cbf}a"])
                    em.dma("sync", cat[cbf][:, 4:8, :], s_r[:, :, blk * 512:(blk + 1) * 512], ["s_s"], [f"cat{cbf}s"])
                em.dma("sync", xt[a][:], x[t * 128:(t + 1) * 128, :], [], [f"xt{a}"])
                for half in range(2):
                    pb = pcc[0] % 4
                    pcc[0] += 1
                    for k in range(8):
                        em.op("tensor", lambda e, k=k, half=half, pb=pb: e.matmul(
                            po[pb][:], lhsT=cat[cbf][:, k, i * 128:(i + 1) * 128], rhs=wo[:, k, half * 512:(half + 1) * 512],
                            start=(k == 0), stop=(k == 7)), [f"cat{cbf}a", f"cat{cbf}s", "wo"], [f"po{pb}"])
                    em.op("vector", lambda e, half=half, pb=pb: e.tensor_tensor(
                        out=x1[a][:, half * 512:(half + 1) * 512], in0=po[pb][:], in1=xt[a][:, half * 512:(half + 1) * 512], op=ALU.add),
                          [f"po{pb}", f"xt{a}"], [f"x1{a}_{half}"])
                em.dma("gpsimd", x1_s[t * 128:(t + 1) * 128, :], x1[a][:], [f"x1{a}_0", f"x1{a}_1"], cwrites=["x1_s"])
                em.op("scalar", lambda e: e.activation(out=junk[:], in_=x1[a][:], func=AF.Square, accum_out=s4[n2][:, 0:1]),
                      [f"x1{a}_0", f"x1{a}_1"], ["junk", f"s4{n2}_0"])
                rstd_chain(s4[n2], f"s4{n2}", 1.0 / D, 1e-6)
                em.op("scalar", lambda e: e.activation(out=hn[n2][:], in_=x1[a][:], func=AF.Copy, scale=s4[n2][:, 3:4]),
                      [f"x1{a}_0", f"x1{a}_1", f"s4{n2}_3"], [f"hn{n2}"])

            def e_stage2(t):
                blk, i = t // 4, t % 4
                cbf = blk % 2
                n2 = t % 2
                for k in range(8):
                    em.op("tensor", lambda e, k=k: e.transpose(out=tp[n2][:, k * 128:(k + 1) * 128], in_=hn[n2][:, k * 128:(k + 1) * 128],
                                                               identity=identb[:]), [f"hn{n2}", "identb"], [f"tp{n2}"])
                em.op("vector", lambda e: e.tensor_copy(out=hT[cbf][:, :, i * 128:(i + 1) * 128],
                                                        in_=tp[n2][:].rearrange("p (k j) -> p k j", k=8)),
                      [f"tp{n2}"], [f"hT{cbf}"])
                if i != 3:
                    return
                seg = (blk * 512) // SEG
                off = seg * SEGP + 1 + (blk * 512 - seg * SEG)
                em.dma("gpsimd", h_r[:, :, off:off + 512], hT[cbf][:], [f"hT{cbf}"], cwrites=["h_s"])
                if (blk + 1) * 512 == SEG:
                    em.op("vector", lambda e: e.tensor_scalar(out=halo[0][:], in0=hT[cbf][:, :, 511:512], scalar1=fl[:, 0:1], scalar2=None,
                                                              op0=ALU.mult), [f"hT{cbf}", "fl"], ["halo0"])
                    em.dma("gpsimd", h_r[:, :, SEGP:SEGP + 1], halo[0][:], ["halo0"], cwrites=["h_s"], allow_slow_non_contiguous=True)
                if blk * 512 == SEG:
                    em.op("vector", lambda e: e.tensor_scalar(out=halo[1][:], in0=hT[cbf][:, :, 0:1], scalar1=fl[:, 0:1], scalar2=None,
                                                              op0=ALU.mult), [f"hT{cbf}", "fl"], ["halo1"])
                    em.dma("gpsimd", h_r[:, :, SEGP - 1:SEGP], halo[1][:], ["halo1"], cwrites=["h_s"], allow_slow_non_contiguous=True)

            e_stage1(0)
            for t in range(NT):
                if t + 1 < NT:
                    e_stage1(t + 1)
                e_stage2(t)
            em.barrier()

    def phF():
        with ExitStack() as st:
            sb = lambda n, s, dt=F32: st.enter_context(nc.sbuf_tensor(n, list(s), dt))
            ps = lambda n, s, dt=F32: st.enter_context(nc.psum_tensor(n, list(s), dt))
            gf = sb("f_gf", [128, 8])
            em.dma("sync", gf[:], gffn_pk[:, :], [], ["gf"])
            with ExitStack() as st2:
                sb2 = lambda n, s, dt=F32: st2.enter_context(nc.sbuf_tensor(n, list(s), dt))
                wst = [sb2(f"f_wst{i}", [128, 2 * DFF]) for i in range(2)]
                wsb = [sb2(f"f_wsb{i}", [128, 2 * DFF], BF16) for i in range(2)]
                wus_r = wus.rearrange("m p k c -> p k m c")
                for k in range(8):
                    b = k % 2
                    em.dma("sync", wst[b][:], w_up[k * 128:(k + 1) * 128, :], [], [f"wst{b}"])
                    if b == 0:
                        em.op("vector", lambda e, k=k, b=b: e.tensor_scalar(out=wsb[b][:], in0=wst[b][:], scalar1=gf[:, k:k + 1], scalar2=None, op0=ALU.mult),
                              [f"wst{b}", "gf"], [f"wsb{b}"])
                    else:
                        em.op("scalar", lambda e, k=k, b=b: e.activation(out=wsb[b][:], in_=wst[b][:], func=AF.Copy, scale=gf[:, k:k + 1]),
                              [f"wst{b}", "gf"], [f"wsb{b}"])
                    em.dma("gpsimd", wus_r[:, k, :, :], wsb[b][:].rearrange("p (m c) -> p m c", c=128), [f"wsb{b}"], cwrites=["wus"])
                em.barrier()
            wd = sb("f_wd", [128, 22, D], BF16)
            em.dma("gpsimd", wd[:], w_down.rearrange("(m p) n -> p m n", p=128), [], ["wd"])
            cw = sb("f_cw", [128, 44, 3])
            cbv = sb("f_cb", [128, 44])
            em.dma("sync", cw[:], cw_pk[:, :, :], [], ["cw"])
            em.dma("sync", cbv[:], cb_pk[:, :], [], ["cbv"])
            gfin_t = sb("f_gfin", [128, D])
            em.dma("sync", gfin_t[:], gfin[0:1, :].to_broadcast([128, D]), [], ["gfin"])
            hw = [sb(f"f_hw{i}", [128, 8, 512], BF16) for i in range(3)]
            wt_ = [sb(f"f_wt{i}", [128, 8, 128], BF16) for i in range(4)]
            zc = [sb(f"f_zc{i}", [128, 512]) for i in range(4)]
            ga = [sb(f"f_ga{i}", [128, 512]) for i in range(2)]
            act = [sb(f"f_act{i}", [128, 22, 512], BF16) for i in range(2)]
            x1t = [sb(f"f_x1{i}", [128, D]) for i in range(8)]
            x2t = [sb(f"f_x2{i}", [128, D]) for i in range(2)]
            yt = [sb(f"f_yt{i}", [128, D]) for i in range(2)]
            junk = sb("f_junk", [128, D], BF16)
            s4 = [sb(f"f_s4{i}", [128, 4]) for i in range(2)]
            pu = [ps(f"f_pu{i}", [128, 512]) for i in range(3)]
            pd = [ps(f"f_pd{i}", [128, 512]) for i in range(4)]
            h_r = h_s.rearrange("k p t -> p k t")
            cnts = {"uc": 0, "dc": 0, "tc": 0}
            wins = []
            for seg in range(2):
                b = 0
                while 510 * b < SEG:
                    W_ = min(512, SEGP - 510 * b)
                    wins.append(dict(seg=seg, b=b, W=W_, nv=W_ - 2, c0=seg * SEGP + 510 * b, tok0=seg * SEG + 510 * b, idx=len(wins)))
                    b += 1

            def f_load(w):
                hb = w["idx"] % 3
                em.dma("sync", hw[hb][:, :, 0:w["W"]], h_r[:, :, w["c0"]:w["c0"] + w["W"]], ["h_s"], [f"hw{hb}"])

            def f_loadx(w):
                w["xb"] = []
                ntt = (w["nv"] + 127) // 128
                for tt in range(ntt):
                    r0 = tt * 128
                    nr = min(128, w["nv"] - r0)
                    xb_ = cnts["tc"] % 8
                    cnts["tc"] += 1
                    w["xb"].append(xb_)
                    em.dma("sync", x1t[xb_][0:nr, :], x1_s[w["tok0"] + r0:w["tok0"] + r0 + nr, :], ["x1_s"], [f"x1t{xb_}"])

            def f_up(w, mm0, mm1):
                hb, ab = w["idx"] % 3, w["idx"] % 2
                W_, nv = w["W"], w["nv"]
                for mm in range(mm0, mm1):
                    m = (mm // 2) + (22 if mm % 2 else 0)
                    uc = cnts["uc"]
                    cnts["uc"] += 1
                    w4, p3, z4 = uc % 4, uc % 3, uc % 4
                    em.dma("sync", wt_[w4][:], wus[m, :, :, :], ["wus"], [f"wt{w4}"])
                    for k in range(8):
                        em.op("tensor", lambda e, k=k, w4=w4, p3=p3: e.matmul(
                            pu[p3][:, 0:W_], lhsT=wt_[w4][:, k, :], rhs=hw[hb][:, k, 0:W_], start=(k == 0), stop=(k == 7)),
                              [f"wt{w4}", f"hw{hb}"], [f"pu{p3}"])
                    em.op("scalar", lambda e, m=m, p3=p3, z4=z4: e.activation(
                        out=zc[z4][:, 0:nv], in_=pu[p3][:, 1:1 + nv], func=AF.Identity, scale=cw[:, m, 1:2], bias=cbv[:, m:m + 1]),
                          [f"pu{p3}", "cw", "cbv"], [f"zc{z4}"])
                    em.op("vector", lambda e, m=m, p3=p3, z4=z4: e.scalar_tensor_tensor(
                        out=zc[z4][:, 0:nv], in0=pu[p3][:, 0:nv], scalar=cw[:, m, 0:1], in1=zc[z4][:, 0:nv], op0=ALU.mult, op1=ALU.add),
                          [f"pu{p3}", "cw", f"zc{z4}"], [f"zc{z4}"])
                    em.op("vector", lambda e, m=m, p3=p3, z4=z4: e.scalar_tensor_tensor(
                        out=zc[z4][:, 0:nv], in0=pu[p3][:, 2:2 + nv], scalar=cw[:, m, 2:3], in1=zc[z4][:, 0:nv], op0=ALU.mult, op1=ALU.add),
                          [f"pu{p3}", "cw", f"zc{z4}"], [f"zc{z4}"])
                    g2_ = (mm // 2) % 2
                    if mm % 2 == 0:
                        em.op("scalar", lambda e, z4=z4, g2_=g2_: e.activation(out=ga[g2_][:, 0:nv], in_=zc[z4][:, 0:nv], func=AF.Gelu_apprx_tanh),
                              [f"zc{z4}"], [f"ga{g2_}"])
                    else:
                        mg = mm // 2
                        em.op("gpsimd", lambda e, z4=z4, g2_=g2_, mg=mg: e.tensor_tensor(
                            out=act[ab][:, mg, 0:nv], in0=ga[g2_][:, 0:nv], in1=zc[z4][:, 0:nv], op=ALU.mult),
                              [f"ga{g2_}", f"zc{z4}"], [f"act{ab}"])

            def f_down(w):
                ab = w["idx"] % 2
                nv, tok0 = w["nv"], w["tok0"]
                ntt = (nv + 127) // 128
                for tt in range(ntt):
                    r0 = tt * 128
                    nr = min(128, nv - r0)
                    xb_ = w["xb"][tt]
                    ob = xb_ % 2
                    for half in range(2):
                        p4 = cnts["dc"] % 4
                        cnts["dc"] += 1
                        for m in range(22):
                            em.op("tensor", lambda e, m=m, half=half, p4=p4, r0=r0, nr=nr: e.matmul(
                                pd[p4][0:nr, :], lhsT=act[ab][:, m, r0:r0 + nr], rhs=wd[:, m, half * 512:(half + 1) * 512],
                                start=(m == 0), stop=(m == 21)), [f"act{ab}", "wd"], [f"pd{p4}"])
                        em.op("vector", lambda e, half=half, p4=p4, nr=nr, xb_=xb_, ob=ob: e.tensor_tensor(
                            out=x2t[ob][0:nr, half * 512:(half + 1) * 512], in0=pd[p4][0:nr, :], in1=x1t[xb_][0:nr, half * 512:(half + 1) * 512], op=ALU.add),
                              [f"pd{p4}", f"x1t{xb_}"], [f"x2t{ob}_{half}"])
                    em.op("scalar", lambda e, nr=nr, ob=ob: e.activation(out=junk[0:nr, :], in_=x2t[ob][0:nr, :], func=AF.Square, accum_out=s4[ob][0:nr, 0:1]),
                          [f"x2t{ob}_0", f"x2t{ob}_1"], ["junk", f"s4{ob}_0"])
                    rstd_chain(s4[ob], f"s4{ob}", 1.0 / D, 1e-6)
                    em.op("vector", lambda e, nr=nr, ob=ob: e.scalar_tensor_tensor(
                        out=yt[ob][0:nr, :], in0=x2t[ob][0:nr, :], scalar=s4[ob][0:nr, 3:4], in1=gfin_t[0:nr, :], op0=ALU.mult, op1=ALU.mult),
                          [f"x2t{ob}_0", f"x2t{ob}_1", f"s4{ob}_3", "gfin"], [f"yt{ob}"])
                    em.dma("gpsimd", y[tok0 + r0:tok0 + r0 + nr, :], yt[ob][0:nr, :], [f"yt{ob}"], cwrites=["y"])

            NPRE = 8
            f_load(wins[0])
            if len(wins) > 1:
                f_load(wins[1])
            f_loadx(wins[0])
            f_up(wins[0], 0, 44)
            for wi_, w in enumerate(wins):
                nxt = wins[wi_ + 1] if wi_ + 1 < len(wins) else None
                nn = wins[wi_ + 2] if wi_ + 2 < len(wins) else None
                if nxt is not None:
                    f_up(nxt, 0, NPRE)
                if nn is not None:
                    f_load(nn)
                if nxt is not None:
                    f_loadx(nxt)
                f_down(w)
                if nxt is not None:
                    f_up(nxt, NPRE, 44)
            em.barrier()

    if "A" in phases:
        phA()
    if "B" in phases:
        phB()
    if "C" in phases:
        phC(0)
        phC(1)
    if "D" in phases:
        phD()
    if "E" in phases:
        phE()
    if "F" in phases:
        phF()
    em.finish()
    top.close()
    return nc


def make_consts():
    c = np.zeros((128, NCON), np.float32)
    p = np.arange(128, dtype=np.float32)[:, None]
    j = np.arange(128, dtype=np.float32)[None, :]
    c[:, ID0:ID0 + 128] = np.eye(128, dtype=np.float32)
    c[:, TL0:TL0 + 128] = (p <= j).astype(np.float32)
    c[:, TU0:TU0 + 128] = (p >= j).astype(np.float32)
    qf = np.arange(512, dtype=np.float32)[None, :]
    c[:, RP0:RP0 + 512] = qf - p
    for m in range(4):
        c[:, AR0 + 512 * m:AR0 + 512 * (m + 1)] = np.abs(qf - p - 128.0 * m)
    c[:, JF0:JF0 + 128] = j + 1.0
    c[:, JB0:JB0 + 128] = 128.0 - j
    c[:, QF0:QF0 + 512] = qf
    c[:, QR0:QR0 + 512] = 511.0 - qf
    c[:, JC0] = -(p[:, 0] + 1.0)
    c[:, JC0 + 1] = -(128.0 - p[:, 0])
    return c


def make_cbias(T, cross_val):
    NT, NB, SEG = T // 128, T // 512, T // 2
    cb = np.zeros((4, NB, NT), np.float32)
    for h in range(4):
        for Qb in range(NB):
            for kt in range(NT):
                rel = kt - 4 * Qb
                if 0 <= rel < 4:
                    v = 0.0
                elif rel < 0:
                    v = -SLOPES[h] * (512 * Qb - 128 * kt)
                else:
                    v = -SLOPES[h] * (128 * kt - 512 * Qb)
                if (512 * Qb) // SEG != (128 * kt) // SEG:
                    v += cross_val
                cb[h, Qb, kt] = v
    return np.ascontiguousarray(np.broadcast_to(cb.reshape(1, -1), (128, 4 * NB * NT))).astype(np.float32)


def host_weights(inp):
    f = lambda a: np.ascontiguousarray(np.asarray(a, dtype=np.float32))
    w = {}
    w["w_in"] = f(inp["w_in"][0])
    w["w_out"] = f(inp["w_out"][0])
    w["w_up"] = f(inp["w_up"][0])
    w["w_down"] = f(inp["w_down"][0])
    w["w_glu"] = f(inp["w_glu"][0])
    w["gmix_pk"] = f(np.asarray(inp["g_mix_norm"][0]).reshape(8, 128).T)
    w["gffn_pk"] = f(np.asarray(inp["g_ffn_norm"][0]).reshape(8, 128).T)
    w["gfin"] = f(np.asarray(inp["g_final"]).reshape(1, D))
    w["gsub"] = f(np.asarray(inp["g_subln"][0]).reshape(128, 1))
    w["lamv"] = f(np.concatenate([np.asarray(inp[k][0]) for k in ("lambda_q1", "lambda_k1", "lambda_q2", "lambda_k2")]).reshape(1, 256))
    w["cw_pk"] = f(np.asarray(inp["conv_w"][0]).reshape(3, 44, 128).transpose(2, 1, 0))
    w["cb_pk"] = f(np.asarray(inp["conv_b"][0]).reshape(44, 128).T)
    sv = np.zeros((128, 12), np.float32)
    sv[:, 0:4] = np.asarray(inp["ssm_d"][0]).reshape(4, 128).T
    sv[:, 4:8] = np.asarray(inp["b_glu"][0]).reshape(4, 128).T
    sv[:, 8:12] = np.asarray(inp["g_ssm_out"][0]).reshape(4, 128).T
    w["ssmv"] = sv
    a_re = np.asarray(inp["ssm_a_re"][0], np.float32)
    a_im = np.asarray(inp["ssm_a_im"][0], np.float32)
    ls = np.asarray(inp["ssm_log_step"][0], np.float32)
    lsx = np.repeat(ls[:, :, None], 64, axis=2)
    a_row = np.stack([a_re.reshape(2, 2048), a_im.reshape(2, 2048), lsx.reshape(2, 2048)], axis=1)
    w["a_row"] = f(a_row)
    a_sm = np.zeros((2, 128, 48), np.float32)
    for d in range(2):
        a_sm[d, :, 0:16] = a_re[d].reshape(16, 128).T
        a_sm[d, :, 16:32] = a_im[d].reshape(16, 128).T
        a_sm[d, :, 32:48] = lsx[d].reshape(16, 128).T
    w["a_sm"] = a_sm
    b_re = np.asarray(inp["ssm_b_re"][0], np.float32)
    b_im = np.asarray(inp["ssm_b_im"][0], np.float32)
    c_re = np.asarray(inp["ssm_c_re"][0], np.float32)
    c_im = np.asarray(inp["ssm_c_im"][0], np.float32)
    bmA = np.zeros((2, 128, 4, 2, 512), np.float32)
    bmB = np.zeros((2, 128, 4, 2, 512), np.float32)
    cm = np.zeros((2, 128, 8, 4, 128), np.float32)
    for d in range(2):
        for g in range(32):
            kq, gg = g // 8, g % 8
            pq, gl = gg // 4, gg % 4
            rows = slice(gg * 16, gg * 16 + 16)
            bmA[d, rows, kq, pq, gl * 64:gl * 64 + 64] = b_re[d, g].T
            bmA[d, rows, kq, pq, 256 + gl * 64:256 + gl * 64 + 64] = b_im[d, g].T
            bmB[d, rows, kq, pq, gl * 64:gl * 64 + 64] = b_im[d, g].T
            bmB[d, rows, kq, pq, 256 + gl * 64:256 + gl * 64 + 64] = b_re[d, g].T
            p = g // 4
            half, glh = gl // 2, gl % 2
            srows = slice(glh * 64, glh * 64 + 64)
            chc = slice((g % 8) * 16, (g % 8) * 16 + 16)
            cm[d, srows, p, half, chc] = c_re[d, g].T
            cm[d, srows, p, 2 + half, chc] = c_im[d, g].T
    w["bmA"] = bmA.reshape(2, 128, 4096)
    w["bmB"] = bmB.reshape(2, 128, 4096)
    w["cmat"] = cm.reshape(2, 128, 4096)
    w["consts"] = make_consts()
    return w


_NC_CACHE = {}


def kernel(**inputs):
    T = 8192
    w = host_weights(inputs)
    xp = np.asarray(inputs["x_prompt"], np.float32)
    xs = np.asarray(inputs["x_sample"], np.float32)
    cb_p = make_cbias(T, 0.0)
    cb_s = make_cbias(T, -30000.0)
    in_maps = []
    for c in range(8):
        m = dict(w)
        if c < 4:
            m["x"] = np.ascontiguousarray(xp[c])
            m["cbias"] = cb_p
            m["flag"] = np.ones((128, 1), np.float32)
        else:
            m["x"] = np.ascontiguousarray(xs[2 * (c - 4):2 * (c - 4) + 2].reshape(T, D))
            m["cbias"] = cb_s
            m["flag"] = np.zeros((128, 1), np.float32)
        in_maps.append(m)
    if T not in _NC_CACHE:
        _NC_CACHE[T] = build(T)
    res = run_bass_kernel_spmd(_NC_CACHE[T], in_maps, core_ids=list(range(8)))
    ys = [np.asarray(r["y"], np.float32) for r in res.results]
    y_prompt = np.stack(ys[0:4], axis=0)
    y_sample = np.concatenate([ys[c].reshape(2, T // 2, D) for c in range(4, 8)], axis=0)
    return (y_prompt, y_sample)
```
